# Optimizing a Trainium2 kernel written in Bass

```python
import math
import jax, jax.numpy as jnp
from jax import lax
import numpy as np

D_MODEL = 1024
BATCH = 8
SEQ = 4096
DEPTH = 2

HEAD_DIM = 64
ROPE_THETA = 500000.0
ROPE_FRAC = 4
QBLK = 128
EPS = 1e-6

DIFF_HEADS = 4
DIFF_DC = HEAD_DIM // 2
DIFF_DV = HEAD_DIM
MLSTM_HEADS = 4
MLSTM_DQK = HEAD_DIM
MLSTM_DV = HEAD_DIM
MLSTM_CHUNK = 64
CONV_W = 4
SB_HEADS = 4
SB_D = HEAD_DIM
DSA_HEADS = 4
DSA_D = HEAD_DIM
IDX_HEADS = 8
IDX_D = 32
DSA_TOPK_MAX = 256
N_BRANCH = 4
BRANCH_W = 4 * HEAD_DIM
D_FF = 2816

IN_COLS = (
    ("diff_q", DIFF_HEADS * 2 * DIFF_DC),
    ("diff_k", DIFF_HEADS * 2 * DIFF_DC),
    ("diff_v", DIFF_HEADS * DIFF_DV),
    ("ml_qk", 2 * MLSTM_HEADS * MLSTM_DQK),
    ("ml_v", MLSTM_HEADS * MLSTM_DV),
    ("ml_i", MLSTM_HEADS),
    ("ml_f", MLSTM_HEADS),
    ("ml_o", MLSTM_HEADS * MLSTM_DV),
    ("sb_q", SB_HEADS * SB_D),
    ("sb_k", SB_HEADS * SB_D),
    ("sb_v", SB_HEADS * SB_D),
    ("dsa_q", DSA_HEADS * DSA_D),
    ("dsa_k", DSA_D),
    ("dsa_v", DSA_D),
    ("idx_q", IDX_HEADS * IDX_D),
    ("idx_k", IDX_D),
    ("idx_w", IDX_HEADS),
    ("gates", N_BRANCH * D_MODEL),
)
N_IN = sum(w for _, w in IN_COLS)

kernel_name = "hybrid_gated_diff_mlstm_stickbreak_dsa"


def rmsnorm(x, g):
    xf = x.astype(jnp.float32)
    y = xf * lax.rsqrt(jnp.mean(xf * xf, axis=-1, keepdims=True) + EPS)
    return (y * g.astype(jnp.float32)).astype(x.dtype)


def rope(x, pos, rot_dim):
    half = rot_dim // 2
    inv = ROPE_THETA ** (-jnp.arange(half, dtype=jnp.float32) / half)
    ang = pos.astype(jnp.float32)[..., None] * inv
    ang = ang.reshape(ang.shape[:2] + (1,) * (x.ndim - 3) + (half,))
    cos, sin = jnp.cos(ang).astype(x.dtype), jnp.sin(ang).astype(x.dtype)
    x1, x2, rest = x[..., :half], x[..., half:rot_dim], x[..., rot_dim:]
    return jnp.concatenate([x1 * cos - x2 * sin, x1 * sin + x2 * cos, rest], axis=-1)


def split_cols(z):
    out, off = {}, 0
    for name, w in IN_COLS:
        out[name] = z[..., off:off + w]
        off += w
    return out


def causal_mask(q0, q1, strict):
    qpos = jnp.arange(q0, q1)[:, None]
    kpos = jnp.arange(q1)[None, :]
    return (kpos < qpos) if strict else (kpos <= qpos)


def sweep(block_fn, seq):
    return jnp.concatenate([block_fn(q0, q0 + QBLK) for q0 in range(0, seq, QBLK)], axis=1)


def swiglu(h, w_gu, w_down):
    g, u = jnp.split(h @ w_gu, 2, axis=-1)
    return (jax.nn.silu(g) * u) @ w_down


def causal_conv(x, w, b):
    y = lax.conv_general_dilated(x, w[:, None, :].astype(x.dtype), window_strides=(1,),
                                 padding=[(CONV_W - 1, 0)],
                                 dimension_numbers=("NWC", "WIO", "NWC"),
                                 feature_group_count=x.shape[-1])
    return y + b


def diff_attention(c, pos, qk_g, lam_p, head_g, layer_idx):
    B, S, _ = c["diff_q"].shape
    q = c["diff_q"].reshape(B, S, DIFF_HEADS, 2, DIFF_DC)
    k = c["diff_k"].reshape(B, S, DIFF_HEADS, 2, DIFF_DC)
    v = c["diff_v"].reshape(B, S, DIFF_HEADS, DIFF_DV)
    q = rope(rmsnorm(q, qk_g[0]), pos, DIFF_DC // ROPE_FRAC)
    k = rope(rmsnorm(k, qk_g[1]), pos, DIFF_DC // ROPE_FRAC)
    lam_init = 0.8 - 0.6 * math.exp(-0.3 * layer_idx)
    lp = lam_p.astype(jnp.float32)
    lam = jnp.exp(jnp.sum(lp[0] * lp[1])) - jnp.exp(jnp.sum(lp[2] * lp[3])) + lam_init
    scale = DIFF_DC ** -0.5

    def block(q0, q1):
        s = jnp.einsum("bqhcd,bkhcd->bhcqk", q[:, q0:q1], k[:, :q1]).astype(jnp.float32) * scale
        p = jax.nn.softmax(jnp.where(causal_mask(q0, q1, False), s, -jnp.inf), axis=-1)
        a = p[:, :, 0] - lam * p[:, :, 1]
        return jnp.einsum("bhqk,bkhd->bqhd", a.astype(v.dtype), v[:, :q1])

    o = rmsnorm(sweep(block, S), head_g) * (1.0 - lam_init)
    return o.reshape(B, S, DIFF_HEADS * DIFF_DV)


def mlstm_chunkwise(q, k, v, ig, lf):
    B, S, H, DK = q.shape
    DV = v.shape[-1]
    L = MLSTM_CHUNK
    NC = S // L

    def chunks(a):
        a = a.reshape((B, NC, L, H) + a.shape[3:])
        return jnp.moveaxis(a, (1, 3), (0, 2))

    tri = jnp.tril(jnp.ones((L, L), dtype=bool))

    def step(carry, inp):
        C, n, m = carry
        qb, kb, vb, ib, fb = inp
        b = jnp.cumsum(fb, axis=-1)
        dl = jnp.where(tri, b[..., :, None] - b[..., None, :] + ib[..., None, :], -jnp.inf)
        inter = b + m[..., None]
        mt = jnp.maximum(inter, jnp.max(dl, axis=-1))
        dw = jnp.exp(dl - mt[..., None])
        iw = jnp.exp(inter - mt)
        s = jnp.einsum("bhtd,bhsd->bhts", qb, kb) * dw
        num = iw[..., None] * jnp.einsum("bhtd,bhde->bhte", qb, C) + jnp.einsum("bhts,bhse->bhte", s, vb)
        den = iw * jnp.einsum("bhtd,bhd->bht", qb, n) + jnp.sum(s, axis=-1)
        h = num / jnp.maximum(jnp.abs(den), jnp.exp(-mt))[..., None]
        bl = b[..., -1]
        g = bl[..., None] - b + ib
        m_new = jnp.maximum(bl + m, jnp.max(g, axis=-1))
        decay = jnp.exp(bl + m - m_new)
        wk = jnp.exp(g - m_new[..., None])
        C = decay[..., None, None] * C + jnp.einsum("bhs,bhsd,bhse->bhde", wk, kb, vb)
        n = decay[..., None] * n + jnp.einsum("bhs,bhsd->bhd", wk, kb)
        return (C, n, m_new), h

    f32 = jnp.float32
    init = (jnp.zeros((B, H, DK, DV), f32), jnp.zeros((B, H, DK), f32), jnp.zeros((B, H), f32))
    _, hs = lax.scan(step, init, (chunks(q.astype(f32)), chunks(k.astype(f32)), chunks(v.astype(f32)),
                                  chunks(ig), chunks(lf)))
    return jnp.moveaxis(hs, (0, 2), (1, 3)).reshape(B, S, H, DV)


def mlstm(c, conv_w, conv_b, gate_b, head_g):
    B, S, _ = c["ml_v"].shape
    qk = jax.nn.silu(causal_conv(c["ml_qk"], conv_w, conv_b))
    q, k = jnp.split(qk, 2, axis=-1)
    q = q.reshape(B, S, MLSTM_HEADS, MLSTM_DQK) * (MLSTM_DQK ** -0.5)
    k = k.reshape(B, S, MLSTM_HEADS, MLSTM_DQK)
    v = c["ml_v"].reshape(B, S, MLSTM_HEADS, MLSTM_DV)
    gb = gate_b.astype(jnp.float32)
    ig = c["ml_i"].astype(jnp.float32) + gb[0]
    lf = jax.nn.log_sigmoid(c["ml_f"].astype(jnp.float32) + gb[1])
    h = mlstm_chunkwise(q, k, v, ig, lf).astype(c["ml_v"].dtype)
    o = jax.nn.sigmoid(c["ml_o"]).reshape(B, S, MLSTM_HEADS, MLSTM_DV)
    return (o * rmsnorm(h, head_g)).reshape(B, S, MLSTM_HEADS * MLSTM_DV)


def stick_breaking(c):
    B, S, _ = c["sb_q"].shape
    q = c["sb_q"].reshape(B, S, SB_HEADS, SB_D)
    k = c["sb_k"].reshape(B, S, SB_HEADS, SB_D)
    v = c["sb_v"].reshape(B, S, SB_HEADS, SB_D)
    scale = SB_D ** -0.5

    def block(q0, q1):
        z = jnp.einsum("bqhd,bkhd->bhqk", q[:, q0:q1], k[:, :q1]).astype(jnp.float32) * scale
        mask = causal_mask(q0, q1, True)
        log_1m = jnp.where(mask, jax.nn.log_sigmoid(-z), 0.0)
        after = lax.cumsum(log_1m, axis=3, reverse=True) - log_1m
        a = jnp.where(mask, jnp.exp(jax.nn.log_sigmoid(z) + after), 0.0)
        return jnp.einsum("bhqk,bkhd->bqhd", a.astype(v.dtype), v[:, :q1])

    return sweep(block, S).reshape(B, S, SB_HEADS * SB_D)


def dsa(c, pos, qk_g):
    B, S, _ = c["dsa_q"].shape
    q = rope(rmsnorm(c["dsa_q"].reshape(B, S, DSA_HEADS, DSA_D), qk_g[0]), pos, DSA_D // ROPE_FRAC)
    k = rope(rmsnorm(c["dsa_k"], qk_g[1]), pos, DSA_D // ROPE_FRAC)
    v = c["dsa_v"]
    qi = rope(c["idx_q"].reshape(B, S, IDX_HEADS, IDX_D), pos, IDX_D // ROPE_FRAC)
    ki = rope(c["idx_k"], pos, IDX_D // ROPE_FRAC)
    wi = c["idx_w"]
    topk = min(DSA_TOPK_MAX, S // 4)
    scale = DSA_D ** -0.5
    gather = jax.vmap(lambda a, i: a[i])

    def block(q0, q1):
        qpos = jnp.arange(q0, q1)
        r = jax.nn.relu(jnp.einsum("bqhd,bkd->bqhk", qi[:, q0:q1], ki[:, :q1]).astype(jnp.float32))
        score = jnp.einsum("bqh,bqhk->bqk", wi[:, q0:q1].astype(jnp.float32), r)
        score = jnp.where(jnp.arange(q1)[None, None, :] <= qpos[None, :, None], score, -jnp.inf)
        _, sel = lax.top_k(score, min(topk, q1))
        valid = sel <= qpos[None, :, None]
        ks, vs = gather(k, sel), gather(v, sel)
        s = jnp.einsum("bqhd,bqnd->bhqn", q[:, q0:q1], ks).astype(jnp.float32) * scale
        p = jax.nn.softmax(jnp.where(valid[:, None], s, -jnp.inf), axis=-1)
        return jnp.einsum("bhqn,bqnd->bqhd", p.astype(v.dtype), vs)

    return sweep(block, S).reshape(B, S, DSA_HEADS * DSA_D)


def setup_inputs(seed: int = 0) -> dict:
    key = jax.random.key(seed)
    ks = jax.random.split(key, 24)
    nrm = lambda k, shape, s: jax.random.normal(k, shape, jnp.float32) * s
    gain = lambda k, shape: 1.0 + 0.01 * jax.random.normal(k, shape, jnp.float32)
    ml_w = 2 * MLSTM_HEADS * MLSTM_DQK
    i_bias = nrm(ks[10], (DEPTH, MLSTM_HEADS), 0.1)
    f_bias = jnp.linspace(3.0, 6.0, MLSTM_HEADS)[None, :] + nrm(ks[11], (DEPTH, MLSTM_HEADS), 0.1)
    positions = (jnp.arange(SEQ, dtype=jnp.int32)[None, :]
                 + jax.random.randint(ks[1], (BATCH, 1), 0, 1024, dtype=jnp.int32))
    return {
        "x": nrm(ks[0], (BATCH, SEQ, D_MODEL), 1.0),
        "positions": positions,
        "ffn1_norm": gain(ks[2], (DEPTH, D_MODEL)),
        "ffn1_w_gu": nrm(ks[3], (DEPTH, D_MODEL, 2 * D_FF), D_MODEL ** -0.5),
        "ffn1_w_down": nrm(ks[4], (DEPTH, D_FF, D_MODEL), D_FF ** -0.5),
        "mix_norm": gain(ks[5], (DEPTH, D_MODEL)),
        "w_in": nrm(ks[6], (DEPTH, D_MODEL, N_IN), D_MODEL ** -0.5),
        "diff_qk_norm": gain(ks[7], (DEPTH, 2, DIFF_DC)),
        "diff_lambda": nrm(ks[8], (DEPTH, 4, DIFF_DC), 0.1),
        "diff_head_norm": gain(ks[9], (DEPTH, DIFF_DV)),
        "ml_conv_w": nrm(ks[12], (DEPTH, CONV_W, ml_w), CONV_W ** -0.5),
        "ml_conv_b": nrm(ks[13], (DEPTH, ml_w), 0.01),
        "ml_gate_bias": jnp.stack([i_bias, f_bias], axis=1),
        "ml_head_norm": gain(ks[14], (DEPTH, MLSTM_DV)),
        "dsa_qk_norm": gain(ks[15], (DEPTH, 2, DSA_D)),
        "w_branch": nrm(ks[16], (DEPTH, N_BRANCH, BRANCH_W, D_MODEL), BRANCH_W ** -0.5),
        "w_out": nrm(ks[17], (DEPTH, D_MODEL, D_MODEL), D_MODEL ** -0.5),
        "ffn2_norm": gain(ks[18], (DEPTH, D_MODEL)),
        "ffn2_w_gu": nrm(ks[19], (DEPTH, D_MODEL, 2 * D_FF), D_MODEL ** -0.5),
        "ffn2_w_down": nrm(ks[20], (DEPTH, D_FF, D_MODEL), D_FF ** -0.5),
    }


def reference(x, positions, ffn1_norm, ffn1_w_gu, ffn1_w_down, mix_norm, w_in, diff_qk_norm,
              diff_lambda, diff_head_norm, ml_conv_w, ml_conv_b, ml_gate_bias, ml_head_norm,
              dsa_qk_norm, w_branch, w_out, ffn2_norm, ffn2_w_gu, ffn2_w_down):
    B, S, _ = x.shape
    for l in range(DEPTH):
        x = x + 0.5 * swiglu(rmsnorm(x, ffn1_norm[l]), ffn1_w_gu[l], ffn1_w_down[l])
        h = rmsnorm(x, mix_norm[l])
        c = split_cols(h @ w_in[l])
        outs = (
            diff_attention(c, positions, diff_qk_norm[l], diff_lambda[l], diff_head_norm[l], l),
            mlstm(c, ml_conv_w[l], ml_conv_b[l], ml_gate_bias[l], ml_head_norm[l]),
            stick_breaking(c),
            dsa(c, positions, dsa_qk_norm[l]),
        )
        gates = jax.nn.sigmoid(c["gates"]).reshape(B, S, N_BRANCH, D_MODEL)
        y = gates[:, :, 0] * (outs[0] @ w_branch[l, 0])
        for bi in range(1, N_BRANCH):
            y = y + gates[:, :, bi] * (outs[bi] @ w_branch[l, bi])
        x = x + y @ w_out[l]
        x = x + 0.5 * swiglu(rmsnorm(x, ffn2_norm[l]), ffn2_w_gu[l], ffn2_w_down[l])
    return x
```

```python
import math
from contextlib import ExitStack, contextmanager
import numpy as np
import concourse.bass as bass
import concourse.mybir as mybir
from concourse.bass_utils import run_bass_kernel_spmd

F32 = mybir.dt.float32
BF16 = mybir.dt.bfloat16
I32 = mybir.dt.int32
ALU = mybir.AluOpType
AF = mybir.ActivationFunctionType
AX = mybir.AxisListType

D_MODEL = 1024
D_FF = 2816
NFF = D_FF // 128
NDC = D_MODEL // 128
EPS = 1e-6
ROPE_THETA = 500000.0
N_FM = 19
N_TM = 840
TOPK = 256
NEG = -1.0e30

SAME_ENG_SYNC = True
ENGS = ("pe", "act", "dve", "pool", "sp")
_ENG_ATTR = {"pe": "tensor", "act": "scalar", "dve": "vector", "pool": "gpsimd", "sp": "sync"}


class Buf:
    __slots__ = ("name", "t", "w", "r", "dsem", "dcnt")

    def __init__(self, name, t):
        self.name = name
        self.t = t
        self.w = None
        self.r = {}
        self.dsem = None
        self.dcnt = 0

    def __getitem__(self, idx):
        return self.t[idx]


class Prog:
    def __init__(self, nc):
        self.nc = nc
        self.ops = {e: [] for e in ENGS}
        self.cnt = {e: 0 for e in ENGS}
        self.seen = {e: {} for e in ENGS}
        self.bufs = {}
        self.dsems = {}
        self.free_dsems = []
        self.sw_sems = set()
        self.bg_sems = set()
        self.nid = 0
        self.ninst = 0
        self.stack = None
        self.phase_bufs = []
        self.psum_ring = []
        self.psum_i = 0
        self.ring = list(range(8))

    def _reg(self, name, t):
        b = Buf(name, t)
        self.bufs[t.name] = b
        return b

    def sb(self, name, shape, dt):
        self.nid += 1
        t = self.stack.enter_context(self.nc.sbuf_tensor(f"{name}_{self.nid}", list(shape), dt))
        b = self._reg(name, t)
        self.phase_bufs.append(b)
        return b

    def sb_global(self, name, shape, dt):
        self.nid += 1
        t = self.nc.alloc_sbuf_tensor(f"{name}_{self.nid}", list(shape), dt)
        return self._reg(name, t)

    def init_psum(self):
        for i in range(8):
            t = self.nc.alloc_psum_tensor(f"psb{i}", [128, 512], F32)
            self.psum_ring.append(self._reg(f"psb{i}", t))

    def psum(self):
        r = self.ring
        b = self.psum_ring[r[self.psum_i % len(r)]]
        self.psum_i += 1
        return b

    def psum_pools(self, *sizes):
        out, k = [], 0
        for n in sizes:
            out.append([self.psum_ring[k + i] for i in range(n)])
            k += n
        self.ring = list(range(k, 8)) or [7]
        return out

    def psum_reserve(self, n):
        self.ring = list(range(n, 8))
        return [self.psum_ring[i] for i in range(n)]

    @contextmanager
    def phase(self, name):
        self.stack = ExitStack()
        self.phase_bufs = []
        self.ring = list(range(8))
        try:
            yield
        finally:
            self.barrier()
            for b in self.phase_bufs:
                if b.dsem is not None and b.dsem not in self.sw_sems:
                    self.free_dsems.append((b.dsem, b.dcnt))
                self.bufs.pop(b.t.name, None)
            self.stack.close()
            self.stack = None

    def _deps(self, eng, reads, writes):
        need = {}

        def add(k, v):
            if k == eng and (not SAME_ENG_SYNC or eng == "pe"):
                return
            if need.get(k, 0) < v:
                need[k] = v
        for b in reads:
            if b.w is not None:
                add(*b.w)
        for b in writes:
            if b.w is not None:
                add(*b.w)
            for k, v in b.r.items():
                add(k, v)
        out = []
        seen = self.seen[eng]
        for k, v in need.items():
            if seen.get(k, 0) >= v:
                continue
            seen[k] = v
            out.append((k, v))
        return out

    def _mark(self, ev, reads, writes):
        for b in reads:
            if b.r.get(ev[0], 0) < ev[1]:
                b.r[ev[0]] = ev[1]
        for b in writes:
            b.w = ev
            b.r = {}
        self.ninst += 1

    def _classify(self, kw):
        reads, writes = [], []
        for k, v in kw.items():
            if not hasattr(v, "tensor") or not hasattr(v, "space"):
                continue
            b = self.bufs.get(v.tensor.name)
            if b is None:
                continue
            if k in ("out", "accum_out", "ap"):
                if b not in writes:
                    writes.append(b)
            else:
                if b not in reads:
                    reads.append(b)
        return reads, writes

    def op(self, eng, name, _after=(), **kw):
        reads, writes = self._classify(kw)
        for b in _after:
            if b not in reads:
                reads.append(b)
        waits = self._deps(eng, reads, writes)
        self.cnt[eng] += 1
        ev = (eng, self.cnt[eng])
        self.ops[eng].append((waits, name, kw, (eng, 1)))
        self._mark(ev, reads, writes)

    def dma(self, q, out, in_, **kw):
        bo = self.bufs.get(out.tensor.name)
        bi = self.bufs.get(in_.tensor.name)
        owner = bo if bo is not None else bi
        assert owner is not None
        if owner.dsem is None:
            if self.free_dsems and q != "pool":
                owner.dsem, owner.dcnt = self.free_dsems.pop()
            else:
                owner.dsem = f"d{len(self.dsems)}"
            self.dsems[owner.dsem] = owner
        if q == "pool":
            self.sw_sems.add(owner.dsem)
        reads = [bi] if bi is not None else []
        writes = [bo] if bo is not None else []
        waits = self._deps(q, reads, writes)
        if owner.dcnt > 0:
            k, v = owner.dsem, owner.dcnt
            if self.seen[q].get(k, 0) < v:
                self.seen[q][k] = v
                waits.append((k, v))
        owner.dcnt += 16
        ev = (owner.dsem, owner.dcnt)
        d = dict(out=out, in_=in_)
        d.update(kw)
        self.ops[q].append((waits, "dma_start", d, (owner.dsem, 16)))
        self._mark(ev, reads, writes)

    def barrier(self):
        for e in ENGS:
            waits = []
            seen = self.seen[e]
            for e2 in ENGS:
                if e2 != e and self.cnt[e2] > seen.get(e2, 0):
                    seen[e2] = self.cnt[e2]
                    waits.append((e2, self.cnt[e2]))
            for k, b in self.dsems.items():
                if k in self.bg_sems:
                    continue
                if b.dsem == k and b.dcnt > seen.get(k, 0):
                    seen[k] = b.dcnt
                    waits.append((k, b.dcnt))
            self.ops[e].append((waits, None, None, None))

    def emit(self):
        nc = self.nc
        ops = self.ops
        keys = set(ENGS)
        for e in ENGS:
            for waits, name, kw, inc in ops[e]:
                for k, v in waits:
                    keys.add(k)
                if inc is not None:
                    keys.add(inc[0])
        sems = {k: nc.alloc_semaphore(f"s_{k}") for k in sorted(keys)}
        needed = {e: set() for e in ENGS}
        for e in ENGS:
            for waits, name, kw, inc in ops[e]:
                for k, v in waits:
                    if k in needed:
                        needed[k].add(v)
        remap = {}
        for e in ENGS:
            m = {}
            c = 0
            n = 0
            for waits, name, kw, inc in ops[e]:
                if inc is not None and inc[0] == e:
                    n += 1
                    if n in needed[e]:
                        c += 1
                        m[n] = c
            remap[e] = m

        def run(engobj, ename):
            n = 0
            for waits, name, kw, inc in ops[ename]:
                for k, v in waits:
                    if k in remap:
                        v = remap[k][v]
                    engobj.wait_ge(sems[k], v)
                if name is None:
                    continue
                ins = getattr(engobj, name)(**kw)
                if inc[0] == ename:
                    n += 1
                    if n in remap[ename]:
                        ins.then_inc(sems[ename], 1)
                else:
                    ins.then_inc(sems[inc[0]], inc[1])

        with nc.Block() as block:
            @block.tensor
            def _(e):
                run(e, "pe")

            @block.scalar
            def _(e):
                run(e, "act")

            @block.vector
            def _(e):
                run(e, "dve")

            @block.gpsimd
            def _(e):
                run(e, "pool")

            @block.sync
            def _(e):
                run(e, "sp")


class Stream:
    def __init__(self, P, ring, srcs, look=None, q="sp", outf=None, pre=None):
        self.P, self.ring, self.srcs, self.q = P, ring, srcs, q
        self.look = (len(ring) - 1) if look is None else look
        self.issued = 0
        self.outf = outf
        self.pre = pre

    def get(self, i):
        while self.issued < len(self.srcs) and self.issued <= i + self.look:
            k = self.issued
            b = self.ring[k % len(self.ring)]
            if self.pre is not None and self.pre[k] is not None:
                self.pre[k](b)
            self.P.dma(self.q, out=(b[:] if self.outf is None else self.outf[k](b)), in_=self.srcs[k])
            self.issued += 1
        return self.ring[i % len(self.ring)]


C_FFN1 = 0
C_MIX = 8
C_FFN2 = 16
C_DQN = 24
C_DKN = 25
C_DHN = 26
C_MHN = 27
C_SQN = 28
C_SKN = 29
C_CW = 30
C_CB = 46
C_GB = 50
NCOL = 52
CC_INV32, CC_SGN32, CC_INV64, CC_SGN64, CC_EPS, CC_ONE = 0, 1, 2, 3, 4, 5

CH_DQ, CH_DK, CH_MQ, CH_MK, CH_MO, CH_SQ, CH_SK, CH_AQ, CH_IQ, CH_MISC = 0, 2, 4, 6, 8, 10, 12, 14, 16, 18
TM_DV, TM_MV, TM_SV, TM_AV, TM_IW = 0, 256, 512, 768, 832


class Builder:
    def __init__(self, S, L, TT=1024, debug=False):
        self.S, self.L, self.TT, self.debug = S, L, TT, debug
        self.skip = ()
        self.wready = {}
        assert S % 512 == 0 and TT % 512 == 0 and S % TT == 0
        self.nc = nc = bass.Bass("TRN2", target_bir_lowering=False)
        self.P = P = Prog(nc)
        P.init_psum()
        ein = lambda n, shp, dt=F32: nc.dram_tensor(n, list(shp), dt, kind="ExternalInput")
        okind = "ExternalOutput" if debug else "Internal"
        scr = lambda n, shp, dt: nc.dram_tensor(n, list(shp), dt, kind=okind)
        self.xT = ein("xT", [D_MODEL, S])
        self.pos = ein("pos", [1, S], I32)
        self.w_gu = [ein(f"gu{i}", [L, NFF, 128, NDC * 256]) for i in (1, 2)]
        self.w_dn = [ein(f"dn{i}", [L, NDC, 128, NFF * 128]) for i in (1, 2)]
        self.w_fm = ein("wfm", [L, N_FM, 128, NDC * 128])
        self.w_tm = ein("wtm", [L, 128, NDC * N_TM])
        self.w_gt = ein("wgt", [L, 32, 128, NDC * 128])
        self.w_br = ein("wbr", [L, 32, 128, 2 * 128])
        self.w_out = ein("wout", [L, NDC, 128, NDC * 128])
        self.colp = ein("colp", [L, 128, NCOL])
        self.lamb = ein("lamb", [L, 128, 128])
        self.ccol = ein("ccol", [128, 8])
        self.cmatb = ein("cmatb", [128, 6, 128])
        self.cmatf = ein("cmatf", [128, 5, 128])
        self.outT = nc.dram_tensor("outT", [D_MODEL, S], F32, kind="ExternalOutput")
        self.b_gu = [scr(f"bgu{i}", [L, NFF, 128, NDC * 256], BF16) for i in (1, 2)]
        self.b_dn = [scr(f"bdn{i}", [L, NDC, 128, NFF * 128], BF16) for i in (1, 2)]
        self.b_fm = scr("bfm", [L, N_FM, 128, NDC * 128], BF16)
        self.b_tm = scr("btm", [L, 128, NDC * N_TM], BF16)
        self.b_gt = scr("bgt", [L, 32, 128, NDC * 128], BF16)
        self.b_br = scr("bbr", [L, 32, 128, 2 * 128], BF16)
        self.b_out = scr("bout", [L, NDC, 128, NDC * 128], BF16)
        self.xs = [scr(f"xs{i}", [D_MODEL, S], F32) for i in range(3)]
        self.zT = scr("zT", [N_FM * 128, S], F32)
        self.pT = scr("pT", [N_FM * 128, S], BF16)
        self.vtok = scr("vtok", [S, N_TM], F32)
        self.oT = scr("oT", [4, 256, S], BF16)
        self.rope = scr("rope", [4, 128, S], F32)
        self.grow = scr("grow", [8, S], F32)
        self.dbg = scr("dbg", [16, 128, 512], F32) if debug else None
        self.dbg_i = 0

    def consts(self):
        P = self.P
        self.k_ccol = P.sb_global("ccol", [128, 8], F32)
        self.k_matb = P.sb_global("cmatb", [128, 6, 128], BF16)
        self.k_matf = P.sb_global("cmatf", [128, 5, 128], F32)
        self.k_colp = P.sb_global("colp", [128, self.L, NCOL], F32)
        P.dma("sp", out=self.k_ccol[:], in_=self.ccol.ap())
        P.dma("pool", out=self.k_matb[:], in_=self.cmatb.ap())
        P.dma("sp", out=self.k_matf[:], in_=self.cmatf.ap())
        P.dma("sp", out=self.k_colp[:], in_=self.colp.ap().rearrange("l p c -> p l c"))
        self.ONES = self.k_matb[:, 0, :]
        self.BLK32 = self.k_matb[:, 1, :]
        self.BLK64 = self.k_matb[:, 2, :]
        self.UGE = self.k_matb[:, 3, :]
        self.MLE = self.k_matb[:, 4, :]
        self.MLT = self.k_matb[:, 5, :]
        self.PERM32 = self.k_matf[:, 0, :]
        self.PERM64 = self.k_matf[:, 1, :]
        self.ADDMASK = self.k_matf[:, 2, :]
        self.IDENTF = self.k_matf[:, 3, :]
        self.POW2 = self.k_matf[:, 4, :]

    def dump(self, ap, tag=""):
        if self.dbg is None or self.dbg_i >= 16:
            return
        P = self.P
        np_, nf = ap.shape[0], ap.shape[-1]
        t = P.sb("dbgt", [128, 512], F32)
        P.op("dve", "tensor_copy", out=t[0:np_, 0:nf], in_=ap)
        P.dma("sp", out=self.dbg.ap()[self.dbg_i, 0:np_, 0:nf], in_=t[0:np_, 0:nf])
        print("dbg", self.dbg_i, tag, np_, nf)
        self.dbg_i += 1

    @staticmethod
    def pipe(iters, stages, skews, drip=None, rate=2):
        n, D = len(iters), max(skews)
        for s_ in range(n + D):
            for f, k in zip(stages, skews):
                i = s_ - k
                if 0 <= i < n:
                    f(iters[i])
            if drip:
                for _ in range(rate):
                    if drip:
                        drip.pop(0)()
        while drip:
            drip.pop(0)()

    def col(self, l, c, p0=0, p1=128):
        return self.k_colp[p0:p1, l, c:c + 1]

    def cc(self, c, p0=0, p1=128):
        return self.k_ccol[p0:p1, c:c + 1]

    def cast_weights(self):
        P = self.P
        self.wready = {}
        jobs = []
        for l in range(self.L):
            for i in (0, 1):
                jobs.append((("gu", i, l), self.b_gu[i].ap()[l], self.w_gu[i].ap()[l], 2))
                jobs.append((("dn", i, l), self.b_dn[i].ap()[l].rearrange("s p (a n) -> s p a n", a=2),
                             self.w_dn[i].ap()[l].rearrange("s p (a n) -> s p a n", a=2), 2))
                if i == 0:
                    jobs.append((("fm", l), self.b_fm.ap()[l], self.w_fm.ap()[l], 1))
                    jobs.append((("tm", l), self.b_tm.ap()[l].rearrange("p (a n) -> p a n", a=8),
                                 self.w_tm.ap()[l].rearrange("p (a n) -> p a n", a=8), 1))
                    jobs.append((("gt", l), self.b_gt.ap()[l], self.w_gt.ap()[l], 2))
                    jobs.append((("br", l), self.b_br.ap()[l], self.w_br.ap()[l], 1))
                    jobs.append((("out", l), self.b_out.ap()[l], self.w_out.ap()[l], 1))
        k = 0

        def cp(owners, key, dst, src, nsplit):
            nonlocal k
            n0 = src.shape[0]
            step = (n0 + nsplit - 1) // nsplit
            evs = []
            for i in range(0, n0, step):
                ow = owners[k % len(owners)]
                k += 1
                self._dram_dma("pool", dst[i:i + step], src[i:i + step], ow)
                evs.append((ow.dsem, ow.dcnt))
            self.wready[key] = evs
        with P.phase("cast"):
            dummy = [P.sb(f"cd{i}", [1, 8], F32) for i in range(4)]
            for key, dst, src, ns in jobs[:2]:
                cp(dummy, key, dst, src, ns)
        self.wready[jobs[0][0]] = []
        self.wready[jobs[1][0]] = []
        bg = [P.sb_global(f"cg{i}", [1, 8], F32) for i in range(4)]
        for key, dst, src, ns in jobs[2:]:
            cp(bg, key, dst, src, ns)
        for b in bg:
            P.bg_sems.add(b.dsem)

    def need_weights(self, *keys):
        P = self.P
        waits = []
        for key in keys:
            for (sk, v) in self.wready.get(key, []):
                if P.seen["sp"].get(sk, 0) < v:
                    P.seen["sp"][sk] = v
                    waits.append((sk, v))
        if waits:
            P.ops["sp"].append((waits, None, None, None))

    def _dram_dma(self, q, out, in_, owner):
        P = self.P
        if owner.dsem is None:
            owner.dsem = f"d{len(P.dsems)}"
            P.dsems[owner.dsem] = owner
            P.sw_sems.add(owner.dsem)
        waits = []
        if owner.dcnt > 0:
            waits.append((owner.dsem, owner.dcnt))
        owner.dcnt += 16
        P.ops[q].append((waits, "dma_start", dict(out=out, in_=in_), (owner.dsem, 16)))
        P.ninst += 1

    def norm_parts(self, l, gcol, xt, hT, sq, rstd, TT):
        P = self.P

        ops = []
        for c in range(NDC):
            ops.append(lambda c=c: P.op("act", "activation", out=sq[:, c, :], in_=xt[:, c, :], func=AF.Square))

        def p1(hf):
            cs = slice(hf * 512, (hf + 1) * 512)
            ps = P.psum()
            for c in range(NDC):
                P.op("pe", "matmul", out=ps[:], lhsT=self.ONES, rhs=sq[:, c, cs], start=(c == 0), stop=(c == NDC - 1))
            P.op("act", "activation", out=rstd[:, cs], in_=ps[:], func=AF.Ln, scale=1.0 / D_MODEL, bias=self.cc(CC_EPS))
            P.op("act", "activation", out=rstd[:, cs], in_=rstd[:, cs], func=AF.Exp, scale=-0.5)
        for hf in range(TT // 512):
            ops.append(lambda hf=hf: p1(hf))
        for hf in range(TT // 512):
            cs = slice(hf * 512, (hf + 1) * 512)
            for c in range(NDC):
                ops.append(lambda c=c, cs=cs: P.op("dve", "scalar_tensor_tensor", out=hT[:, c, cs], in0=xt[:, c, cs], scalar=self.col(l, gcol + c),
                                                   in1=rstd[:, cs], op0=ALU.mult, op1=ALU.mult))
        return ops


    def ffn(self, l, which, xsrc, xdst):
        P, S, TT = self.P, self.S, self.TT
        NH = TT // 512
        gcol = C_FFN1 if which == 0 else C_FFN2
        bgu, bdn = self.b_gu[which].ap()[l], self.b_dn[which].ap()[l]
        with P.phase("ffn"):
            self.need_weights(("gu", which, l), ("dn", which, l))
            xts = [P.sb("xt", [128, NDC, TT], F32) for _ in range(2)]
            hTs = [P.sb("hT", [128, NDC, TT], BF16) for _ in range(2)]
            sq = P.sb("sq", [128, NDC, TT], BF16)
            rstd = P.sb("rstd", [128, TT], F32)
            aT = P.sb("aT", [128, NFF, TT], BF16)
            wg = [P.sb("wg", [128, NDC, 256], BF16) for _ in range(3)]
            wd = [P.sb("wd", [128, NFF, 128], BF16) for _ in range(2)]
            sg = [P.sb("sg", [128, 512], F32) for _ in range(2)]
            xv = lambda t, ti: t.ap().rearrange("(c p) s -> p c s", p=128)[:, :, ti * TT:(ti + 1) * TT]
            nt = S // TT
            sx = Stream(P, xts, [xv(xsrc, ti) for ti in range(nt)], look=0)
            sgu = Stream(P, wg, [bgu[j].rearrange("p (c n) -> p c n", c=NDC) for ti in range(nt) for j in range(NFF)])
            sdn = Stream(P, wd, [bdn[d].rearrange("p (c n) -> p c n", c=NFF) for ti in range(nt) for d in range(NDC)])
            for p_ in self.norm_parts(l, gcol, sx.get(0), hTs[0], sq, rstd, TT):
                p_()
            for ti in range(nt):
                xt = sx.get(ti)
                hT = hTs[ti % 2]
                sgu.get(ti * NFF)
                nxt = self.norm_parts(l, gcol, sx.get(ti + 1), hTs[(ti + 1) % 2], sq, rstd, TT) if ti + 1 < nt else None
                for j in range(NFF):
                    w = sgu.get(ti * NFF + j)
                    if nxt and j >= 2:
                        for _ in range(2):
                            if nxt:
                                nxt.pop(0)()
                    if j == NFF - 2:
                        sdn.get(ti * NDC)
                    for hf in range(NH):
                        cs = slice(hf * 512, (hf + 1) * 512)
                        pg, pu = P.psum(), P.psum()
                        for c in range(NDC):
                            P.op("pe", "matmul", out=pg[:], lhsT=w[:, c, 0:128], rhs=hT[:, c, cs], start=(c == 0), stop=(c == NDC - 1))
                        for c in range(NDC):
                            P.op("pe", "matmul", out=pu[:], lhsT=w[:, c, 128:256], rhs=hT[:, c, cs], start=(c == 0), stop=(c == NDC - 1))
                        s_ = sg[(j * NH + hf) % 2]
                        P.op("act", "activation", out=s_[:], in_=pg[:], func=AF.Silu)
                        P.op("dve", "tensor_tensor", out=aT[:, j, cs], in0=s_[:], in1=pu[:], op=ALU.mult)
                while nxt:
                    nxt.pop(0)()
                for d in range(NDC):
                    w = sdn.get(ti * NDC + d)
                    for hf in range(NH):
                        cs = slice(hf * 512, (hf + 1) * 512)
                        po = P.psum()
                        for c in range(NFF):
                            P.op("pe", "matmul", out=po[:], lhsT=w[:, c, :], rhs=aT[:, c, cs], start=(c == 0), stop=(c == NFF - 1))
                        P.op("dve", "scalar_tensor_tensor", out=xt[:, d, cs], in0=po[:], scalar=0.5, in1=xt[:, d, cs],
                             op0=ALU.mult, op1=ALU.add)
                P.dma("sp", out=xv(xdst, ti), in_=xt[:])

    def inproj(self, l, xsrc):
        P, S, TT = self.P, self.S, self.TT
        NH = TT // 512
        bfm, btm = self.b_fm.ap()[l], self.b_tm.ap()[l]
        with P.phase("inproj"):
            self.need_weights(("fm", l), ("tm", l))
            xts = [P.sb("xt", [128, NDC, TT], F32) for _ in range(2)]
            hTs = [P.sb("hT", [128, NDC, TT], BF16) for _ in range(2)]
            sq = P.sb("sq", [128, NDC, TT], BF16)
            rstd = P.sb("rstd", [128, TT], F32)
            wf = [P.sb("wf", [128, NDC, 128], BF16) for _ in range(3)]
            wt = P.sb("wt", [128, NDC, N_TM], BF16)
            zs = [P.sb("zs", [128, 512], F32) for _ in range(4)]
            vs = [P.sb("vs", [128, N_TM], F32) for _ in range(2)]
            xv = lambda t, ti: t.ap().rearrange("(c p) s -> p c s", p=128)[:, :, ti * TT:(ti + 1) * TT]
            nt = S // TT
            P.dma("sp", out=wt[:], in_=btm.rearrange("p (c n) -> p c n", c=NDC))
            sx = Stream(P, xts, [xv(xsrc, ti) for ti in range(nt)], look=0)
            sfm = Stream(P, wf, [bfm[j].rearrange("p (c n) -> p c n", c=NDC) for ti in range(nt) for j in range(N_FM)])
            zi = 0
            for p_ in self.norm_parts(l, C_MIX, sx.get(0), hTs[0], sq, rstd, TT):
                p_()
            for ti in range(nt):
                xt = sx.get(ti)
                hT = hTs[ti % 2]
                sfm.get(ti * N_FM)
                nxt = self.norm_parts(l, C_MIX, sx.get(ti + 1), hTs[(ti + 1) % 2], sq, rstd, TT) if ti + 1 < nt else None
                for j in range(N_FM):
                    w = sfm.get(ti * N_FM + j)
                    if nxt and j >= 1:
                        for _ in range(2):
                            if nxt:
                                nxt.pop(0)()
                    for hf in range(NH):
                        cs = slice(hf * 512, (hf + 1) * 512)
                        ps = P.psum()
                        for c in range(NDC):
                            P.op("pe", "matmul", out=ps[:], lhsT=w[:, c, :], rhs=hT[:, c, cs], start=(c == 0), stop=(c == NDC - 1))
                        z = zs[zi % 4]
                        if zi % 2 == 0:
                            P.op("act", "activation", out=z[:], in_=ps[:], func=AF.Copy)
                        else:
                            P.op("dve", "tensor_copy", out=z[:], in_=ps[:])
                        zi += 1
                        P.dma("sp", out=self.zT.ap()[j * 128:(j + 1) * 128, ti * TT + hf * 512: ti * TT + (hf + 1) * 512], in_=z[:])
                while nxt:
                    nxt.pop(0)()
                for tb in range(TT // 128):
                    ts_ = slice(tb * 128, (tb + 1) * 128)
                    v = vs[tb % 2]
                    for (c0, c1) in ((0, 512), (512, N_TM)):
                        ps = P.psum()
                        for c in range(NDC):
                            P.op("pe", "matmul", out=ps[:, 0:c1 - c0], lhsT=hT[:, c, ts_], rhs=wt[:, c, c0:c1], start=(c == 0), stop=(c == NDC - 1))
                        if c0 == 0:
                            P.op("act", "activation", out=v[:, c0:c1], in_=ps[:, 0:c1 - c0], func=AF.Copy)
                        else:
                            P.op("dve", "tensor_copy", out=v[:, c0:c1], in_=ps[:, 0:c1 - c0])
                    r0 = ti * TT + tb * 128
                    P.dma("sp", out=self.vtok.ap()[r0:r0 + 128, :], in_=v[:])

    def outproj(self, l, xsrc, xdst):
        P, S, TT = self.P, self.S, self.TT
        NH = TT // 512
        bgt, bbr, bout = self.b_gt.ap()[l], self.b_br.ap()[l], self.b_out.ap()[l]
        with P.phase("outproj"):
            self.need_weights(("gt", l), ("br", l), ("out", l))
            xts = [P.sb("xt", [128, NDC, TT], F32) for _ in range(2)]
            hTs = [P.sb("hT", [128, NDC, TT], BF16) for _ in range(2)]
            sq = P.sb("sq", [128, NDC, TT], BF16)
            rstd = P.sb("rstd", [128, TT], F32)
            ob = P.sb("ob", [128, 8, TT], BF16)
            yT = P.sb("yT", [128, NDC, TT], BF16)
            wgs = [P.sb("wgs", [128, NDC, 128], BF16) for _ in range(8)]
            wbs = [P.sb("wbs", [128, 2, 128], BF16) for _ in range(8)]
            wo = [P.sb("wo", [128, NDC, 128], BF16) for _ in range(2)]
            sg = [P.sb("sg", [128, 512], F32) for _ in range(2)]
            tb_ = [P.sb("tb", [128, 512], F32) for _ in range(4)]
            xv = lambda t, ti: t.ap().rearrange("(c p) s -> p c s", p=128)[:, :, ti * TT:(ti + 1) * TT]
            nt = S // TT
            sx = Stream(P, xts, [xv(xsrc, ti) for ti in range(nt)], look=0)
            order = [(b, d) for ti in range(nt) for d in range(NDC) for b in range(4)]
            sg1 = Stream(P, wgs, [bgt[b * 8 + d].rearrange("p (c n) -> p c n", c=NDC) for (b, d) in order], look=4)
            sg2 = Stream(P, wbs, [bbr[b * 8 + d].rearrange("p (c n) -> p c n", c=2) for (b, d) in order], look=4)
            swo = Stream(P, wo, [bout[d].rearrange("p (c n) -> p c n", c=NDC) for ti in range(nt) for d in range(NDC)])
            for p_ in self.norm_parts(l, C_MIX, sx.get(0), hTs[0], sq, rstd, TT):
                p_()
            for ti in range(nt):
                xt = sx.get(ti)
                hT = hTs[ti % 2]
                P.dma("sp", out=ob[:], in_=self.oT.ap().rearrange("b (c p) s -> p (b c) s", p=128)[:, :, ti * TT:(ti + 1) * TT])
                nxt = self.norm_parts(l, C_MIX, sx.get(ti + 1), hTs[(ti + 1) % 2], sq, rstd, TT) if ti + 1 < nt else None
                for d in range(NDC):
                    if nxt:
                        for _ in range(4):
                            if nxt:
                                nxt.pop(0)()
                    ws = []
                    for b in range(4):
                        i_ = (ti * NDC + d) * 4 + b
                        ws.append((sg1.get(i_), sg2.get(i_)))
                    for hf in range(NH):
                        cs = slice(hf * 512, (hf + 1) * 512)
                        for b in range(4):
                            w1, w2 = ws[b]
                            pg, pu = P.psum(), P.psum()
                            for c in range(NDC):
                                P.op("pe", "matmul", out=pg[:], lhsT=w1[:, c, :], rhs=hT[:, c, cs], start=(c == 0), stop=(c == NDC - 1))
                            for c in range(2):
                                P.op("pe", "matmul", out=pu[:], lhsT=w2[:, c, :], rhs=ob[:, b * 2 + c, cs], start=(c == 0), stop=(c == 1))
                            s_ = sg[b % 2]
                            P.op("act", "activation", out=s_[:], in_=pg[:], func=AF.Sigmoid)
                            P.op("dve", "tensor_tensor", out=tb_[b][:], in0=s_[:], in1=pu[:], op=ALU.mult)
                        P.op("pool", "tensor_tensor", out=tb_[0][:], in0=tb_[0][:], in1=tb_[1][:], op=ALU.add)
                        P.op("pool", "tensor_tensor", out=tb_[2][:], in0=tb_[2][:], in1=tb_[3][:], op=ALU.add)
                        P.op("pool", "tensor_tensor", out=yT[:, d, cs], in0=tb_[0][:], in1=tb_[2][:], op=ALU.add)
                while nxt:
                    nxt.pop(0)()
                for d in range(NDC):
                    w = swo.get(ti * NDC + d)
                    for hf in range(NH):
                        cs = slice(hf * 512, (hf + 1) * 512)
                        po = P.psum()
                        for c in range(NDC):
                            P.op("pe", "matmul", out=po[:], lhsT=w[:, c, :], rhs=yT[:, c, cs], start=(c == 0), stop=(c == NDC - 1))
                        P.op("dve", "tensor_tensor", out=xt[:, d, cs], in0=po[:], in1=xt[:, d, cs], op=ALU.add)
                P.dma("sp", out=xv(xdst, ti), in_=xt[:])

    def build(self, phases=("diff", "ml", "sb", "dsa")):
        P = self.P
        self.phases = phases
        skip = self.skip
        self.consts()
        if "cast" not in skip:
            self.cast_weights()
        if "rope" not in skip:
            self.rope_tables()
        x_in = self.xT
        for l in range(self.L):
            x1, x2 = self.xs[0], self.xs[1]
            last = (l == self.L - 1)
            if "ffn" not in skip:
                self.ffn(l, 0, x_in, x1)
            if "inproj" not in skip:
                self.inproj(l, x1)
            self.mixers(l)
            if "outproj" not in skip:
                self.outproj(l, x1, x2)
            x3 = self.outT if last else self.xs[2]
            if "ffn" not in skip:
                self.ffn(l, 1, x2, x3)
            x_in = x3
        P.barrier()
        P.emit()
        return self.nc

    def rope_tables(self):
        P, S = self.P, self.S
        TWO_PI = 2.0 * math.pi
        C1 = 6.28125
        C2 = TWO_PI - C1
        MAGIC = 12582912.0
        with P.phase("rope"):
            posi = P.sb("posi", [128, S], I32)
            posf = P.sb("posf", [128, S], F32)
            ang = P.sb("ang", [128, S], F32)
            a2 = P.sb("a2", [128, S], F32)
            tt = P.sb("tt", [128, S], F32)
            rr = [P.sb("rr", [128, S], F32) for _ in range(2)]
            P.dma("sp", out=posi[:], in_=self.pos.ap().broadcast_to([128, S]))
            P.op("dve", "tensor_copy", out=posf[:], in_=posi[:])
            k = 0
            for (ci, sg) in ((CC_INV32, CC_SGN32), (CC_INV64, CC_SGN64)):
                P.op("dve", "tensor_scalar", out=ang[:], in0=posf[:], scalar1=self.cc(ci), scalar2=None, op0=ALU.mult)
                for kind in (0, 1):
                    r = rr[k % 2]
                    P.op("dve", "tensor_scalar", out=a2[:], in0=ang[:], scalar1=(math.pi / 2 if kind == 0 else 0.0), scalar2=None, op0=ALU.add)
                    P.op("dve", "tensor_scalar", out=tt[:], in0=a2[:], scalar1=1.0 / TWO_PI, scalar2=MAGIC, op0=ALU.mult, op1=ALU.add)
                    P.op("dve", "tensor_scalar", out=tt[:], in0=tt[:], scalar1=-MAGIC, scalar2=None, op0=ALU.add)
                    P.op("dve", "scalar_tensor_tensor", out=a2[:], in0=tt[:], scalar=-C1, in1=a2[:], op0=ALU.mult, op1=ALU.add)
                    P.op("dve", "scalar_tensor_tensor", out=a2[:], in0=tt[:], scalar=-C2, in1=a2[:], op0=ALU.mult, op1=ALU.add)
                    P.op("dve", "tensor_scalar", out=a2[:], in0=a2[:], scalar1=3.1415925, scalar2=-3.1415925, op0=ALU.min, op1=ALU.max)
                    P.op("act", "activation", out=r[:], in_=a2[:], func=AF.Sin)
                    if kind == 1:
                        P.op("dve", "tensor_scalar", out=r[:], in0=r[:], scalar1=self.cc(sg), scalar2=None, op0=ALU.mult)
                    P.dma("sp", out=self.rope.ap()[k], in_=r[:])
                    k += 1

    def prep(self, l):
        P, S = self.P, self.S
        CW = min(S, 1024)
        with P.phase("prep"):
            tabs = [P.sb("tab", [128, CW], F32) for _ in range(4)]
            zin = [P.sb("zin", [128, CW + 3], F32) for _ in range(4)]
            sqs = [P.sb("sq", [128, CW], BF16) for _ in range(2)]
            rstds = [P.sb("rstd", [128, 512], F32) for _ in range(4)]
            qns = [P.sb("qn", [128, CW], F32) for _ in range(2)]
            t1s = [P.sb("t1", [128, 512], F32) for _ in range(4)]
            t2s = [P.sb("t2", [128, 512], F32) for _ in range(4)]
            outs = [P.sb("po", [128, CW], BF16) for _ in range(3)]
            zi = 0
            nr = [0, 0]

            def normrope(z, o, G, gcol, p0, p1):
                blk, perm = (self.BLK32, self.PERM32) if G == 32 else (self.BLK64, self.PERM64)
                ct, st = (tabs[0], tabs[1]) if G == 32 else (tabs[2], tabs[3])
                sq, qn = sqs[nr[0] % 2], qns[nr[0] % 2]
                nr[0] += 1
                if gcol is not None:
                    P.op("act", "activation", out=sq[:], in_=z, func=AF.Square)
                hs = []
                for hf in range(CW // 512):
                    k_ = nr[1] % 4
                    nr[1] += 1
                    hs.append(dict(cs=slice(hf * 512, (hf + 1) * 512), rstd=rstds[k_], t1=t1s[k_], t2=t2s[k_]))
                if gcol is not None:
                    for h_ in hs:
                        h_["ps"] = P.psum()
                        P.op("pe", "matmul", out=h_["ps"][:], lhsT=blk, rhs=sq[:, h_["cs"]], start=True, stop=True)
                    for h_ in hs:
                        P.op("act", "activation", out=h_["rstd"][:], in_=h_["ps"][:], func=AF.Ln, scale=1.0 / G, bias=self.cc(CC_EPS))
                    for h_ in hs:
                        P.op("act", "activation", out=h_["rstd"][:], in_=h_["rstd"][:], func=AF.Exp, scale=-0.5)
                    for h_ in hs:
                        P.op("dve", "scalar_tensor_tensor", out=qn[:, h_["cs"]], in0=z[:, h_["cs"]], scalar=self.col(l, gcol), in1=h_["rstd"][:],
                             op0=ALU.mult, op1=ALU.mult)
                else:
                    for h_ in hs:
                        P.op("pool", "tensor_copy", out=qn[:, h_["cs"]], in_=z[:, h_["cs"]])
                for h_ in hs:
                    h_["ps2"] = P.psum()
                    P.op("pe", "matmul", out=h_["ps2"][:], lhsT=perm, rhs=qn[:, h_["cs"]], start=True, stop=True)
                for h_ in hs:
                    P.op("pool", "tensor_tensor", out=h_["t2"][:], in0=qn[:, h_["cs"]], in1=ct[:, h_["cs"]], op=ALU.mult)
                for h_ in hs:
                    P.op("dve", "tensor_tensor", out=h_["t1"][:], in0=h_["ps2"][:], in1=st[:, h_["cs"]], op=ALU.mult)
                for h_ in hs:
                    P.op("dve", "tensor_tensor", out=o[p0:p1, h_["cs"]], in0=h_["t1"][p0:p1, :], in1=h_["t2"][p0:p1, :], op=ALU.add)

            srcs, outf, pre = [], [], []
            for ct_ in range(S // CW):
                c0 = ct_ * CW
                for j in range(N_FM):
                    rows = slice(j * 128, (j + 1) * 128)
                    conv = CH_MQ <= j < CH_MO
                    if conv and c0 > 0:
                        srcs.append(self.zT.ap()[rows, c0 - 3:c0 + CW]); outf.append(lambda b: b[:]); pre.append(None)
                    else:
                        srcs.append(self.zT.ap()[rows, c0:c0 + CW]); outf.append(lambda b: b[:, 3:])
                        pre.append((lambda b: P.op("pool", "memset", ap=b[:, 0:3], constant=0.0)) if conv else None)
            zs_ = Stream(P, zin, srcs, look=2, outf=outf, pre=pre)
            for ct_ in range(S // CW):
                c0 = ct_ * CW
                for k in range(4):
                    P.dma("sp", out=tabs[k][:], in_=self.rope.ap()[k][:, c0:c0 + CW])
                for j in range(N_FM):
                    zt = zs_.get(ct_ * N_FM + j)
                    o = outs[zi % 3]
                    zi += 1
                    rows = slice(j * 128, (j + 1) * 128)
                    conv = CH_MQ <= j < CH_MO
                    z = zt[:, 3:]
                    if j in (CH_DQ, CH_DQ + 1):
                        normrope(z, o, 32, C_DQN, 0, 128)
                    elif j in (CH_DK, CH_DK + 1):
                        normrope(z, o, 32, C_DKN, 0, 128)
                    elif conv:
                        kk = j - CH_MQ
                        qn = qns[nr[0] % 2]
                        nr[0] += 1
                        P.op("dve", "tensor_scalar", out=qn[:], in0=zt[:, 3:CW + 3], scalar1=self.col(l, C_CW + kk * 4 + 3),
                             scalar2=self.col(l, C_CB + kk), op0=ALU.mult, op1=ALU.add)
                        for tap in (2, 1, 0):
                            P.op("dve", "scalar_tensor_tensor", out=qn[:], in0=zt[:, tap:CW + tap], scalar=self.col(l, C_CW + kk * 4 + tap),
                                 in1=qn[:], op0=ALU.mult, op1=ALU.add)
                        P.op("act", "activation", out=o[:], in_=qn[:], func=AF.Silu)
                    elif j in (CH_MO, CH_MO + 1):
                        P.op("act", "activation", out=o[:], in_=z, func=AF.Sigmoid)
                    elif CH_SQ <= j < CH_AQ:
                        P.op("pool", "tensor_copy", out=o[:], in_=z)
                    elif j in (CH_AQ, CH_AQ + 1):
                        normrope(z, o, 64, C_SQN, 0, 128)
                    elif j in (CH_IQ, CH_IQ + 1):
                        normrope(z, o, 32, None, 0, 128)
                    else:
                        normrope(z, o, 64, C_SKN, 0, 64)
                        normrope(z, o, 32, None, 64, 96)
                        P.op("pool", "memset", ap=o[96:128, :], constant=0.0)
                    P.dma("sp", out=self.pT.ap()[rows, c0:c0 + CW], in_=o[:])

    def load_vaug(self, vaug, vst, col0, ones=True):
        P, S = self.P, self.S
        NB = S // 128
        P.dma("sp", out=vst[:], in_=self.vtok.ap().rearrange("(nb p) n -> p nb n", p=128)[:, :, col0:col0 + 64])
        P.op("act", "activation", out=vaug[:, :, 0:64], in_=vst[:], func=AF.Copy)
        if ones:
            P.op("pool", "memset", ap=vaug[:, :, 64:128], constant=1.0)

    def load_masked_q(self, qms, chunk, G):
        P = self.P
        for g, qm in enumerate(qms):
            P.op("pool", "memset", ap=qm[:], constant=0.0)
            P.dma("sp", out=qm[g * G:(g + 1) * G, :], in_=self.pT.ap()[chunk * 128 + g * G:chunk * 128 + (g + 1) * G, :])

    def head_norm_store(self, l, o32, gcol, scale, gate, dst, scr):
        P = self.P
        sqh, rs, ob = scr
        ops = []
        ops.append(lambda: P.op("act", "activation", out=sqh[0:64, :], in_=o32[0:64, :], func=AF.Square))

        def mm():
            ps = P.psum()
            P.op("pe", "matmul", out=ps[0:64, :], lhsT=self.k_matb[0:64, 2, 0:64], rhs=sqh[0:64, :], start=True, stop=True)
            P.op("act", "activation", out=rs[0:64, :], in_=ps[0:64, :], func=AF.Ln, scale=1.0 / 64, bias=self.cc(CC_EPS, 0, 64))
        ops.append(mm)
        ops.append(lambda: P.op("act", "activation", out=rs[0:64, :], in_=rs[0:64, :], func=AF.Exp, scale=-0.5))
        ops.append(lambda: P.op("dve", "scalar_tensor_tensor", out=o32[0:64, :], in0=o32[0:64, :], scalar=self.col(l, gcol, 0, 64),
                                in1=rs[0:64, :], op0=ALU.mult, op1=ALU.mult))
        if gate is not None:
            ops.append(lambda: P.op("dve", "tensor_tensor", out=ob[0:64, :], in0=o32[0:64, :], in1=gate, op=ALU.mult))
        else:
            ops.append(lambda: P.op("act", "activation", out=ob[0:64, :], in_=o32[0:64, :], func=AF.Copy, scale=float(scale)))
        ops.append(lambda: P.dma("sp", out=dst, in_=ob[0:64, :]))
        return ops

    def sb_attn(self, l):
        P, S = self.P, self.S
        NB = S // 128
        with P.phase("sb"):
            (acc,), zring, wring, tring = P.psum_pools(1, 3, 2, 2)
            qms2 = [[P.sb("qm", [128, S], BF16) for _ in range(2)] for _ in range(2)]
            kT2_ = [P.sb("kT", [128, S], BF16) for _ in range(2)]
            vst2 = [P.sb("vst", [128, NB, 64], F32) for _ in range(2)]
            V2 = [P.sb("V", [128, NB, 128], BF16) for _ in range(2)]

            def load_head(h):
                if h % 2 == 0:
                    self.load_masked_q(qms2[(h // 2) % 2], CH_SQ + h // 2, 64)
                    P.dma("sp", out=kT2_[(h // 2) % 2][:], in_=self.pT.ap()[(CH_SK + h // 2) * 128:(CH_SK + h // 2 + 1) * 128, :])
                self.load_vaug(V2[h % 2], vst2[h % 2], TM_SV + h * 64)
            R = P.sb("R", [128, 512], F32)
            E = [P.sb("E", [128, 512], F32) for _ in range(3)]
            spb = [P.sb("spb", [128, 512], BF16) for _ in range(3)]
            t1 = [P.sb("t1", [128, 512], F32) for _ in range(3)]
            Pm = [P.sb("Pm", [128, 512], BF16) for _ in range(3)]
            ob = [P.sb("ob", [64, 512], BF16) for _ in range(2)]
            zer = P.sb("zer", [128, 512], BF16)
            P.op("pool", "memset", ap=zer[:], constant=0.0)
            g = 0

            def stA(it):
                P.op("pe", "matmul", out=it["zp"][:, it["cs"]], lhsT=it["kT"][:, it["kb"] * 128:(it["kb"] + 1) * 128],
                     rhs=it["qm"][:, it["q0"] + it["c0"]:it["q0"] + 512], start=True, stop=True)

            def stB(it):
                cs, c0 = it["cs"], it["c0"]
                P.op("act", "activation", out=it["e"][:, cs], in_=it["zp"][:, cs], func=AF.Exp, scale=0.125)
                P.op("act", "activation", out=it["s"][:, cs], in_=it["e"][:, cs], func=AF.Ln, bias=1.0)
                if it["j"] >= 0:
                    P.op("pool", "tensor_tensor", out=it["s"][:, c0:c0 + 128], in0=it["s"][:, c0:c0 + 128], in1=self.MLT, op=ALU.mult)

            def stC(it):
                cs = it["cs"]
                if it["first"]:
                    P.op("pool", "memset", ap=R[:], constant=0.0)
                P.op("pe", "matmul", out=it["wp"][:, cs], lhsT=self.UGE, rhs=it["s"][:, cs], start=True, stop=True)
                P.op("pe", "matmul", out=it["tp"][:, cs], lhsT=self.ONES, rhs=it["s"][:, cs], start=True, stop=True)
                pv = it["prev"]
                if pv is not None:
                    P.op("dve", "tensor_tensor", out=R[:, pv["cs"]], in0=R[:, pv["cs"]], in1=pv["tp"][:, pv["cs"]], op=ALU.subtract)
                P.op("dve", "scalar_tensor_tensor", _after=[it["e"]], out=it["t"][:, cs], in0=it["zp"][:, cs], scalar=0.125, in1=R[:, cs],
                     op0=ALU.mult, op1=ALU.add)

            def stD(it):
                cs, c0 = it["cs"], it["c0"]
                P.op("dve", "tensor_tensor", out=it["t"][:, cs], in0=it["t"][:, cs], in1=it["wp"][:, cs], op=ALU.subtract)
                P.op("act", "activation", out=it["p"][:, cs], in_=it["t"][:, cs], func=AF.Exp)
                if it["j"] >= 0:
                    P.op("pool", "tensor_tensor", out=it["p"][:, c0:c0 + 128], in0=it["p"][:, c0:c0 + 128], in1=self.MLT, op=ALU.mult)

            def stE(it):
                if it["first"]:
                    P.op("pe", "matmul", out=acc[:, :], lhsT=it["V"][:, 0, :], rhs=zer[:], start=True, stop=False, skip_group_check=True)
                P.op("pe", "matmul", out=acc[:, it["cs"]], lhsT=it["V"][:, it["kb"], :], rhs=it["p"][:, it["cs"]], start=False, stop=it["last"], skip_group_check=True)
                if it["last"]:
                    o_ = ob[it["oi"] % 2]
                    P.op("act", "activation", out=o_[:], in_=acc[0:64, :], func=AF.Copy)
                    P.dma("sp", out=self.oT.ap()[2, it["h"] * 64:(it["h"] + 1) * 64, it["q0"]:it["q0"] + 512], in_=o_[:])

            for h in range(4):
                r0 = (h % 2) * 64
                if h == 0:
                    load_head(0)
                if h + 1 < 4:
                    load_head(h + 1)
                qm, kT, V = qms2[(h // 2) % 2][h % 2], kT2_[(h // 2) % 2], V2[h % 2]
                iters = []
                for qt in range(S // 512):
                    q0 = qt * 512
                    nkb = 4 * qt + 4
                    prev = None
                    for kb in range(nkb - 1, -1, -1):
                        j = kb - 4 * qt
                        c0 = 128 * max(j, 0)
                        it = dict(kb=kb, j=j, c0=c0, cs=slice(c0, 512), q0=q0, qm=qm, zp=zring[g % 3], wp=wring[g % 2], tp=tring[g % 2],
                                  e=E[g % 3], s=spb[g % 3], t=t1[g % 3], p=Pm[g % 3], first=(kb == nkb - 1), last=(kb == 0), prev=prev, kT=kT, V=V,
                                  h=h, oi=h * 8 + qt)
                        g += 1
                        iters.append(it)
                        prev = it
                self.pipe(iters, (stA, stB, stC, stD, stE), (0, 1, 2, 3, 4))

    def diff_attn(self, l):
        P, S = self.P, self.S
        NB = S // 128
        lam_init = 0.8 - 0.6 * math.exp(-0.3 * l)
        with P.phase("diff"):
            accA, accB, zring = P.psum_pools(2, 2, 3)
            acc_sets = (accA, accB)
            pend = []
            nq = 0
            qms2 = [[P.sb("qm", [128, S], BF16) for _ in range(4)] for _ in range(2)]
            kT2_ = [P.sb("kT", [128, S], BF16) for _ in range(2)]
            vst2 = [P.sb("vst", [128, NB, 64], F32) for _ in range(2)]
            V2 = [P.sb("V", [128, NB, 128], BF16) for _ in range(2)]

            def load_head(h):
                if h % 2 == 0:
                    self.load_masked_q(qms2[(h // 2) % 2], CH_DQ + h // 2, 32)
                    P.dma("sp", out=kT2_[(h // 2) % 2][:], in_=self.pT.ap()[(CH_DK + h // 2) * 128:(CH_DK + h // 2 + 1) * 128, :])
                self.load_vaug(V2[h % 2], vst2[h % 2], TM_DV + h * 64)
            Pm = [P.sb("Pm", [128, 512], BF16) for _ in range(4)]

            def stA(i_):
                P.op("pe", "matmul", out=i_["zp"][:, i_["cs"]], lhsT=i_["kT"][:, i_["kb"] * 128:(i_["kb"] + 1) * 128],
                     rhs=i_["qm"][:, i_["q0"] + i_["c0"]:i_["q0"] + 512], start=True, stop=True)

            def stB(i_):
                cs, c0 = i_["cs"], i_["c0"]
                P.op("act", "activation", out=i_["p"][:, cs], in_=i_["zp"][:, cs], func=AF.Exp, scale=32 ** -0.5)
                if i_["j"] >= 0:
                    P.op("pool", "tensor_tensor", out=i_["p"][:, c0:c0 + 128], in0=i_["p"][:, c0:c0 + 128], in1=self.MLE, op=ALU.mult)

            def stC(i_):
                P.op("pe", "matmul", out=i_["acc"][i_["c"]][:, i_["cs"]], lhsT=i_["V"][:, i_["kb"], :], rhs=i_["p"][:, i_["cs"]],
                     start=i_["first"], stop=i_["last"], skip_group_check=True)
            lt = P.sb("lt", [128, 128], F32)
            lp = P.sb("lp", [128, 64], F32)
            ls = P.sb("ls", [128, 4], F32)
            rr = [P.sb("rr", [64, 512], F32) for _ in range(2)]
            tt = [P.sb("tt", [64, 512], F32) for _ in range(2)]
            scr = (P.sb("sqh", [64, 512], BF16), P.sb("rs", [64, 512], F32), P.sb("obf", [64, 512], BF16))
            P.dma("sp", out=lt[:], in_=self.lamb.ap()[l])
            P.op("dve", "tensor_tensor", out=lp[:, 0:32], in0=lt[:, 0:32], in1=lt[:, 32:64], op=ALU.mult)
            P.op("dve", "tensor_tensor", out=lp[:, 32:64], in0=lt[:, 64:96], in1=lt[:, 96:128], op=ALU.mult)
            P.op("dve", "tensor_reduce", out=ls[:, 0:1], in_=lp[:, 0:32], axis=AX.X, op=ALU.add)
            P.op("dve", "tensor_reduce", out=ls[:, 1:2], in_=lp[:, 32:64], axis=AX.X, op=ALU.add)
            P.op("act", "activation", out=ls[:, 0:2], in_=ls[:, 0:2], func=AF.Exp)
            P.op("dve", "tensor_tensor", out=ls[:, 2:3], in0=ls[:, 1:2], in1=ls[:, 0:1], op=ALU.subtract)
            P.op("dve", "tensor_scalar", out=ls[:, 3:4], in0=ls[:, 2:3], scalar1=-lam_init, scalar2=None, op0=ALU.add)
            nlam = ls[0:64, 3:4]
            it = 0
            for h in range(4):
                r0 = (h % 2) * 64
                if h == 0:
                    load_head(0)
                if h + 1 < 4:
                    load_head(h + 1)
                qms, kT, V = qms2[(h // 2) % 2], kT2_[(h // 2) % 2], V2[h % 2]
                for qt in range(S // 512):
                    q0 = qt * 512
                    nkb = 4 * qt + 4
                    acc = acc_sets[nq % 2]
                    nq += 1
                    iters = []
                    for c in range(2):
                        for kb in range(nkb):
                            j = kb - 4 * qt
                            c0 = 128 * max(j, 0)
                            iters.append(dict(c=c, kb=kb, j=j, c0=c0, cs=slice(c0, 512), q0=q0, qm=qms[(h % 2) * 2 + c], kT=kT, V=V, zp=zring[it % 3], p=Pm[it % 4], acc=acc,
                                              first=(kb == 0), last=(kb == nkb - 1)))
                            it += 1
                    self.pipe(iters, (stA, stB, stC), (0, 1, 2), drip=pend)

                    def epi(acc=acc, h=h, q0=q0):
                        ops = []
                        for c in range(2):
                            ops.append(lambda c=c: P.op("act", "activation", out=rr[c][:], in_=acc[c][64:128, :], func=AF.Ln))
                            ops.append(lambda c=c: P.op("act", "activation", out=rr[c][:], in_=rr[c][:], func=AF.Exp, scale=-1.0))
                            ops.append(lambda c=c: P.op("dve", "tensor_tensor", out=tt[c][:], in0=acc[c][0:64, :], in1=rr[c][:], op=ALU.mult))
                        ops.append(lambda: P.op("dve", "scalar_tensor_tensor", out=tt[0][:], in0=tt[1][:], scalar=nlam, in1=tt[0][:],
                                                op0=ALU.mult, op1=ALU.add))
                        ops += self.head_norm_store(l, tt[0], C_DHN, 1.0 - lam_init, None, self.oT.ap()[0, h * 64:(h + 1) * 64, q0:q0 + 512], scr)
                        return ops
                    pend = epi()
            while pend:
                pend.pop(0)()

    def mlstm(self, l):
        P, S = self.P, self.S
        NB = S // 128
        g0 = CH_MISC * 128 + 96
        with P.phase("mlg"):
            zi = P.sb("zi", [4, S], F32)
            zf = P.sb("zf", [4, S], F32)
            on = P.sb("on", [4, S], F32)
            cs_ = P.sb("cs", [4, S], F32)
            P.dma("sp", out=zi[:], in_=self.zT.ap()[g0:g0 + 4, :])
            P.dma("sp", out=zf[:], in_=self.zT.ap()[g0 + 4:g0 + 8, :])
            P.op("pool", "memset", ap=on[:], constant=1.0)
            P.op("dve", "tensor_scalar", out=zi[:], in0=zi[:], scalar1=self.col(l, C_GB, 0, 4), scalar2=None, op0=ALU.add)
            P.op("dve", "tensor_scalar", out=zf[:], in0=zf[:], scalar1=self.col(l, C_GB + 1, 0, 4), scalar2=None, op0=ALU.add)
            P.op("act", "activation", out=zf[:], in_=zf[:], func=AF.Exp, scale=-1.0)
            P.op("act", "activation", out=zf[:], in_=zf[:], func=AF.Ln, bias=1.0)
            src, dst = zf, on
            d_ = 1
            while d_ < S:
                P.op("dve", "tensor_copy", out=dst[:, 0:d_], in_=src[:, 0:d_])
                P.op("dve", "tensor_tensor", out=dst[:, d_:S], in0=src[:, d_:S], in1=src[:, 0:S - d_], op=ALU.add)
                src, dst = dst, src
                d_ *= 2
            P.op("dve", "tensor_copy", out=cs_[:], in_=src[:])
            P.op("dve", "tensor_tensor", out=zi[:], in0=zi[:], in1=cs_[:], op=ALU.add)
            P.op("dve", "tensor_scalar", out=cs_[:], in0=cs_[:], scalar1=-1.0, scalar2=None, op0=ALU.mult)
            P.dma("sp", out=self.grow.ap()[0:4, :], in_=cs_[:])
            P.dma("sp", out=self.grow.ap()[4:8, :], in_=zi[:])
        with P.phase("ml"):
            accs, zring = P.psum_pools(2, 4)
            pend = []
            nq = 0
            qms = [P.sb("qm", [128, S], BF16) for _ in range(2)]
            kT = P.sb("kT", [128, S], BF16)
            og = P.sb("og", [64, S], BF16)
            vst = P.sb("vst", [128, NB, 64], F32)
            V = P.sb("V", [128, NB, 128], BF16)
            Bbc = P.sb("Bbc", [128, S], F32)
            acol = P.sb("acol", [128, NB], F32)
            D = [P.sb("D", [128, 512], F32) for _ in range(3)]
            Pm = [P.sb("Pm", [128, 512], BF16) for _ in range(4)]

            def stA(i_):
                cs = i_["cs"]
                P.op("pe", "matmul", out=i_["zp"][:, cs], lhsT=kT[:, i_["kb"] * 128:(i_["kb"] + 1) * 128],
                     rhs=i_["qm"][:, i_["q0"] + i_["c0"]:i_["q0"] + 512], start=True, stop=True)
                P.op("act", "activation", out=i_["d"][:, cs], in_=Bbc[:, i_["q0"] + i_["c0"]:i_["q0"] + 512], func=AF.Exp,
                     bias=acol[:, i_["kb"]:i_["kb"] + 1])

            def stB(i_):
                cs, c0 = i_["cs"], i_["c0"]
                P.op("dve", "scalar_tensor_tensor", out=i_["p"][:, cs], in0=i_["zp"][:, cs], scalar=0.125, in1=i_["d"][:, cs],
                     op0=ALU.mult, op1=ALU.mult)
                if i_["j"] >= 0:
                    P.op("pool", "tensor_tensor", out=i_["p"][:, c0:c0 + 128], in0=i_["p"][:, c0:c0 + 128], in1=self.MLE, op=ALU.mult)

            def stC(i_):
                P.op("pe", "matmul", out=i_["acc"][:, i_["cs"]], lhsT=V[:, i_["kb"], :], rhs=i_["p"][:, i_["cs"]], start=i_["first"], stop=i_["last"], skip_group_check=True)
            dd = P.sb("dd", [64, 512], F32)
            hh = P.sb("hh", [64, 512], F32)
            scr = (P.sb("sqh", [64, 512], BF16), P.sb("rs", [64, 512], F32), P.sb("obf", [64, 512], BF16))
            it = 0
            for h in range(4):
                r0 = (h % 2) * 64
                if h % 2 == 0:
                    self.load_masked_q(qms, CH_MQ + h // 2, 64)
                    P.dma("sp", out=kT[:], in_=self.pT.ap()[(CH_MK + h // 2) * 128:(CH_MK + h // 2 + 1) * 128, :])
                P.dma("sp", out=og[:], in_=self.pT.ap()[(CH_MO + h // 2) * 128 + r0:(CH_MO + h // 2) * 128 + r0 + 64, :])
                P.dma("sp", out=Bbc[:], in_=self.grow.ap()[h:h + 1, :].broadcast_to([128, S]))
                P.dma("sp", out=acol[:], in_=self.grow.ap()[4 + h].rearrange("(nb p) -> p nb", p=128), allow_slow_non_contiguous=True)
                self.load_vaug(V, vst, TM_MV + h * 64)
                for qt in range(S // 512):
                    q0 = qt * 512
                    nkb = 4 * qt + 4
                    acc = accs[nq % 2]
                    nq += 1
                    iters = []
                    for kb in range(nkb):
                        j = kb - 4 * qt
                        c0 = 128 * max(j, 0)
                        iters.append(dict(kb=kb, j=j, c0=c0, cs=slice(c0, 512), q0=q0, qm=qms[h % 2], zp=zring[it % 4], d=D[it % 3], p=Pm[it % 4], acc=acc,
                                          first=(kb == 0), last=(kb == nkb - 1)))
                        it += 1
                    self.pipe(iters, (stA, stB, stC), (0, 1, 2), drip=pend)

                    def epi(acc=acc, h=h, q0=q0):
                        ops = [lambda: P.op("act", "activation", out=dd[:], in_=acc[64:128, :], func=AF.Abs),
                               lambda: P.op("dve", "tensor_scalar", out=dd[:], in0=dd[:], scalar1=1.0, scalar2=None, op0=ALU.max),
                               lambda: P.op("act", "activation", out=dd[:], in_=dd[:], func=AF.Ln),
                               lambda: P.op("act", "activation", out=dd[:], in_=dd[:], func=AF.Exp, scale=-1.0),
                               lambda: P.op("dve", "tensor_tensor", out=hh[:], in0=acc[0:64, :], in1=dd[:], op=ALU.mult)]
                        ops += self.head_norm_store(l, hh, C_MHN, 1.0, og[:, q0:q0 + 512], self.oT.ap()[1, h * 64:(h + 1) * 64, q0:q0 + 512], scr)
                        return ops
                    pend = epi()
                while pend:
                    pend.pop(0)()

    def dsa(self, l):
        P, S = self.P, self.S
        NB = S // 128
        NIT = 16
        with P.phase("dsa"):
            acc, zring = P.psum_pools(4, 4)
            P.ring = [4, 5, 6, 7]
            qms = [P.sb("qm", [128, S], BF16) for _ in range(4)]
            kT2 = P.sb("kT2", [128, S], BF16)
            qiT = [P.sb("qiT", [128, S], BF16) for _ in range(2)]
            kiT4 = P.sb("kiT4", [128, S], BF16)
            wq = P.sb("wq", [128, NB, 8], F32)
            V = P.sb("V", [128, NB, 128], BF16)
            score = P.sb("score", [128, S], F32)
            scoreB = P.sb("scoreB", [128, S], F32)
            msel = P.sb("msel", [128, S], F32)
            vst = msel[:, 0:NB * 64].rearrange("p (a b) -> p a b", b=64)
            junk = P.sb("junk", [128, S], BF16)
            junkB = P.sb("junkB", [128, S], BF16)
            smB = P.sb("smB", [128, 8], F32)
            wkB = P.sb("wkB", [128, 32], F32)
            nwkB = P.sb("nwkB", [128, 32], F32)
            nm = P.sb("nm", [128, 2], F32)
            maskT = P.sb("maskT", [128, NB, 512], BF16)
            rl = [P.sb("rl", [128, 512], F32) for _ in range(3)]
            E = [P.sb("E", [128, 512], BF16) for _ in range(4)]
            Pm = [P.sb("Pm", [128, 512], BF16) for _ in range(4)]

            def stA(i_):
                P.op("pe", "matmul", out=i_["zp"][:, i_["cs"]], lhsT=kT2[:, i_["kb"] * 128:(i_["kb"] + 1) * 128],
                     rhs=qms[i_["h"]][:, i_["q0"] + i_["c0"]:i_["q0"] + 512], start=True, stop=True)

            def stB(i_):
                cs = i_["cs"]
                P.op("act", "activation", out=i_["e"][:, cs], in_=i_["zp"][:, cs], func=AF.Exp, scale=0.125)
                P.op("dve", "tensor_tensor", out=i_["p"][:, cs], in0=i_["e"][:, cs], in1=maskT[:, i_["kb"], cs], op=ALU.mult)

            def stC(i_):
                P.op("pe", "matmul", out=acc[i_["h"]][:, i_["cs"]], lhsT=V[:, i_["kb"], :], rhs=i_["p"][:, i_["cs"]],
                     start=i_["first"], stop=i_["last"], skip_group_check=True)
            sm = P.sb("sm", [128, 8], F32)
            wk = P.sb("wk", [128, 32], F32)
            nwk = P.sb("nwk", [128, 32], F32)
            rr = P.sb("rr", [64, 512], F32)
            ob = [P.sb("ob", [64, 512], BF16) for _ in range(2)]
            for c in range(2):
                self.load_masked_q(qms[2 * c:2 * c + 2], CH_AQ + c, 64)
                P.dma("sp", out=qiT[c][:], in_=self.pT.ap()[(CH_IQ + c) * 128:(CH_IQ + c + 1) * 128, :])
                P.dma("sp", out=kT2[c * 64:(c + 1) * 64, :], in_=self.pT.ap()[CH_MISC * 128:CH_MISC * 128 + 64, :])
            for c in range(4):
                P.dma("sp", out=kiT4[c * 32:(c + 1) * 32, :], in_=self.pT.ap()[CH_MISC * 128 + 64:CH_MISC * 128 + 96, :])
            P.dma("sp", out=wq[:], in_=self.vtok.ap().rearrange("(nb p) n -> p nb n", p=128)[:, :, TM_IW:TM_IW + 8])
            self.load_vaug(V, vst, TM_AV)
            it = 0

            def indexer(qb, sc):
                nonlocal it
                n = (qb + 1) * 128
                for kc in range((n + 511) // 512):
                    nk = min(512, n - kc * 512)
                    ks = slice(kc * 512, kc * 512 + nk)
                    for hh in range(8):
                        g_ = hh % 4
                        rp = P.psum()
                        r_ = rl[it % 3]
                        it += 1
                        P.op("pe", "matmul", out=rp[:, 0:nk], lhsT=qiT[hh // 4][32 * g_:32 * g_ + 32, qb * 128:(qb + 1) * 128],
                             rhs=kiT4[32 * g_:32 * g_ + 32, ks], start=True, stop=True, tile_position=(32 * g_, 0))
                        P.op("act", "activation", out=r_[:, 0:nk], in_=rp[:, 0:nk], func=AF.Relu)
                        if hh == 0:
                            P.op("dve", "tensor_scalar", out=sc[:, ks], in0=r_[:, 0:nk], scalar1=wq[:, qb, 0:1], scalar2=None, op0=ALU.mult)
                        else:
                            P.op("dve", "scalar_tensor_tensor", out=sc[:, ks], in0=r_[:, 0:nk], scalar=wq[:, qb, hh:hh + 1], in1=sc[:, ks],
                                 op0=ALU.mult, op1=ALU.add)
                P.op("pool", "tensor_tensor", out=sc[:, qb * 128:n], in0=sc[:, qb * 128:n], in1=self.ADDMASK, op=ALU.add)

            def bis_setup(qb, sc, sm_, wk_, nwk_):
                n = (qb + 1) * 128
                P.op("dve", "tensor_reduce", out=sm_[:, 5:6], in_=sc[:, 0:n], axis=AX.X, op=ALU.max)
                P.op("dve", "tensor_reduce", out=sm_[:, 0:1], in_=sc[:, 0:qb * 128], axis=AX.X, op=ALU.min)
                P.op("dve", "tensor_tensor", out=sm_[:, 1:2], in0=sm_[:, 5:6], in1=sm_[:, 0:1], op=ALU.subtract)
                P.op("dve", "tensor_scalar", out=sm_[:, 1:2], in0=sm_[:, 1:2], scalar1=1.0001, scalar2=1e-6, op0=ALU.mult, op1=ALU.add)
                P.op("dve", "tensor_scalar", out=wk_[:, 0:NIT + 1], in0=self.POW2[:, 0:NIT + 1], scalar1=sm_[:, 1:2], scalar2=None, op0=ALU.mult)
                P.op("dve", "tensor_scalar", out=nwk_[:, 0:NIT + 1], in0=wk_[:, 0:NIT + 1], scalar1=-1.0, scalar2=None, op0=ALU.mult)

            def transposes(qb, j4):
                for kb0 in range(0, qb + 1, 4):
                    nb_ = min(4, qb + 1 - kb0)
                    tp = P.psum()
                    for i in range(nb_):
                        P.op("pe", "transpose", out=tp[:, i * 128:(i + 1) * 128], in_=msel[:, (kb0 + i) * 128:(kb0 + i + 1) * 128], identity=self.IDENTF)
                    P.op("act", "activation", out=maskT[:, kb0:kb0 + nb_, j4 * 128:(j4 + 1) * 128],
                         in_=tp[:, 0:nb_ * 128].rearrange("p (a b) -> p a b", a=nb_), func=AF.Copy)

            for qt in range(S // 512):
                q0 = qt * 512
                for jp in range(2):
                    qa, qb_ = 4 * qt + 2 * jp, 4 * qt + 2 * jp + 1
                    na, nb2 = (qa + 1) * 128, (qb_ + 1) * 128
                    indexer(qa, score)
                    indexer(qb_, scoreB)
                    if qa < 2:
                        P.op("dve", "tensor_scalar", out=msel[:, 0:na], in0=score[:, 0:na], scalar1=-1.0e29, scalar2=None, op0=ALU.is_ge)
                        transposes(qa, 2 * jp)
                        P.op("dve", "tensor_scalar", out=msel[:, 0:nb2], in0=scoreB[:, 0:nb2], scalar1=-1.0e29, scalar2=None, op0=ALU.is_ge)
                        transposes(qb_, 2 * jp + 1)
                        continue
                    bis_setup(qa, score, sm, wk, nwk)
                    bis_setup(qb_, scoreB, smB, wkB, nwkB)
                    P.op("dve", "tensor_tensor", out=sm[:, 2:3], in0=sm[:, 0:1], in1=wk[:, 0:1], op=ALU.add)
                    P.op("dve", "scalar_tensor_tensor", out=nm[:, 0:1], in0=smB[:, 0:1], scalar=-1.0, in1=nwkB[:, 0:1], op0=ALU.mult, op1=ALU.add)
                    for k in range(NIT):
                        P.op("dve", "tensor_scalar", out=junk[:, 0:na], in0=score[:, 0:na], scalar1=sm[:, 2:3], scalar2=None,
                             op0=ALU.is_ge, op1=ALU.add, accum_out=sm[:, 3:4])
                        P.op("dve", "scalar_tensor_tensor", out=sm[:, 4:5], in0=sm[:, 3:4], scalar=TOPK - 0.5, in1=wk[:, k:k + 1],
                             op0=ALU.is_ge, op1=ALU.mult)
                        P.op("dve", "scalar_tensor_tensor", out=sm[:, 2:3], in0=sm[:, 4:5], scalar=nwk[:, k + 1:k + 2], in1=sm[:, 2:3],
                             op0=ALU.add, op1=ALU.add)
                        P.op("act", "activation", out=junkB[:, 0:nb2], in_=scoreB[:, 0:nb2], func=AF.Sign, bias=nm[:, k % 2:k % 2 + 1],
                             accum_out=smB[:, 3:4])
                        P.op("act", "activation", out=smB[:, 4:5], in_=smB[:, 3:4], func=AF.Sign, bias=float(nb2 - 2 * TOPK + 1))
                        P.op("act", "activation", out=nm[:, (k + 1) % 2:(k + 1) % 2 + 1], in_=smB[:, 4:5], func=AF.Identity,
                             scale=nwkB[:, k + 1:k + 2], bias=nm[:, k % 2:k % 2 + 1])
                    P.op("dve", "tensor_tensor", out=sm[:, 6:7], in0=sm[:, 2:3], in1=nwk[:, NIT:NIT + 1], op=ALU.add)
                    P.op("dve", "scalar_tensor_tensor", out=smB[:, 6:7], in0=nm[:, NIT % 2:NIT % 2 + 1], scalar=-1.0, in1=nwkB[:, NIT:NIT + 1],
                         op0=ALU.mult, op1=ALU.add)
                    P.op("dve", "tensor_scalar", out=msel[:, 0:na], in0=score[:, 0:na], scalar1=sm[:, 6:7], scalar2=None, op0=ALU.is_ge)
                    transposes(qa, 2 * jp)
                    P.op("dve", "tensor_scalar", out=msel[:, 0:nb2], in0=scoreB[:, 0:nb2], scalar1=smB[:, 6:7], scalar2=None, op0=ALU.is_ge)
                    transposes(qb_, 2 * jp + 1)
                nkb = 4 * qt + 4
                iters = []
                for kb in range(nkb):
                    j = kb - 4 * qt
                    c0 = 128 * max(j, 0)
                    for h in range(4):
                        iters.append(dict(h=h, kb=kb, j=j, c0=c0, cs=slice(c0, 512), q0=q0, zp=zring[it % 4], e=E[it % 4], p=Pm[it % 4],
                                          first=(kb == 0), last=(kb == nkb - 1)))
                        it += 1
                self.pipe(iters, (stA, stB, stC), (0, 1, 2))
                for h in range(4):
                    o_ = ob[h % 2]
                    P.op("act", "activation", out=rr[:], in_=acc[h][64:128, :], func=AF.Ln)
                    P.op("act", "activation", out=rr[:], in_=rr[:], func=AF.Exp, scale=-1.0)
                    P.op("dve", "tensor_tensor", out=o_[:], in0=acc[h][0:64, :], in1=rr[:], op=ALU.mult)
                    P.dma("sp", out=self.oT.ap()[3, h * 64:(h + 1) * 64, q0:q0 + 512], in_=o_[:])

    def mixers(self, l):
        ph = self.phases
        if "prep" not in self.skip:
            self.prep(l)
        if "diff" in ph:
            self.diff_attn(l)
        if "ml" in ph:
            self.mlstm(l)
        if "sb" in ph:
            self.sb_attn(l)
        if "dsa" in ph:
            self.dsa(l)


_IN_OFF = {}
_o = 0
for _n, _w in (("diff_q", 256), ("diff_k", 256), ("diff_v", 256), ("ml_qk", 512), ("ml_v", 256), ("ml_i", 4), ("ml_f", 4),
               ("ml_o", 256), ("sb_q", 256), ("sb_k", 256), ("sb_v", 256), ("dsa_q", 256), ("dsa_k", 64), ("dsa_v", 64),
               ("idx_q", 256), ("idx_k", 32), ("idx_w", 8), ("gates", 4096)):
    _IN_OFF[_n] = (_o, _o + _w)
    _o += _w


def _slab(w, n):
    K, N = w.shape
    return np.ascontiguousarray(w.reshape(K // 128, 128, N // n, n).transpose(2, 1, 0, 3).reshape(N // n, 128, (K // 128) * n))


def _cols(*names):
    idx = []
    for nm in names:
        a, b = _IN_OFF[nm]
        idx.extend(range(a, b))
    return np.array(idx)


def make_constants():
    p = np.arange(128)
    ccol = np.zeros((128, 8), np.float32)
    d32, d64 = p % 32, p % 64
    ccol[:, CC_INV32] = np.where(d32 < 8, ROPE_THETA ** (-(d32 % 4) / 4.0), 0.0)
    ccol[:, CC_SGN32] = np.where(d32 < 4, -1.0, np.where(d32 < 8, 1.0, 0.0))
    ccol[:, CC_INV64] = np.where(d64 < 16, ROPE_THETA ** (-(d64 % 8) / 8.0), 0.0)
    ccol[:, CC_SGN64] = np.where(d64 < 8, -1.0, np.where(d64 < 16, 1.0, 0.0))
    ccol[:, CC_EPS] = EPS
    ccol[:, CC_ONE] = 1.0
    i, j = p[:, None], p[None, :]
    cmatb = np.zeros((128, 6, 128), np.float32)
    cmatb[:, 0] = 1.0
    cmatb[:, 1] = (i // 32 == j // 32)
    cmatb[:, 2] = (i // 64 == j // 64)
    cmatb[:, 3] = (i >= j)
    cmatb[:, 4] = (i <= j)
    cmatb[:, 5] = (i < j)
    cmatf = np.zeros((128, 5, 128), np.float32)
    cmatf[:, 4, :] = (2.0 ** -(np.arange(128, dtype=np.float64) + 1.0)).astype(np.float32)[None, :]
    part32 = np.where(d32 < 4, p + 4, np.where(d32 < 8, p - 4, -1))
    part64 = np.where(d64 < 8, p + 8, np.where(d64 < 16, p - 8, -1))
    for m in range(128):
        if part32[m] >= 0:
            cmatf[part32[m], 0, m] = 1.0
        if part64[m] >= 0:
            cmatf[part64[m], 1, m] = 1.0
    cmatf[:, 2] = np.where(j <= i, 0.0, NEG)
    cmatf[:, 3] = (i == j)
    return ccol, cmatb, cmatf


def prep_shared(inp):
    L = inp["w_in"].shape[0]
    f32 = lambda a: np.ascontiguousarray(a, dtype=np.float32)
    out = {}
    for i, nm in ((1, "ffn1"), (2, "ffn2")):
        gu = np.asarray(inp[f"{nm}_w_gu"])
        dn = np.asarray(inp[f"{nm}_w_down"])
        g2 = np.concatenate([gu[:, :, :D_FF].reshape(L, D_MODEL, NFF, 128), gu[:, :, D_FF:].reshape(L, D_MODEL, NFF, 128)], axis=3)
        out[f"gu{i}"] = f32(np.stack([_slab(g2[l].reshape(D_MODEL, NFF * 256), 256) for l in range(L)]))
        out[f"dn{i}"] = f32(np.stack([_slab(dn[l], 128) for l in range(L)]))
    w_in = np.asarray(inp["w_in"])
    fm_idx = _cols("diff_q", "diff_k", "ml_qk", "ml_o", "sb_q", "sb_k", "dsa_q", "idx_q", "dsa_k", "idx_k", "ml_i", "ml_f")
    tm_idx = _cols("diff_v", "ml_v", "sb_v", "dsa_v", "idx_w")
    wfm = np.zeros((L, D_MODEL, N_FM * 128), np.float32)
    wfm[:, :, :len(fm_idx)] = w_in[:, :, fm_idx]
    out["wfm"] = f32(np.stack([_slab(wfm[l], 128) for l in range(L)]))
    wtm = w_in[:, :, tm_idx]
    out["wtm"] = f32(wtm.reshape(L, NDC, 128, N_TM).transpose(0, 2, 1, 3).reshape(L, 128, NDC * N_TM))
    g0 = _IN_OFF["gates"][0]
    out["wgt"] = f32(np.stack([_slab(w_in[l][:, g0:], 128) for l in range(L)]))
    wb = np.asarray(inp["w_branch"])
    out["wbr"] = f32(np.stack([np.concatenate([_slab(wb[l, b], 128) for b in range(4)], axis=0) for l in range(L)]))
    out["wout"] = f32(np.stack([_slab(np.asarray(inp["w_out"])[l], 128) for l in range(L)]))
    p = np.arange(128)
    colp = np.zeros((L, 128, NCOL), np.float32)
    for l in range(L):
        colp[l, :, C_FFN1:C_FFN1 + 8] = np.asarray(inp["ffn1_norm"])[l].reshape(8, 128).T
        colp[l, :, C_MIX:C_MIX + 8] = np.asarray(inp["mix_norm"])[l].reshape(8, 128).T
        colp[l, :, C_FFN2:C_FFN2 + 8] = np.asarray(inp["ffn2_norm"])[l].reshape(8, 128).T
        colp[l, :, C_DQN] = np.asarray(inp["diff_qk_norm"])[l, 0][p % 32]
        colp[l, :, C_DKN] = np.asarray(inp["diff_qk_norm"])[l, 1][p % 32]
        colp[l, :, C_DHN] = np.asarray(inp["diff_head_norm"])[l][p % 64]
        colp[l, :, C_MHN] = np.asarray(inp["ml_head_norm"])[l][p % 64]
        colp[l, :, C_SQN] = np.asarray(inp["dsa_qk_norm"])[l, 0][p % 64]
        colp[l, :, C_SKN] = np.asarray(inp["dsa_qk_norm"])[l, 1][p % 64]
        cw = np.asarray(inp["ml_conv_w"])[l]
        cb = np.asarray(inp["ml_conv_b"])[l]
        for k in range(4):
            for tap in range(4):
                colp[l, :, C_CW + k * 4 + tap] = cw[tap, k * 128:(k + 1) * 128]
            colp[l, :, C_CB + k] = cb[k * 128:(k + 1) * 128]
        gb = np.asarray(inp["ml_gate_bias"])[l]
        colp[l, 0:4, C_GB] = gb[0]
        colp[l, 0:4, C_GB + 1] = gb[1]
    out["colp"] = colp
    lam = np.asarray(inp["diff_lambda"]).reshape(L, 1, 128)
    out["lamb"] = f32(np.broadcast_to(lam, (L, 128, 128)))
    out["ccol"], out["cmatb"], out["cmatf"] = make_constants()
    return out


_CACHE = {}


def kernel(**inputs):
    x = np.asarray(inputs["x"])
    B, S, D = x.shape
    L = np.asarray(inputs["w_in"]).shape[0]
    shared = prep_shared(inputs)
    key = (S, L)
    if key not in _CACHE:
        _CACHE[key] = Builder(S, L).build()
    nc = _CACHE[key]
    pos = np.asarray(inputs["positions"]).astype(np.int32)
    in_maps = []
    for b in range(B):
        m = dict(shared)
        m["xT"] = np.ascontiguousarray(x[b].T)
        m["pos"] = np.ascontiguousarray(pos[b].reshape(1, S))
        in_maps.append(m)
    res = run_bass_kernel_spmd(nc, in_maps, core_ids=list(range(B)))
    out = np.stack([np.ascontiguousarray(r["outT"].T) for r in res.results], axis=0)
    return out.astype(np.float32)
```

```python
import math
from contextlib import ExitStack, contextmanager
import numpy as np
import concourse.bass as bass
import concourse.mybir as mybir
from concourse.bass_utils import run_bass_kernel_spmd

F32 = mybir.dt.float32
BF16 = mybir.dt.bfloat16
I32 = mybir.dt.int32
ALU = mybir.AluOpType
AF = mybir.ActivationFunctionType
AX = mybir.AxisListType

D_MODEL = 1024
D_FF = 2816
NFF = D_FF // 128
NDC = D_MODEL // 128
EPS = 1e-6
ROPE_THETA = 500000.0
N_FM = 19
N_TM = 840
TOPK = 256
NEG = -1.0e30

SAME_ENG_SYNC = True
ENGS = ("pe", "act", "dve", "pool", "sp")
_ENG_ATTR = {"pe": "tensor", "act": "scalar", "dve": "vector", "pool": "gpsimd", "sp": "sync"}


class Buf:
    __slots__ = ("name", "t", "w", "r", "dsem", "dcnt")

    def __init__(self, name, t):
        self.name = name
        self.t = t
        self.w = None
        self.r = {}
        self.dsem = None
        self.dcnt = 0

    def __getitem__(self, idx):
        return self.t[idx]


class Prog:
    def __init__(self, nc):
        self.nc = nc
        self.ops = {e: [] for e in ENGS}
        self.cnt = {e: 0 for e in ENGS}
        self.seen = {e: {} for e in ENGS}
        self.bufs = {}
        self.dsems = {}
        self.free_dsems = []
        self.sw_sems = set()
        self.bg_sems = set()
        self.nid = 0
        self.ninst = 0
        self.stack = None
        self.phase_bufs = []
        self.psum_ring = []
        self.psum_i = 0
        self.ring = list(range(8))

    def _reg(self, name, t):
        b = Buf(name, t)
        self.bufs[t.name] = b
        return b

    def sb(self, name, shape, dt):
        self.nid += 1
        t = self.stack.enter_context(self.nc.sbuf_tensor(f"{name}_{self.nid}", list(shape), dt))
        b = self._reg(name, t)
        self.phase_bufs.append(b)
        return b

    def sb_global(self, name, shape, dt):
        self.nid += 1
        t = self.nc.alloc_sbuf_tensor(f"{name}_{self.nid}", list(shape), dt)
        return self._reg(name, t)

    def init_psum(self):
        for i in range(8):
            t = self.nc.alloc_psum_tensor(f"psb{i}", [128, 512], F32)
            self.psum_ring.append(self._reg(f"psb{i}", t))

    def psum(self):
        r = self.ring
        b = self.psum_ring[r[self.psum_i % len(r)]]
        self.psum_i += 1
        return b

    def psum_pools(self, *sizes):
        out, k = [], 0
        for n in sizes:
            out.append([self.psum_ring[k + i] for i in range(n)])
            k += n
        self.ring = list(range(k, 8)) or [7]
        return out

    def psum_reserve(self, n):
        self.ring = list(range(n, 8))
        return [self.psum_ring[i] for i in range(n)]

    @contextmanager
    def phase(self, name):
        self.stack = ExitStack()
        self.phase_bufs = []
        self.ring = list(range(8))
        try:
            yield
        finally:
            self.barrier()
            for b in self.phase_bufs:
                if b.dsem is not None and b.dsem not in self.sw_sems:
                    self.free_dsems.append((b.dsem, b.dcnt))
                self.bufs.pop(b.t.name, None)
            self.stack.close()
            self.stack = None

    def _deps(self, eng, reads, writes):
        need = {}

        def add(k, v):
            if k == eng and (not SAME_ENG_SYNC or eng == "pe"):
                return
            if need.get(k, 0) < v:
                need[k] = v
        for b in reads:
            if b.w is not None:
                add(*b.w)
        for b in writes:
            if b.w is not None:
                add(*b.w)
            for k, v in b.r.items():
                add(k, v)
        out = []
        seen = self.seen[eng]
        for k, v in need.items():
            if seen.get(k, 0) >= v:
                continue
            seen[k] = v
            out.append((k, v))
        return out

    def _mark(self, ev, reads, writes):
        for b in reads:
            if b.r.get(ev[0], 0) < ev[1]:
                b.r[ev[0]] = ev[1]
        for b in writes:
            b.w = ev
            b.r = {}
        self.ninst += 1

    def _classify(self, kw):
        reads, writes = [], []
        for k, v in kw.items():
            if not hasattr(v, "tensor") or not hasattr(v, "space"):
                continue
            b = self.bufs.get(v.tensor.name)
            if b is None:
                continue
            if k in ("out", "accum_out", "ap"):
                if b not in writes:
                    writes.append(b)
            else:
                if b not in reads:
                    reads.append(b)
        return reads, writes

    def op(self, eng, name, _after=(), **kw):
        reads, writes = self._classify(kw)
        for b in _after:
            if b not in reads:
                reads.append(b)
        waits = self._deps(eng, reads, writes)
        self.cnt[eng] += 1
        ev = (eng, self.cnt[eng])
        self.ops[eng].append((waits, name, kw, (eng, 1)))
        self._mark(ev, reads, writes)

    def dma(self, q, out, in_, **kw):
        bo = self.bufs.get(out.tensor.name)
        bi = self.bufs.get(in_.tensor.name)
        owner = bo if bo is not None else bi
        assert owner is not None
        if owner.dsem is None:
            if self.free_dsems and q != "pool":
                owner.dsem, owner.dcnt = self.free_dsems.pop()
            else:
                owner.dsem = f"d{len(self.dsems)}"
            self.dsems[owner.dsem] = owner
        if q == "pool":
            self.sw_sems.add(owner.dsem)
        reads = [bi] if bi is not None else []
        writes = [bo] if bo is not None else []
        waits = self._deps(q, reads, writes)
        if owner.dcnt > 0:
            k, v = owner.dsem, owner.dcnt
            if self.seen[q].get(k, 0) < v:
                self.seen[q][k] = v
                waits.append((k, v))
        owner.dcnt += 16
        ev = (owner.dsem, owner.dcnt)
        d = dict(out=out, in_=in_)
        d.update(kw)
        self.ops[q].append((waits, "dma_start", d, (owner.dsem, 16)))
        self._mark(ev, reads, writes)

    def barrier(self):
        for e in ENGS:
            waits = []
            seen = self.seen[e]
            for e2 in ENGS:
                if e2 != e and self.cnt[e2] > seen.get(e2, 0):
                    seen[e2] = self.cnt[e2]
                    waits.append((e2, self.cnt[e2]))
            for k, b in self.dsems.items():
                if k in self.bg_sems:
                    continue
                if b.dsem == k and b.dcnt > seen.get(k, 0):
                    seen[k] = b.dcnt
                    waits.append((k, b.dcnt))
            self.ops[e].append((waits, None, None, None))

    def emit(self):
        nc = self.nc
        ops = self.ops
        keys = set(ENGS)
        for e in ENGS:
            for waits, name, kw, inc in ops[e]:
                for k, v in waits:
                    keys.add(k)
                if inc is not None:
                    keys.add(inc[0])
        sems = {k: nc.alloc_semaphore(f"s_{k}") for k in sorted(keys)}
        needed = {e: set() for e in ENGS}
        for e in ENGS:
            for waits, name, kw, inc in ops[e]:
                for k, v in waits:
                    if k in needed:
                        needed[k].add(v)
        remap = {}
        for e in ENGS:
            m = {}
            c = 0
            n = 0
            for waits, name, kw, inc in ops[e]:
                if inc is not None and inc[0] == e:
                    n += 1
                    if n in needed[e]:
                        c += 1
                        m[n] = c
            remap[e] = m

        def run(engobj, ename):
            n = 0
            for waits, name, kw, inc in ops[ename]:
                for k, v in waits:
                    if k in remap:
                        v = remap[k][v]
                    engobj.wait_ge(sems[k], v)
                if name is None:
                    continue
                ins = getattr(engobj, name)(**kw)
                if inc[0] == ename:
                    n += 1
                    if n in remap[ename]:
                        ins.then_inc(sems[ename], 1)
                else:
                    ins.then_inc(sems[inc[0]], inc[1])

        with nc.Block() as block:
            @block.tensor
            def _(e):
                run(e, "pe")

            @block.scalar
            def _(e):
                run(e, "act")

            @block.vector
            def _(e):
                run(e, "dve")

            @block.gpsimd
            def _(e):
                run(e, "pool")

            @block.sync
            def _(e):
                run(e, "sp")


class Stream:
    def __init__(self, P, ring, srcs, look=None, q="sp", outf=None, pre=None):
        self.P, self.ring, self.srcs, self.q = P, ring, srcs, q
        self.look = (len(ring) - 1) if look is None else look
        self.issued = 0
        self.outf = outf
        self.pre = pre

    def get(self, i):
        while self.issued < len(self.srcs) and self.issued <= i + self.look:
            k = self.issued
            b = self.ring[k % len(self.ring)]
            if self.pre is not None and self.pre[k] is not None:
                self.pre[k](b)
            self.P.dma(self.q, out=(b[:] if self.outf is None else self.outf[k](b)), in_=self.srcs[k])
            self.issued += 1
        return self.ring[i % len(self.ring)]


C_FFN1 = 0
C_MIX = 8
C_FFN2 = 16
C_DQN = 24
C_DKN = 25
C_DHN = 26
C_MHN = 27
C_SQN = 28
C_SKN = 29
C_CW = 30
C_CB = 46
C_GB = 50
NCOL = 52
CC_INV32, CC_SGN32, CC_INV64, CC_SGN64, CC_EPS, CC_ONE = 0, 1, 2, 3, 4, 5

CH_DQ, CH_DK, CH_MQ, CH_MK, CH_MO, CH_SQ, CH_SK, CH_AQ, CH_IQ, CH_MISC = 0, 2, 4, 6, 8, 10, 12, 14, 16, 18
TM_DV, TM_MV, TM_SV, TM_AV, TM_IW = 0, 256, 512, 768, 832


class Builder:
    def __init__(self, S, L, TT=1024, debug=False):
        self.S, self.L, self.TT, self.debug = S, L, TT, debug
        self.skip = ()
        self.wready = {}
        assert S % 512 == 0 and TT % 512 == 0 and S % TT == 0
        self.nc = nc = bass.Bass("TRN2", target_bir_lowering=False)
        self.P = P = Prog(nc)
        P.init_psum()
        ein = lambda n, shp, dt=F32: nc.dram_tensor(n, list(shp), dt, kind="ExternalInput")
        okind = "ExternalOutput" if debug else "Internal"
        scr = lambda n, shp, dt: nc.dram_tensor(n, list(shp), dt, kind=okind)
        self.xT = ein("xT", [D_MODEL, S])
        self.pos = ein("pos", [1, S], I32)
        self.w_gu = [ein(f"gu{i}", [L, NFF, 128, NDC * 256]) for i in (1, 2)]
        self.w_dn = [ein(f"dn{i}", [L, NDC, 128, NFF * 128]) for i in (1, 2)]
        self.w_fm = ein("wfm", [L, N_FM, 128, NDC * 128])
        self.w_tm = ein("wtm", [L, 128, NDC * N_TM])
        self.w_gt = ein("wgt", [L, 32, 128, NDC * 128])
        self.w_br = ein("wbr", [L, 32, 128, 2 * 128])
        self.w_out = ein("wout", [L, NDC, 128, NDC * 128])
        self.colp = ein("colp", [L, 128, NCOL])
        self.lamb = ein("lamb", [L, 128, 128])
        self.ccol = ein("ccol", [128, 8])
        self.cmatb = ein("cmatb", [128, 6, 128])
        self.cmatf = ein("cmatf", [128, 5, 128])
        self.outT = nc.dram_tensor("outT", [D_MODEL, S], F32, kind="ExternalOutput")
        self.b_gu = [scr(f"bgu{i}", [L, NFF, 128, NDC * 256], BF16) for i in (1, 2)]
        self.b_dn = [scr(f"bdn{i}", [L, NDC, 128, NFF * 128], BF16) for i in (1, 2)]
        self.b_fm = scr("bfm", [L, N_FM, 128, NDC * 128], BF16)
        self.b_tm = scr("btm", [L, 128, NDC * N_TM], BF16)
        self.b_gt = scr("bgt", [L, 32, 128, NDC * 128], BF16)
        self.b_br = scr("bbr", [L, 32, 128, 2 * 128], BF16)
        self.b_out = scr("bout", [L, NDC, 128, NDC * 128], BF16)
        self.xs = [scr(f"xs{i}", [D_MODEL, S], F32) for i in range(3)]
        self.zT = scr("zT", [N_FM * 128, S], F32)
        self.pT = scr("pT", [N_FM * 128, S], BF16)
        self.vtok = scr("vtok", [S, N_TM], F32)
        self.oT = scr("oT", [4, 256, S], BF16)
        self.rope = scr("rope", [4, 128, S], F32)
        self.grow = scr("grow", [8, S], F32)
        self.dbg = scr("dbg", [16, 128, 512], F32) if debug else None
        self.dbg_i = 0

    def consts(self):
        P = self.P
        self.k_ccol = P.sb_global("ccol", [128, 8], F32)
        self.k_matb = P.sb_global("cmatb", [128, 6, 128], BF16)
        self.k_matf = P.sb_global("cmatf", [128, 5, 128], F32)
        self.k_colp = P.sb_global("colp", [128, self.L, NCOL], F32)
        P.dma("sp", out=self.k_ccol[:], in_=self.ccol.ap())
        P.dma("pool", out=self.k_matb[:], in_=self.cmatb.ap())
        P.dma("sp", out=self.k_matf[:], in_=self.cmatf.ap())
        P.dma("sp", out=self.k_colp[:], in_=self.colp.ap().rearrange("l p c -> p l c"))
        self.ONES = self.k_matb[:, 0, :]
        self.BLK32 = self.k_matb[:, 1, :]
        self.BLK64 = self.k_matb[:, 2, :]
        self.UGE = self.k_matb[:, 3, :]
        self.MLE = self.k_matb[:, 4, :]
        self.MLT = self.k_matb[:, 5, :]
        self.PERM32 = self.k_matf[:, 0, :]
        self.PERM64 = self.k_matf[:, 1, :]
        self.ADDMASK = self.k_matf[:, 2, :]
        self.IDENTF = self.k_matf[:, 3, :]
        self.POW2 = self.k_matf[:, 4, :]

    def dump(self, ap, tag=""):
        if self.dbg is None or self.dbg_i >= 16:
            return
        P = self.P
        np_, nf = ap.shape[0], ap.shape[-1]
        t = P.sb("dbgt", [128, 512], F32)
        P.op("dve", "tensor_copy", out=t[0:np_, 0:nf], in_=ap)
        P.dma("sp", out=self.dbg.ap()[self.dbg_i, 0:np_, 0:nf], in_=t[0:np_, 0:nf])
        print("dbg", self.dbg_i, tag, np_, nf)
        self.dbg_i += 1

    @staticmethod
    def pipe(iters, stages, skews, drip=None, rate=2):
        n, D = len(iters), max(skews)
        for s_ in range(n + D):
            for f, k in zip(stages, skews):
                i = s_ - k
                if 0 <= i < n:
                    f(iters[i])
            if drip:
                for _ in range(rate):
                    if drip:
                        drip.pop(0)()
        while drip:
            drip.pop(0)()

    def col(self, l, c, p0=0, p1=128):
        return self.k_colp[p0:p1, l, c:c + 1]

    def cc(self, c, p0=0, p1=128):
        return self.k_ccol[p0:p1, c:c + 1]

    def cast_weights(self):
        P = self.P
        self.wready = {}
        jobs = []
        for l in range(self.L):
            for i in (0, 1):
                jobs.append((("gu", i, l), self.b_gu[i].ap()[l], self.w_gu[i].ap()[l], 2))
                jobs.append((("dn", i, l), self.b_dn[i].ap()[l].rearrange("s p (a n) -> s p a n", a=2),
                             self.w_dn[i].ap()[l].rearrange("s p (a n) -> s p a n", a=2), 2))
                if i == 0:
                    jobs.append((("fm", l), self.b_fm.ap()[l], self.w_fm.ap()[l], 1))
                    jobs.append((("tm", l), self.b_tm.ap()[l].rearrange("p (a n) -> p a n", a=8),
                                 self.w_tm.ap()[l].rearrange("p (a n) -> p a n", a=8), 1))
                    jobs.append((("gt", l), self.b_gt.ap()[l], self.w_gt.ap()[l], 2))
                    jobs.append((("br", l), self.b_br.ap()[l], self.w_br.ap()[l], 1))
                    jobs.append((("out", l), self.b_out.ap()[l], self.w_out.ap()[l], 1))
        k = 0

        def cp(owners, key, dst, src, nsplit):
            nonlocal k
            n0 = src.shape[0]
            step = (n0 + nsplit - 1) // nsplit
            evs = []
            for i in range(0, n0, step):
                ow = owners[k % len(owners)]
                k += 1
                self._dram_dma("pool", dst[i:i + step], src[i:i + step], ow)
                evs.append((ow.dsem, ow.dcnt))
            self.wready[key] = evs
        with P.phase("cast"):
            dummy = [P.sb(f"cd{i}", [1, 8], F32) for i in range(4)]
            for key, dst, src, ns in jobs[:2]:
                cp(dummy, key, dst, src, ns)
        self.wready[jobs[0][0]] = []
        self.wready[jobs[1][0]] = []
        bg = [P.sb_global(f"cg{i}", [1, 8], F32) for i in range(4)]
        for key, dst, src, ns in jobs[2:]:
            cp(bg, key, dst, src, ns)
        for b in bg:
            P.bg_sems.add(b.dsem)

    def need_weights(self, *keys):
        P = self.P
        waits = []
        for key in keys:
            for (sk, v) in self.wready.get(key, []):
                if P.seen["sp"].get(sk, 0) < v:
                    P.seen["sp"][sk] = v
                    waits.append((sk, v))
        if waits:
            P.ops["sp"].append((waits, None, None, None))

    def _dram_dma(self, q, out, in_, owner):
        P = self.P
        if owner.dsem is None:
            owner.dsem = f"d{len(P.dsems)}"
            P.dsems[owner.dsem] = owner
            P.sw_sems.add(owner.dsem)
        waits = []
        if owner.dcnt > 0:
            waits.append((owner.dsem, owner.dcnt))
        owner.dcnt += 16
        P.ops[q].append((waits, "dma_start", dict(out=out, in_=in_), (owner.dsem, 16)))
        P.ninst += 1

    def norm_parts(self, l, gcol, xt, hT, sq, rstd, TT):
        P = self.P

        ops = []
        for c in range(NDC):
            ops.append(lambda c=c: P.op("act", "activation", out=sq[:, c, :], in_=xt[:, c, :], func=AF.Square))

        def p1(hf):
            cs = slice(hf * 512, (hf + 1) * 512)
            ps = P.psum()
            for c in range(NDC):
                P.op("pe", "matmul", out=ps[:], lhsT=self.ONES, rhs=sq[:, c, cs], start=(c == 0), stop=(c == NDC - 1))
            P.op("act", "activation", out=rstd[:, cs], in_=ps[:], func=AF.Ln, scale=1.0 / D_MODEL, bias=self.cc(CC_EPS))
            P.op("act", "activation", out=rstd[:, cs], in_=rstd[:, cs], func=AF.Exp, scale=-0.5)
        for hf in range(TT // 512):
            ops.append(lambda hf=hf: p1(hf))
        for hf in range(TT // 512):
            cs = slice(hf * 512, (hf + 1) * 512)
            for c in range(NDC):
                ops.append(lambda c=c, cs=cs: P.op("dve", "scalar_tensor_tensor", out=hT[:, c, cs], in0=xt[:, c, cs], scalar=self.col(l, gcol + c),
                                                   in1=rstd[:, cs], op0=ALU.mult, op1=ALU.mult))
        return ops


    def ffn(self, l, which, xsrc, xdst):
        P, S, TT = self.P, self.S, self.TT
        NH = TT // 512
        gcol = C_FFN1 if which == 0 else C_FFN2
        bgu, bdn = self.b_gu[which].ap()[l], self.b_dn[which].ap()[l]
        with P.phase("ffn"):
            self.need_weights(("gu", which, l), ("dn", which, l))
            xts = [P.sb("xt", [128, NDC, TT], F32) for _ in range(2)]
            hTs = [P.sb("hT", [128, NDC, TT], BF16) for _ in range(2)]
            sq = P.sb("sq", [128, NDC, TT], BF16)
            rstd = P.sb("rstd", [128, TT], F32)
            aT = P.sb("aT", [128, NFF, TT], BF16)
            wg = [P.sb("wg", [128, NDC, 256], BF16) for _ in range(3)]
            wd = [P.sb("wd", [128, NFF, 128], BF16) for _ in range(2)]
            sg = [P.sb("sg", [128, 512], F32) for _ in range(2)]
            xv = lambda t, ti: t.ap().rearrange("(c p) s -> p c s", p=128)[:, :, ti * TT:(ti + 1) * TT]
            nt = S // TT
            sx = Stream(P, xts, [xv(xsrc, ti) for ti in range(nt)], look=0)
            sgu = Stream(P, wg, [bgu[j].rearrange("p (c n) -> p c n", c=NDC) for ti in range(nt) for j in range(NFF)])
            sdn = Stream(P, wd, [bdn[d].rearrange("p (c n) -> p c n", c=NFF) for ti in range(nt) for d in range(NDC)])
            for p_ in self.norm_parts(l, gcol, sx.get(0), hTs[0], sq, rstd, TT):
                p_()
            for ti in range(nt):
                xt = sx.get(ti)
                hT = hTs[ti % 2]
                sgu.get(ti * NFF)
                nxt = self.norm_parts(l, gcol, sx.get(ti + 1), hTs[(ti + 1) % 2], sq, rstd, TT) if ti + 1 < nt else None
                for j in range(NFF):
                    w = sgu.get(ti * NFF + j)
                    if nxt and j >= 2:
                        for _ in range(2):
                            if nxt:
                                nxt.pop(0)()
                    if j == NFF - 2:
                        sdn.get(ti * NDC)
                    for hf in range(NH):
                        cs = slice(hf * 512, (hf + 1) * 512)
                        pg, pu = P.psum(), P.psum()
                        for c in range(NDC):
                            P.op("pe", "matmul", out=pg[:], lhsT=w[:, c, 0:128], rhs=hT[:, c, cs], start=(c == 0), stop=(c == NDC - 1))
                        for c in range(NDC):
                            P.op("pe", "matmul", out=pu[:], lhsT=w[:, c, 128:256], rhs=hT[:, c, cs], start=(c == 0), stop=(c == NDC - 1))
                        s_ = sg[(j * NH + hf) % 2]
                        P.op("act", "activation", out=s_[:], in_=pg[:], func=AF.Silu)
                        P.op("dve", "tensor_tensor", out=aT[:, j, cs], in0=s_[:], in1=pu[:], op=ALU.mult)
                while nxt:
                    nxt.pop(0)()
                for d in range(NDC):
                    w = sdn.get(ti * NDC + d)
                    for hf in range(NH):
                        cs = slice(hf * 512, (hf + 1) * 512)
                        po = P.psum()
                        for c in range(NFF):
                            P.op("pe", "matmul", out=po[:], lhsT=w[:, c, :], rhs=aT[:, c, cs], start=(c == 0), stop=(c == NFF - 1))
                        P.op("dve", "scalar_tensor_tensor", out=xt[:, d, cs], in0=po[:], scalar=0.5, in1=xt[:, d, cs],
                             op0=ALU.mult, op1=ALU.add)
                P.dma("sp", out=xv(xdst, ti), in_=xt[:])

    def inproj(self, l, xsrc):
        P, S, TT = self.P, self.S, self.TT
        NH = TT // 512
        bfm, btm = self.b_fm.ap()[l], self.b_tm.ap()[l]
        with P.phase("inproj"):
            self.need_weights(("fm", l), ("tm", l))
            xts = [P.sb("xt", [128, NDC, TT], F32) for _ in range(2)]
            hTs = [P.sb("hT", [128, NDC, TT], BF16) for _ in range(2)]
            sq = P.sb("sq", [128, NDC, TT], BF16)
            rstd = P.sb("rstd", [128, TT], F32)
            wf = [P.sb("wf", [128, NDC, 128], BF16) for _ in range(3)]
            wt = P.sb("wt", [128, NDC, N_TM], BF16)
            zs = [P.sb("zs", [128, 512], F32) for _ in range(4)]
            vs = [P.sb("vs", [128, N_TM], F32) for _ in range(2)]
            xv = lambda t, ti: t.ap().rearrange("(c p) s -> p c s", p=128)[:, :, ti * TT:(ti + 1) * TT]
            nt = S // TT
            P.dma("sp", out=wt[:], in_=btm.rearrange("p (c n) -> p c n", c=NDC))
            sx = Stream(P, xts, [xv(xsrc, ti) for ti in range(nt)], look=0)
            sfm = Stream(P, wf, [bfm[j].rearrange("p (c n) -> p c n", c=NDC) for ti in range(nt) for j in range(N_FM)])
            zi = 0
            for p_ in self.norm_parts(l, C_MIX, sx.get(0), hTs[0], sq, rstd, TT):
                p_()
            for ti in range(nt):
                xt = sx.get(ti)
                hT = hTs[ti % 2]
                sfm.get(ti * N_FM)
                nxt = self.norm_parts(l, C_MIX, sx.get(ti + 1), hTs[(ti + 1) % 2], sq, rstd, TT) if ti + 1 < nt else None
                for j in range(N_FM):
                    w = sfm.get(ti * N_FM + j)
                    if nxt and j >= 1:
                        for _ in range(2):
                            if nxt:
                                nxt.pop(0)()
                    for hf in range(NH):
                        cs = slice(hf * 512, (hf + 1) * 512)
                        ps = P.psum()
                        for c in range(NDC):
                            P.op("pe", "matmul", out=ps[:], lhsT=w[:, c, :], rhs=hT[:, c, cs], start=(c == 0), stop=(c == NDC - 1))
                        z = zs[zi % 4]
                        if zi % 2 == 0:
                            P.op("act", "activation", out=z[:], in_=ps[:], func=AF.Copy)
                        else:
                            P.op("dve", "tensor_copy", out=z[:], in_=ps[:])
                        zi += 1
                        P.dma("sp", out=self.zT.ap()[j * 128:(j + 1) * 128, ti * TT + hf * 512: ti * TT + (hf + 1) * 512], in_=z[:])
                while nxt:
                    nxt.pop(0)()
                for tb in range(TT // 128):
                    ts_ = slice(tb * 128, (tb + 1) * 128)
                    v = vs[tb % 2]
                    for (c0, c1) in ((0, 512), (512, N_TM)):
                        ps = P.psum()
                        for c in range(NDC):
                            P.op("pe", "matmul", out=ps[:, 0:c1 - c0], lhsT=hT[:, c, ts_], rhs=wt[:, c, c0:c1], start=(c == 0), stop=(c == NDC - 1))
                        if c0 == 0:
                            P.op("act", "activation", out=v[:, c0:c1], in_=ps[:, 0:c1 - c0], func=AF.Copy)
                        else:
                            P.op("dve", "tensor_copy", out=v[:, c0:c1], in_=ps[:, 0:c1 - c0])
                    r0 = ti * TT + tb * 128
                    P.dma("sp", out=self.vtok.ap()[r0:r0 + 128, :], in_=v[:])

    def outproj(self, l, xsrc, xdst):
        P, S, TT = self.P, self.S, self.TT
        NH = TT // 512
        bgt, bbr, bout = self.b_gt.ap()[l], self.b_br.ap()[l], self.b_out.ap()[l]
        with P.phase("outproj"):
            self.need_weights(("gt", l), ("br", l), ("out", l))
            xts = [P.sb("xt", [128, NDC, TT], F32) for _ in range(2)]
            hTs = [P.sb("hT", [128, NDC, TT], BF16) for _ in range(2)]
            sq = P.sb("sq", [128, NDC, TT], BF16)
            rstd = P.sb("rstd", [128, TT], F32)
            ob = P.sb("ob", [128, 8, TT], BF16)
            yT = P.sb("yT", [128, NDC, TT], BF16)
            wgs = [P.sb("wgs", [128, NDC, 128], BF16) for _ in range(8)]
            wbs = [P.sb("wbs", [128, 2, 128], BF16) for _ in range(8)]
            wo = [P.sb("wo", [128, NDC, 128], BF16) for _ in range(2)]
            sg = [P.sb("sg", [128, 512], F32) for _ in range(2)]
            tb_ = [P.sb("tb", [128, 512], F32) for _ in range(4)]
            xv = lambda t, ti: t.ap().rearrange("(c p) s -> p c s", p=128)[:, :, ti * TT:(ti + 1) * TT]
            nt = S // TT
            sx = Stream(P, xts, [xv(xsrc, ti) for ti in range(nt)], look=0)
            order = [(b, d) for ti in range(nt) for d in range(NDC) for b in range(4)]
            sg1 = Stream(P, wgs, [bgt[b * 8 + d].rearrange("p (c n) -> p c n", c=NDC) for (b, d) in order], look=4)
            sg2 = Stream(P, wbs, [bbr[b * 8 + d].rearrange("p (c n) -> p c n", c=2) for (b, d) in order], look=4)
            swo = Stream(P, wo, [bout[d].rearrange("p (c n) -> p c n", c=NDC) for ti in range(nt) for d in range(NDC)])
            for p_ in self.norm_parts(l, C_MIX, sx.get(0), hTs[0], sq, rstd, TT):
                p_()
            for ti in range(nt):
                xt = sx.get(ti)
                hT = hTs[ti % 2]
                P.dma("sp", out=ob[:], in_=self.oT.ap().rearrange("b (c p) s -> p (b c) s", p=128)[:, :, ti * TT:(ti + 1) * TT])
                nxt = self.norm_parts(l, C_MIX, sx.get(ti + 1), hTs[(ti + 1) % 2], sq, rstd, TT) if ti + 1 < nt else None
                for d in range(NDC):
                    if nxt:
                        for _ in range(4):
                            if nxt:
                                nxt.pop(0)()
                    ws = []
                    for b in range(4):
                        i_ = (ti * NDC + d) * 4 + b
                        ws.append((sg1.get(i_), sg2.get(i_)))
                    for hf in range(NH):
                        cs = slice(hf * 512, (hf + 1) * 512)
                        for b in range(4):
                            w1, w2 = ws[b]
                            pg, pu = P.psum(), P.psum()
                            for c in range(NDC):
                                P.op("pe", "matmul", out=pg[:], lhsT=w1[:, c, :], rhs=hT[:, c, cs], start=(c == 0), stop=(c == NDC - 1))
                            for c in range(2):
                                P.op("pe", "matmul", out=pu[:], lhsT=w2[:, c, :], rhs=ob[:, b * 2 + c, cs], start=(c == 0), stop=(c == 1))
                            s_ = sg[b % 2]
                            P.op("act", "activation", out=s_[:], in_=pg[:], func=AF.Sigmoid)
                            P.op("dve", "tensor_tensor", out=tb_[b][:], in0=s_[:], in1=pu[:], op=ALU.mult)
                        P.op("pool", "tensor_tensor", out=tb_[0][:], in0=tb_[0][:], in1=tb_[1][:], op=ALU.add)
                        P.op("pool", "tensor_tensor", out=tb_[2][:], in0=tb_[2][:], in1=tb_[3][:], op=ALU.add)
                        P.op("pool", "tensor_tensor", out=yT[:, d, cs], in0=tb_[0][:], in1=tb_[2][:], op=ALU.add)
                while nxt:
                    nxt.pop(0)()
                for d in range(NDC):
                    w = swo.get(ti * NDC + d)
                    for hf in range(NH):
                        cs = slice(hf * 512, (hf + 1) * 512)
                        po = P.psum()
                        for c in range(NDC):
                            P.op("pe", "matmul", out=po[:], lhsT=w[:, c, :], rhs=yT[:, c, cs], start=(c == 0), stop=(c == NDC - 1))
                        P.op("dve", "tensor_tensor", out=xt[:, d, cs], in0=po[:], in1=xt[:, d, cs], op=ALU.add)
                P.dma("sp", out=xv(xdst, ti), in_=xt[:])

    def build(self, phases=("diff", "ml", "sb", "dsa")):
        P = self.P
        self.phases = phases
        skip = self.skip
        self.consts()
        if "cast" not in skip:
            self.cast_weights()
        if "rope" not in skip:
            self.rope_tables()
        x_in = self.xT
        for l in range(self.L):
            x1, x2 = self.xs[0], self.xs[1]
            last = (l == self.L - 1)
            if "ffn" not in skip:
                self.ffn(l, 0, x_in, x1)
            if "inproj" not in skip:
                self.inproj(l, x1)
            self.mixers(l)
            if "outproj" not in skip:
                self.outproj(l, x1, x2)
            x3 = self.outT if last else self.xs[2]
            if "ffn" not in skip:
                self.ffn(l, 1, x2, x3)
            x_in = x3
        P.barrier()
        P.emit()
        return self.nc

    def rope_tables(self):
        P, S = self.P, self.S
        TWO_PI = 2.0 * math.pi
        C1 = 6.28125
        C2 = TWO_PI - C1
        MAGIC = 12582912.0
        with P.phase("rope"):
            posi = P.sb("posi", [128, S], I32)
            posf = P.sb("posf", [128, S], F32)
            ang = P.sb("ang", [128, S], F32)
            a2 = P.sb("a2", [128, S], F32)
            tt = P.sb("tt", [128, S], F32)
            rr = [P.sb("rr", [128, S], F32) for _ in range(2)]
            P.dma("sp", out=posi[:], in_=self.pos.ap().broadcast_to([128, S]))
            P.op("dve", "tensor_copy", out=posf[:], in_=posi[:])
            k = 0
            for (ci, sg) in ((CC_INV32, CC_SGN32), (CC_INV64, CC_SGN64)):
                P.op("dve", "tensor_scalar", out=ang[:], in0=posf[:], scalar1=self.cc(ci), scalar2=None, op0=ALU.mult)
                for kind in (0, 1):
                    r = rr[k % 2]
                    P.op("dve", "tensor_scalar", out=a2[:], in0=ang[:], scalar1=(math.pi / 2 if kind == 0 else 0.0), scalar2=None, op0=ALU.add)
                    P.op("dve", "tensor_scalar", out=tt[:], in0=a2[:], scalar1=1.0 / TWO_PI, scalar2=MAGIC, op0=ALU.mult, op1=ALU.add)
                    P.op("dve", "tensor_scalar", out=tt[:], in0=tt[:], scalar1=-MAGIC, scalar2=None, op0=ALU.add)
                    P.op("dve", "scalar_tensor_tensor", out=a2[:], in0=tt[:], scalar=-C1, in1=a2[:], op0=ALU.mult, op1=ALU.add)
                    P.op("dve", "scalar_tensor_tensor", out=a2[:], in0=tt[:], scalar=-C2, in1=a2[:], op0=ALU.mult, op1=ALU.add)
                    P.op("dve", "tensor_scalar", out=a2[:], in0=a2[:], scalar1=3.1415925, scalar2=-3.1415925, op0=ALU.min, op1=ALU.max)
                    P.op("act", "activation", out=r[:], in_=a2[:], func=AF.Sin)
                    if kind == 1:
                        P.op("dve", "tensor_scalar", out=r[:], in0=r[:], scalar1=self.cc(sg), scalar2=None, op0=ALU.mult)
                    P.dma("sp", out=self.rope.ap()[k], in_=r[:])
                    k += 1

    def prep(self, l):
        P, S = self.P, self.S
        CW = min(S, 1024)
        with P.phase("prep"):
            tabs = [P.sb("tab", [128, CW], F32) for _ in range(4)]
            zin = [P.sb("zin", [128, CW + 3], F32) for _ in range(4)]
            sqs = [P.sb("sq", [128, CW], BF16) for _ in range(2)]
            rstds = [P.sb("rstd", [128, 512], F32) for _ in range(4)]
            qns = [P.sb("qn", [128, CW], F32) for _ in range(2)]
            t1s = [P.sb("t1", [128, 512], F32) for _ in range(4)]
            t2s = [P.sb("t2", [128, 512], F32) for _ in range(4)]
            outs = [P.sb("po", [128, CW], BF16) for _ in range(3)]
            zi = 0
            nr = [0, 0]

            def normrope(z, o, G, gcol, p0, p1):
                blk, perm = (self.BLK32, self.PERM32) if G == 32 else (self.BLK64, self.PERM64)
                ct, st = (tabs[0], tabs[1]) if G == 32 else (tabs[2], tabs[3])
                sq, qn = sqs[nr[0] % 2], qns[nr[0] % 2]
                nr[0] += 1
                if gcol is not None:
                    P.op("act", "activation", out=sq[:], in_=z, func=AF.Square)
                hs = []
                for hf in range(CW // 512):
                    k_ = nr[1] % 4
                    nr[1] += 1
                    hs.append(dict(cs=slice(hf * 512, (hf + 1) * 512), rstd=rstds[k_], t1=t1s[k_], t2=t2s[k_]))
                if gcol is not None:
                    for h_ in hs:
                        h_["ps"] = P.psum()
                        P.op("pe", "matmul", out=h_["ps"][:], lhsT=blk, rhs=sq[:, h_["cs"]], start=True, stop=True)
                    for h_ in hs:
                        P.op("act", "activation", out=h_["rstd"][:], in_=h_["ps"][:], func=AF.Ln, scale=1.0 / G, bias=self.cc(CC_EPS))
                    for h_ in hs:
                        P.op("act", "activation", out=h_["rstd"][:], in_=h_["rstd"][:], func=AF.Exp, scale=-0.5)
                    for h_ in hs:
                        P.op("dve", "scalar_tensor_tensor", out=qn[:, h_["cs"]], in0=z[:, h_["cs"]], scalar=self.col(l, gcol), in1=h_["rstd"][:],
                             op0=ALU.mult, op1=ALU.mult)
                else:
                    for h_ in hs:
                        P.op("pool", "tensor_copy", out=qn[:, h_["cs"]], in_=z[:, h_["cs"]])
                for h_ in hs:
                    h_["ps2"] = P.psum()
                    P.op("pe", "matmul", out=h_["ps2"][:], lhsT=perm, rhs=qn[:, h_["cs"]], start=True, stop=True)
                for h_ in hs:
                    P.op("pool", "tensor_tensor", out=h_["t2"][:], in0=qn[:, h_["cs"]], in1=ct[:, h_["cs"]], op=ALU.mult)
                for h_ in hs:
                    P.op("dve", "tensor_tensor", out=h_["t1"][:], in0=h_["ps2"][:], in1=st[:, h_["cs"]], op=ALU.mult)
                for h_ in hs:
                    P.op("dve", "tensor_tensor", out=o[p0:p1, h_["cs"]], in0=h_["t1"][p0:p1, :], in1=h_["t2"][p0:p1, :], op=ALU.add)

            srcs, outf, pre = [], [], []
            for ct_ in range(S // CW):
                c0 = ct_ * CW
                for j in range(N_FM):
                    rows = slice(j * 128, (j + 1) * 128)
                    conv = CH_MQ <= j < CH_MO
                    if conv and c0 > 0:
                        srcs.append(self.zT.ap()[rows, c0 - 3:c0 + CW]); outf.append(lambda b: b[:]); pre.append(None)
                    else:
                        srcs.append(self.zT.ap()[rows, c0:c0 + CW]); outf.append(lambda b: b[:, 3:])
                        pre.append((lambda b: P.op("pool", "memset", ap=b[:, 0:3], constant=0.0)) if conv else None)
            zs_ = Stream(P, zin, srcs, look=2, outf=outf, pre=pre)
            for ct_ in range(S // CW):
                c0 = ct_ * CW
                for k in range(4):
                    P.dma("sp", out=tabs[k][:], in_=self.rope.ap()[k][:, c0:c0 + CW])
                for j in range(N_FM):
                    zt = zs_.get(ct_ * N_FM + j)
                    o = outs[zi % 3]
                    zi += 1
                    rows = slice(j * 128, (j + 1) * 128)
                    conv = CH_MQ <= j < CH_MO
                    z = zt[:, 3:]
                    if j in (CH_DQ, CH_DQ + 1):
                        normrope(z, o, 32, C_DQN, 0, 128)
                    elif j in (CH_DK, CH_DK + 1):
                        normrope(z, o, 32, C_DKN, 0, 128)
                    elif conv:
                        kk = j - CH_MQ
                        qn = qns[nr[0] % 2]
                        nr[0] += 1
                        P.op("dve", "tensor_scalar", out=qn[:], in0=zt[:, 3:CW + 3], scalar1=self.col(l, C_CW + kk * 4 + 3),
                             scalar2=self.col(l, C_CB + kk), op0=ALU.mult, op1=ALU.add)
                        for tap in (2, 1, 0):
                            P.op("dve", "scalar_tensor_tensor", out=qn[:], in0=zt[:, tap:CW + tap], scalar=self.col(l, C_CW + kk * 4 + tap),
                                 in1=qn[:], op0=ALU.mult, op1=ALU.add)
                        P.op("act", "activation", out=o[:], in_=qn[:], func=AF.Silu)
                    elif j in (CH_MO, CH_MO + 1):
                        P.op("act", "activation", out=o[:], in_=z, func=AF.Sigmoid)
                    elif CH_SQ <= j < CH_AQ:
                        P.op("pool", "tensor_copy", out=o[:], in_=z)
                    elif j in (CH_AQ, CH_AQ + 1):
                        normrope(z, o, 64, C_SQN, 0, 128)
                    elif j in (CH_IQ, CH_IQ + 1):
                        normrope(z, o, 32, None, 0, 128)
                    else:
                        normrope(z, o, 64, C_SKN, 0, 64)
                        normrope(z, o, 32, None, 64, 96)
                        P.op("pool", "memset", ap=o[96:128, :], constant=0.0)
                    P.dma("sp", out=self.pT.ap()[rows, c0:c0 + CW], in_=o[:])

    def load_vaug(self, vaug, vst, col0, ones=True):
        P, S = self.P, self.S
        NB = S // 128
        P.dma("sp", out=vst[:], in_=self.vtok.ap().rearrange("(nb p) n -> p nb n", p=128)[:, :, col0:col0 + 64])
        P.op("act", "activation", out=vaug[:, :, 0:64], in_=vst[:], func=AF.Copy)
        if ones:
            P.op("pool", "memset", ap=vaug[:, :, 64:128], constant=1.0)

    def load_masked_q(self, qms, chunk, G):
        P = self.P
        for g, qm in enumerate(qms):
            P.op("pool", "memset", ap=qm[:], constant=0.0)
            P.dma("sp", out=qm[g * G:(g + 1) * G, :], in_=self.pT.ap()[chunk * 128 + g * G:chunk * 128 + (g + 1) * G, :])

    def head_norm_store(self, l, o32, gcol, scale, gate, dst, scr):
        P = self.P
        sqh, rs, ob = scr
        ops = []
        ops.append(lambda: P.op("act", "activation", out=sqh[0:64, :], in_=o32[0:64, :], func=AF.Square))

        def mm():
            ps = P.psum()
            P.op("pe", "matmul", out=ps[0:64, :], lhsT=self.k_matb[0:64, 2, 0:64], rhs=sqh[0:64, :], start=True, stop=True)
            P.op("act", "activation", out=rs[0:64, :], in_=ps[0:64, :], func=AF.Ln, scale=1.0 / 64, bias=self.cc(CC_EPS, 0, 64))
        ops.append(mm)
        ops.append(lambda: P.op("act", "activation", out=rs[0:64, :], in_=rs[0:64, :], func=AF.Exp, scale=-0.5))
        ops.append(lambda: P.op("dve", "scalar_tensor_tensor", out=o32[0:64, :], in0=o32[0:64, :], scalar=self.col(l, gcol, 0, 64),
                                in1=rs[0:64, :], op0=ALU.mult, op1=ALU.mult))
        if gate is not None:
            ops.append(lambda: P.op("dve", "tensor_tensor", out=ob[0:64, :], in0=o32[0:64, :], in1=gate, op=ALU.mult))
        else:
            ops.append(lambda: P.op("act", "activation", out=ob[0:64, :], in_=o32[0:64, :], func=AF.Copy, scale=float(scale)))
        ops.append(lambda: P.dma("sp", out=dst, in_=ob[0:64, :]))
        return ops

    def sb_attn(self, l):
        P, S = self.P, self.S
        NB = S // 128
        with P.phase("sb"):
            (acc,), zring, wring, tring = P.psum_pools(1, 3, 2, 2)
            qms2 = [[P.sb("qm", [128, S], BF16) for _ in range(2)] for _ in range(2)]
            kT2_ = [P.sb("kT", [128, S], BF16) for _ in range(2)]
            vst2 = [P.sb("vst", [128, NB, 64], F32) for _ in range(2)]
            V2 = [P.sb("V", [128, NB, 128], BF16) for _ in range(2)]

            def load_head(h):
                if h % 2 == 0:
                    self.load_masked_q(qms2[(h // 2) % 2], CH_SQ + h // 2, 64)
                    P.dma("sp", out=kT2_[(h // 2) % 2][:], in_=self.pT.ap()[(CH_SK + h // 2) * 128:(CH_SK + h // 2 + 1) * 128, :])
                self.load_vaug(V2[h % 2], vst2[h % 2], TM_SV + h * 64)
            R = P.sb("R", [128, 512], F32)
            E = [P.sb("E", [128, 512], F32) for _ in range(3)]
            spb = [P.sb("spb", [128, 512], BF16) for _ in range(3)]
            t1 = [P.sb("t1", [128, 512], F32) for _ in range(3)]
            Pm = [P.sb("Pm", [128, 512], BF16) for _ in range(3)]
            ob = [P.sb("ob", [64, 512], BF16) for _ in range(2)]
            zer = P.sb("zer", [128, 512], BF16)
            P.op("pool", "memset", ap=zer[:], constant=0.0)
            g = 0

            def stA(it):
                P.op("pe", "matmul", out=it["zp"][:, it["cs"]], lhsT=it["kT"][:, it["kb"] * 128:(it["kb"] + 1) * 128],
                     rhs=it["qm"][:, it["q0"] + it["c0"]:it["q0"] + 512], start=True, stop=True)

            def stB(it):
                cs, c0 = it["cs"], it["c0"]
                P.op("act", "activation", out=it["e"][:, cs], in_=it["zp"][:, cs], func=AF.Exp, scale=0.125)
                P.op("act", "activation", out=it["s"][:, cs], in_=it["e"][:, cs], func=AF.Ln, bias=1.0)
                if it["j"] >= 0:
                    P.op("pool", "tensor_tensor", out=it["s"][:, c0:c0 + 128], in0=it["s"][:, c0:c0 + 128], in1=self.MLT, op=ALU.mult)

            def stC(it):
                cs = it["cs"]
                if it["first"]:
                    P.op("pool", "memset", ap=R[:], constant=0.0)
                P.op("pe", "matmul", out=it["wp"][:, cs], lhsT=self.UGE, rhs=it["s"][:, cs], start=True, stop=True)
                P.op("pe", "matmul", out=it["tp"][:, cs], lhsT=self.ONES, rhs=it["s"][:, cs], start=True, stop=True)
                pv = it["prev"]
                if pv is not None:
                    P.op("dve", "tensor_tensor", out=R[:, pv["cs"]], in0=R[:, pv["cs"]], in1=pv["tp"][:, pv["cs"]], op=ALU.subtract)
                P.op("dve", "scalar_tensor_tensor", _after=[it["e"]], out=it["t"][:, cs], in0=it["zp"][:, cs], scalar=0.125, in1=R[:, cs],
                     op0=ALU.mult, op1=ALU.add)

            def stD(it):
                cs, c0 = it["cs"], it["c0"]
                P.op("dve", "tensor_tensor", out=it["t"][:, cs], in0=it["t"][:, cs], in1=it["wp"][:, cs], op=ALU.subtract)
                P.op("act", "activation", out=it["p"][:, cs], in_=it["t"][:, cs], func=AF.Exp)
                if it["j"] >= 0:
                    P.op("pool", "tensor_tensor", out=it["p"][:, c0:c0 + 128], in0=it["p"][:, c0:c0 + 128], in1=self.MLT, op=ALU.mult)

            def stE(it):
                if it["first"]:
                    P.op("pe", "matmul", out=acc[:, :], lhsT=it["V"][:, 0, :], rhs=zer[:], start=True, stop=False, skip_group_check=True)
                P.op("pe", "matmul", out=acc[:, it["cs"]], lhsT=it["V"][:, it["kb"], :], rhs=it["p"][:, it["cs"]], start=False, stop=it["last"], skip_group_check=True)
                if it["last"]:
                    o_ = ob[it["oi"] % 2]
                    P.op("act", "activation", out=o_[:], in_=acc[0:64, :], func=AF.Copy)
                    P.dma("sp", out=self.oT.ap()[2, it["h"] * 64:(it["h"] + 1) * 64, it["q0"]:it["q0"] + 512], in_=o_[:])

            for h in range(4):
                r0 = (h % 2) * 64
                if h == 0:
                    load_head(0)
                if h + 1 < 4:
                    load_head(h + 1)
                qm, kT, V = qms2[(h // 2) % 2][h % 2], kT2_[(h // 2) % 2], V2[h % 2]
                iters = []
                for qt in range(S // 512):
                    q0 = qt * 512
                    nkb = 4 * qt + 4
                    prev = None
                    for kb in range(nkb - 1, -1, -1):
                        j = kb - 4 * qt
                        c0 = 128 * max(j, 0)
                        it = dict(kb=kb, j=j, c0=c0, cs=slice(c0, 512), q0=q0, qm=qm, zp=zring[g % 3], wp=wring[g % 2], tp=tring[g % 2],
                                  e=E[g % 3], s=spb[g % 3], t=t1[g % 3], p=Pm[g % 3], first=(kb == nkb - 1), last=(kb == 0), prev=prev, kT=kT, V=V,
                                  h=h, oi=h * 8 + qt)
                        g += 1
                        iters.append(it)
                        prev = it
                self.pipe(iters, (stA, stB, stC, stD, stE), (0, 1, 2, 3, 4))

    def diff_attn(self, l):
        P, S = self.P, self.S
        NB = S // 128
        lam_init = 0.8 - 0.6 * math.exp(-0.3 * l)
        with P.phase("diff"):
            accA, accB, zring = P.psum_pools(2, 2, 3)
            acc_sets = (accA, accB)
            pend = []
            nq = 0
            qms2 = [[P.sb("qm", [128, S], BF16) for _ in range(4)] for _ in range(2)]
            kT2_ = [P.sb("kT", [128, S], BF16) for _ in range(2)]
            vst2 = [P.sb("vst", [128, NB, 64], F32) for _ in range(2)]
            V2 = [P.sb("V", [128, NB, 128], BF16) for _ in range(2)]

            def load_head(h):
                if h % 2 == 0:
                    self.load_masked_q(qms2[(h // 2) % 2], CH_DQ + h // 2, 32)
                    P.dma("sp", out=kT2_[(h // 2) % 2][:], in_=self.pT.ap()[(CH_DK + h // 2) * 128:(CH_DK + h // 2 + 1) * 128, :])
                self.load_vaug(V2[h % 2], vst2[h % 2], TM_DV + h * 64)
            Pm = [P.sb("Pm", [128, 512], BF16) for _ in range(4)]

            def stA(i_):
                P.op("pe", "matmul", out=i_["zp"][:, i_["cs"]], lhsT=i_["kT"][:, i_["kb"] * 128:(i_["kb"] + 1) * 128],
                     rhs=i_["qm"][:, i_["q0"] + i_["c0"]:i_["q0"] + 512], start=True, stop=True)

            def stB(i_):
                cs, c0 = i_["cs"], i_["c0"]
                P.op("act", "activation", out=i_["p"][:, cs], in_=i_["zp"][:, cs], func=AF.Exp, scale=32 ** -0.5)
                if i_["j"] >= 0:
                    P.op("pool", "tensor_tensor", out=i_["p"][:, c0:c0 + 128], in0=i_["p"][:, c0:c0 + 128], in1=self.MLE, op=ALU.mult)

            def stC(i_):
                P.op("pe", "matmul", out=i_["acc"][i_["c"]][:, i_["cs"]], lhsT=i_["V"][:, i_["kb"], :], rhs=i_["p"][:, i_["cs"]],
                     start=i_["first"], stop=i_["last"], skip_group_check=True)
            lt = P.sb("lt", [128, 128], F32)
            lp = P.sb("lp", [128, 64], F32)
            ls = P.sb("ls", [128, 4], F32)
            rr = [P.sb("rr", [64, 512], F32) for _ in range(2)]
            tt = [P.sb("tt", [64, 512], F32) for _ in range(2)]
            scr = (P.sb("sqh", [64, 512], BF16), P.sb("rs", [64, 512], F32), P.sb("obf", [64, 512], BF16))
            P.dma("sp", out=lt[:], in_=self.lamb.ap()[l])
            P.op("dve", "tensor_tensor", out=lp[:, 0:32], in0=lt[:, 0:32], in1=lt[:, 32:64], op=ALU.mult)
            P.op("dve", "tensor_tensor", out=lp[:, 32:64], in0=lt[:, 64:96], in1=lt[:, 96:128], op=ALU.mult)
            P.op("dve", "tensor_reduce", out=ls[:, 0:1], in_=lp[:, 0:32], axis=AX.X, op=ALU.add)
            P.op("dve", "tensor_reduce", out=ls[:, 1:2], in_=lp[:, 32:64], axis=AX.X, op=ALU.add)
            P.op("act", "activation", out=ls[:, 0:2], in_=ls[:, 0:2], func=AF.Exp)
            P.op("dve", "tensor_tensor", out=ls[:, 2:3], in0=ls[:, 1:2], in1=ls[:, 0:1], op=ALU.subtract)
            P.op("dve", "tensor_scalar", out=ls[:, 3:4], in0=ls[:, 2:3], scalar1=-lam_init, scalar2=None, op0=ALU.add)
            nlam = ls[0:64, 3:4]
            it = 0
            for h in range(4):
                r0 = (h % 2) * 64
                if h == 0:
                    load_head(0)
                if h + 1 < 4:
                    load_head(h + 1)
                qms, kT, V = qms2[(h // 2) % 2], kT2_[(h // 2) % 2], V2[h % 2]
                for qt in range(S // 512):
                    q0 = qt * 512
                    nkb = 4 * qt + 4
                    acc = acc_sets[nq % 2]
                    nq += 1
                    iters = []
                    for c in range(2):
                        for kb in range(nkb):
                            j = kb - 4 * qt
                            c0 = 128 * max(j, 0)
                            iters.append(dict(c=c, kb=kb, j=j, c0=c0, cs=slice(c0, 512), q0=q0, qm=qms[(h % 2) * 2 + c], kT=kT, V=V, zp=zring[it % 3], p=Pm[it % 4], acc=acc,
                                              first=(kb == 0), last=(kb == nkb - 1)))
                            it += 1
                    self.pipe(iters, (stA, stB, stC), (0, 1, 2), drip=pend)

                    def epi(acc=acc, h=h, q0=q0):
                        ops = []
                        for c in range(2):
                            ops.append(lambda c=c: P.op("act", "activation", out=rr[c][:], in_=acc[c][64:128, :], func=AF.Ln))
                            ops.append(lambda c=c: P.op("act", "activation", out=rr[c][:], in_=rr[c][:], func=AF.Exp, scale=-1.0))
                            ops.append(lambda c=c: P.op("dve", "tensor_tensor", out=tt[c][:], in0=acc[c][0:64, :], in1=rr[c][:], op=ALU.mult))
                        ops.append(lambda: P.op("dve", "scalar_tensor_tensor", out=tt[0][:], in0=tt[1][:], scalar=nlam, in1=tt[0][:],
                                                op0=ALU.mult, op1=ALU.add))
                        ops += self.head_norm_store(l, tt[0], C_DHN, 1.0 - lam_init, None, self.oT.ap()[0, h * 64:(h + 1) * 64, q0:q0 + 512], scr)
                        return ops
                    pend = epi()
            while pend:
                pend.pop(0)()

    def mlstm(self, l):
        P, S = self.P, self.S
        NB = S // 128
        g0 = CH_MISC * 128 + 96
        with P.phase("mlg"):
            zi = P.sb("zi", [4, S], F32)
            zf = P.sb("zf", [4, S], F32)
            on = P.sb("on", [4, S], F32)
            cs_ = P.sb("cs", [4, S], F32)
            P.dma("sp", out=zi[:], in_=self.zT.ap()[g0:g0 + 4, :])
            P.dma("sp", out=zf[:], in_=self.zT.ap()[g0 + 4:g0 + 8, :])
            P.op("pool", "memset", ap=on[:], constant=1.0)
            P.op("dve", "tensor_scalar", out=zi[:], in0=zi[:], scalar1=self.col(l, C_GB, 0, 4), scalar2=None, op0=ALU.add)
            P.op("dve", "tensor_scalar", out=zf[:], in0=zf[:], scalar1=self.col(l, C_GB + 1, 0, 4), scalar2=None, op0=ALU.add)
            P.op("act", "activation", out=zf[:], in_=zf[:], func=AF.Exp, scale=-1.0)
            P.op("act", "activation", out=zf[:], in_=zf[:], func=AF.Ln, bias=1.0)
            src, dst = zf, on
            d_ = 1
            while d_ < S:
                P.op("dve", "tensor_copy", out=dst[:, 0:d_], in_=src[:, 0:d_])
                P.op("dve", "tensor_tensor", out=dst[:, d_:S], in0=src[:, d_:S], in1=src[:, 0:S - d_], op=ALU.add)
                src, dst = dst, src
                d_ *= 2
            P.op("dve", "tensor_copy", out=cs_[:], in_=src[:])
            P.op("dve", "tensor_tensor", out=zi[:], in0=zi[:], in1=cs_[:], op=ALU.add)
            P.op("dve", "tensor_scalar", out=cs_[:], in0=cs_[:], scalar1=-1.0, scalar2=None, op0=ALU.mult)
            P.dma("sp", out=self.grow.ap()[0:4, :], in_=cs_[:])
            P.dma("sp", out=self.grow.ap()[4:8, :], in_=zi[:])
        with P.phase("ml"):
            accs, zring = P.psum_pools(2, 4)
            pend = []
            nq = 0
            qms2 = [[P.sb("qm", [128, S], BF16) for _ in range(2)] for _ in range(2)]
            kT2_ = [P.sb("kT", [128, S], BF16) for _ in range(2)]
            og2 = [P.sb("og", [64, S], BF16) for _ in range(2)]
            vst2 = [P.sb("vst", [128, NB, 64], F32) for _ in range(2)]
            V2 = [P.sb("V", [128, NB, 128], BF16) for _ in range(2)]
            Bbc2 = [P.sb("Bbc", [128, S], F32) for _ in range(2)]
            acol2 = [P.sb("acol", [128, NB], F32) for _ in range(2)]

            def load_head(h):
                r0 = (h % 2) * 64
                if h % 2 == 0:
                    self.load_masked_q(qms2[(h // 2) % 2], CH_MQ + h // 2, 64)
                    P.dma("sp", out=kT2_[(h // 2) % 2][:], in_=self.pT.ap()[(CH_MK + h // 2) * 128:(CH_MK + h // 2 + 1) * 128, :])
                P.dma("sp", out=og2[h % 2][:], in_=self.pT.ap()[(CH_MO + h // 2) * 128 + r0:(CH_MO + h // 2) * 128 + r0 + 64, :])
                P.dma("sp", out=Bbc2[h % 2][:], in_=self.grow.ap()[h:h + 1, :].broadcast_to([128, S]))
                P.dma("sp", out=acol2[h % 2][:], in_=self.grow.ap()[4 + h].rearrange("(nb p) -> p nb", p=128), allow_slow_non_contiguous=True)
                self.load_vaug(V2[h % 2], vst2[h % 2], TM_MV + h * 64)
            D = [P.sb("D", [128, 512], F32) for _ in range(3)]
            Pm = [P.sb("Pm", [128, 512], BF16) for _ in range(4)]

            def stA(i_):
                cs = i_["cs"]
                P.op("pe", "matmul", out=i_["zp"][:, cs], lhsT=i_["kT"][:, i_["kb"] * 128:(i_["kb"] + 1) * 128],
                     rhs=i_["qm"][:, i_["q0"] + i_["c0"]:i_["q0"] + 512], start=True, stop=True)
                P.op("act", "activation", out=i_["d"][:, cs], in_=i_["Bbc"][:, i_["q0"] + i_["c0"]:i_["q0"] + 512], func=AF.Exp,
                     bias=i_["acol"][:, i_["kb"]:i_["kb"] + 1])

            def stB(i_):
                cs, c0 = i_["cs"], i_["c0"]
                P.op("dve", "scalar_tensor_tensor", out=i_["p"][:, cs], in0=i_["zp"][:, cs], scalar=0.125, in1=i_["d"][:, cs],
                     op0=ALU.mult, op1=ALU.mult)
                if i_["j"] >= 0:
                    P.op("pool", "tensor_tensor", out=i_["p"][:, c0:c0 + 128], in0=i_["p"][:, c0:c0 + 128], in1=self.MLE, op=ALU.mult)

            def stC(i_):
                P.op("pe", "matmul", out=i_["acc"][:, i_["cs"]], lhsT=i_["V"][:, i_["kb"], :], rhs=i_["p"][:, i_["cs"]], start=i_["first"], stop=i_["last"], skip_group_check=True)
            dd = P.sb("dd", [64, 512], F32)
            hh = P.sb("hh", [64, 512], F32)
            scr = (P.sb("sqh", [64, 512], BF16), P.sb("rs", [64, 512], F32), P.sb("obf", [64, 512], BF16))
            it = 0
            for h in range(4):
                if h == 0:
                    load_head(0)
                if h + 1 < 4:
                    load_head(h + 1)
                qms, kT, og, V = qms2[(h // 2) % 2], kT2_[(h // 2) % 2], og2[h % 2], V2[h % 2]
                Bbc, acol = Bbc2[h % 2], acol2[h % 2]
                for qt in range(S // 512):
                    q0 = qt * 512
                    nkb = 4 * qt + 4
                    acc = accs[nq % 2]
                    nq += 1
                    iters = []
                    for kb in range(nkb):
                        j = kb - 4 * qt
                        c0 = 128 * max(j, 0)
                        iters.append(dict(kb=kb, j=j, c0=c0, cs=slice(c0, 512), q0=q0, qm=qms[h % 2], kT=kT, V=V, Bbc=Bbc, acol=acol, zp=zring[it % 4], d=D[it % 3], p=Pm[it % 4], acc=acc,
                                          first=(kb == 0), last=(kb == nkb - 1)))
                        it += 1
                    self.pipe(iters, (stA, stB, stC), (0, 1, 2), drip=pend)

                    def epi(acc=acc, h=h, q0=q0):
                        ops = [lambda: P.op("act", "activation", out=dd[:], in_=acc[64:128, :], func=AF.Abs),
                               lambda: P.op("dve", "tensor_scalar", out=dd[:], in0=dd[:], scalar1=1.0, scalar2=None, op0=ALU.max),
                               lambda: P.op("act", "activation", out=dd[:], in_=dd[:], func=AF.Ln),
                               lambda: P.op("act", "activation", out=dd[:], in_=dd[:], func=AF.Exp, scale=-1.0),
                               lambda: P.op("dve", "tensor_tensor", out=hh[:], in0=acc[0:64, :], in1=dd[:], op=ALU.mult)]
                        ops += self.head_norm_store(l, hh, C_MHN, 1.0, og[:, q0:q0 + 512], self.oT.ap()[1, h * 64:(h + 1) * 64, q0:q0 + 512], scr)
                        return ops
                    pend = epi()
                while pend:
                    pend.pop(0)()

    def dsa(self, l):
        P, S = self.P, self.S
        NB = S // 128
        NIT = 16
        with P.phase("dsa"):
            acc, zring = P.psum_pools(4, 4)
            P.ring = [4, 5, 6, 7]
            qms = [P.sb("qm", [128, S], BF16) for _ in range(4)]
            kT2 = P.sb("kT2", [128, S], BF16)
            qiT = [P.sb("qiT", [128, S], BF16) for _ in range(2)]
            kiT4 = P.sb("kiT4", [128, S], BF16)
            wq = P.sb("wq", [128, NB, 8], F32)
            V = P.sb("V", [128, NB, 128], BF16)
            score = P.sb("score", [128, S], F32)
            scoreB = P.sb("scoreB", [128, S], F32)
            msel = P.sb("msel", [128, S], F32)
            vst = msel[:, 0:NB * 64].rearrange("p (a b) -> p a b", b=64)
            junk = P.sb("junk", [128, S], BF16)
            junkB = P.sb("junkB", [128, S], BF16)
            smB = P.sb("smB", [128, 8], F32)
            wkB = P.sb("wkB", [128, 32], F32)
            nwkB = P.sb("nwkB", [128, 32], F32)
            nm = P.sb("nm", [128, 2], F32)
            maskT = P.sb("maskT", [128, NB, 512], BF16)
            rl = [P.sb("rl", [128, 512], F32) for _ in range(3)]
            E = [P.sb("E", [128, 512], BF16) for _ in range(4)]
            Pm = [P.sb("Pm", [128, 512], BF16) for _ in range(4)]

            def stA(i_):
                P.op("pe", "matmul", out=i_["zp"][:, i_["cs"]], lhsT=kT2[:, i_["kb"] * 128:(i_["kb"] + 1) * 128],
                     rhs=qms[i_["h"]][:, i_["q0"] + i_["c0"]:i_["q0"] + 512], start=True, stop=True)

            def stB(i_):
                cs = i_["cs"]
                P.op("act", "activation", out=i_["e"][:, cs], in_=i_["zp"][:, cs], func=AF.Exp, scale=0.125)
                P.op("dve", "tensor_tensor", out=i_["p"][:, cs], in0=i_["e"][:, cs], in1=maskT[:, i_["kb"], cs], op=ALU.mult)

            def stC(i_):
                P.op("pe", "matmul", out=acc[i_["h"]][:, i_["cs"]], lhsT=V[:, i_["kb"], :], rhs=i_["p"][:, i_["cs"]],
                     start=i_["first"], stop=i_["last"], skip_group_check=True)
            sm = P.sb("sm", [128, 8], F32)
            wk = P.sb("wk", [128, 32], F32)
            nwk = P.sb("nwk", [128, 32], F32)
            rr = P.sb("rr", [64, 512], F32)
            ob = [P.sb("ob", [64, 512], BF16) for _ in range(2)]
            for c in range(2):
                self.load_masked_q(qms[2 * c:2 * c + 2], CH_AQ + c, 64)
                P.dma("sp", out=qiT[c][:], in_=self.pT.ap()[(CH_IQ + c) * 128:(CH_IQ + c + 1) * 128, :])
                P.dma("sp", out=kT2[c * 64:(c + 1) * 64, :], in_=self.pT.ap()[CH_MISC * 128:CH_MISC * 128 + 64, :])
            for c in range(4):
                P.dma("sp", out=kiT4[c * 32:(c + 1) * 32, :], in_=self.pT.ap()[CH_MISC * 128 + 64:CH_MISC * 128 + 96, :])
            P.dma("sp", out=wq[:], in_=self.vtok.ap().rearrange("(nb p) n -> p nb n", p=128)[:, :, TM_IW:TM_IW + 8])
            self.load_vaug(V, vst, TM_AV)
            it = 0

            def indexer(qb, sc):
                nonlocal it
                n = (qb + 1) * 128
                for kc in range((n + 511) // 512):
                    nk = min(512, n - kc * 512)
                    ks = slice(kc * 512, kc * 512 + nk)
                    for hh in range(8):
                        g_ = hh % 4
                        rp = P.psum()
                        r_ = rl[it % 3]
                        it += 1
                        P.op("pe", "matmul", out=rp[:, 0:nk], lhsT=qiT[hh // 4][32 * g_:32 * g_ + 32, qb * 128:(qb + 1) * 128],
                             rhs=kiT4[32 * g_:32 * g_ + 32, ks], start=True, stop=True, tile_position=(32 * g_, 0))
                        P.op("act", "activation", out=r_[:, 0:nk], in_=rp[:, 0:nk], func=AF.Relu)
                        if hh == 0:
                            P.op("dve", "tensor_scalar", out=sc[:, ks], in0=r_[:, 0:nk], scalar1=wq[:, qb, 0:1], scalar2=None, op0=ALU.mult)
                        else:
                            P.op("dve", "scalar_tensor_tensor", out=sc[:, ks], in0=r_[:, 0:nk], scalar=wq[:, qb, hh:hh + 1], in1=sc[:, ks],
                                 op0=ALU.mult, op1=ALU.add)
                P.op("pool", "tensor_tensor", out=sc[:, qb * 128:n], in0=sc[:, qb * 128:n], in1=self.ADDMASK, op=ALU.add)

            def bis_setup(qb, sc, sm_, wk_, nwk_):
                n = (qb + 1) * 128
                P.op("dve", "tensor_reduce", out=sm_[:, 5:6], in_=sc[:, 0:n], axis=AX.X, op=ALU.max)
                P.op("dve", "tensor_reduce", out=sm_[:, 0:1], in_=sc[:, 0:qb * 128], axis=AX.X, op=ALU.min)
                P.op("dve", "tensor_tensor", out=sm_[:, 1:2], in0=sm_[:, 5:6], in1=sm_[:, 0:1], op=ALU.subtract)
                P.op("dve", "tensor_scalar", out=sm_[:, 1:2], in0=sm_[:, 1:2], scalar1=1.0001, scalar2=1e-6, op0=ALU.mult, op1=ALU.add)
                P.op("dve", "tensor_scalar", out=wk_[:, 0:NIT + 1], in0=self.POW2[:, 0:NIT + 1], scalar1=sm_[:, 1:2], scalar2=None, op0=ALU.mult)
                P.op("dve", "tensor_scalar", out=nwk_[:, 0:NIT + 1], in0=wk_[:, 0:NIT + 1], scalar1=-1.0, scalar2=None, op0=ALU.mult)

            def transposes(qb, j4):
                for kb0 in range(0, qb + 1, 4):
                    nb_ = min(4, qb + 1 - kb0)
                    tp = P.psum()
                    for i in range(nb_):
                        P.op("pe", "transpose", out=tp[:, i * 128:(i + 1) * 128], in_=msel[:, (kb0 + i) * 128:(kb0 + i + 1) * 128], identity=self.IDENTF)
                    P.op("act", "activation", out=maskT[:, kb0:kb0 + nb_, j4 * 128:(j4 + 1) * 128],
                         in_=tp[:, 0:nb_ * 128].rearrange("p (a b) -> p a b", a=nb_), func=AF.Copy)

            for qt in range(S // 512):
                q0 = qt * 512
                for jp in range(2):
                    qa, qb_ = 4 * qt + 2 * jp, 4 * qt + 2 * jp + 1
                    na, nb2 = (qa + 1) * 128, (qb_ + 1) * 128
                    indexer(qa, score)
                    indexer(qb_, scoreB)
                    if qa < 2:
                        P.op("dve", "tensor_scalar", out=msel[:, 0:na], in0=score[:, 0:na], scalar1=-1.0e29, scalar2=None, op0=ALU.is_ge)
                        transposes(qa, 2 * jp)
                        P.op("dve", "tensor_scalar", out=msel[:, 0:nb2], in0=scoreB[:, 0:nb2], scalar1=-1.0e29, scalar2=None, op0=ALU.is_ge)
                        transposes(qb_, 2 * jp + 1)
                        continue
                    bis_setup(qa, score, sm, wk, nwk)
                    bis_setup(qb_, scoreB, smB, wkB, nwkB)
                    P.op("dve", "tensor_tensor", out=sm[:, 2:3], in0=sm[:, 0:1], in1=wk[:, 0:1], op=ALU.add)
                    P.op("dve", "scalar_tensor_tensor", out=nm[:, 0:1], in0=smB[:, 0:1], scalar=-1.0, in1=nwkB[:, 0:1], op0=ALU.mult, op1=ALU.add)
                    for k in range(NIT):
                        P.op("dve", "tensor_scalar", out=junk[:, 0:na], in0=score[:, 0:na], scalar1=sm[:, 2:3], scalar2=None,
                             op0=ALU.is_ge, op1=ALU.add, accum_out=sm[:, 3:4])
                        P.op("dve", "scalar_tensor_tensor", out=sm[:, 4:5], in0=sm[:, 3:4], scalar=TOPK - 0.5, in1=wk[:, k:k + 1],
                             op0=ALU.is_ge, op1=ALU.mult)
                        P.op("dve", "scalar_tensor_tensor", out=sm[:, 2:3], in0=sm[:, 4:5], scalar=nwk[:, k + 1:k + 2], in1=sm[:, 2:3],
                             op0=ALU.add, op1=ALU.add)
                        P.op("act", "activation", out=junkB[:, 0:nb2], in_=scoreB[:, 0:nb2], func=AF.Sign, bias=nm[:, k % 2:k % 2 + 1],
                             accum_out=smB[:, 3:4])
                        P.op("act", "activation", out=smB[:, 4:5], in_=smB[:, 3:4], func=AF.Sign, bias=float(nb2 - 2 * TOPK + 1))
                        P.op("act", "activation", out=nm[:, (k + 1) % 2:(k + 1) % 2 + 1], in_=smB[:, 4:5], func=AF.Identity,
                             scale=nwkB[:, k + 1:k + 2], bias=nm[:, k % 2:k % 2 + 1])
                    P.op("dve", "tensor_tensor", out=sm[:, 6:7], in0=sm[:, 2:3], in1=nwk[:, NIT:NIT + 1], op=ALU.add)
                    P.op("dve", "scalar_tensor_tensor", out=smB[:, 6:7], in0=nm[:, NIT % 2:NIT % 2 + 1], scalar=-1.0, in1=nwkB[:, NIT:NIT + 1],
                         op0=ALU.mult, op1=ALU.add)
                    P.op("dve", "tensor_scalar", out=msel[:, 0:na], in0=score[:, 0:na], scalar1=sm[:, 6:7], scalar2=None, op0=ALU.is_ge)
                    transposes(qa, 2 * jp)
                    P.op("dve", "tensor_scalar", out=msel[:, 0:nb2], in0=scoreB[:, 0:nb2], scalar1=smB[:, 6:7], scalar2=None, op0=ALU.is_ge)
                    transposes(qb_, 2 * jp + 1)
                nkb = 4 * qt + 4
                iters = []
                for kb in range(nkb):
                    j = kb - 4 * qt
                    c0 = 128 * max(j, 0)
                    for h in range(4):
                        iters.append(dict(h=h, kb=kb, j=j, c0=c0, cs=slice(c0, 512), q0=q0, zp=zring[it % 4], e=E[it % 4], p=Pm[it % 4],
                                          first=(kb == 0), last=(kb == nkb - 1)))
                        it += 1
                self.pipe(iters, (stA, stB, stC), (0, 1, 2))
                for h in range(4):
                    o_ = ob[h % 2]
                    P.op("act", "activation", out=rr[:], in_=acc[h][64:128, :], func=AF.Ln)
                    P.op("act", "activation", out=rr[:], in_=rr[:], func=AF.Exp, scale=-1.0)
                    P.op("dve", "tensor_tensor", out=o_[:], in0=acc[h][0:64, :], in1=rr[:], op=ALU.mult)
                    P.dma("sp", out=self.oT.ap()[3, h * 64:(h + 1) * 64, q0:q0 + 512], in_=o_[:])

    def mixers(self, l):
        ph = self.phases
        if "prep" not in self.skip:
            self.prep(l)
        if "diff" in ph:
            self.diff_attn(l)
        if "ml" in ph:
            self.mlstm(l)
        if "sb" in ph:
            self.sb_attn(l)
        if "dsa" in ph:
            self.dsa(l)


_IN_OFF = {}
_o = 0
for _n, _w in (("diff_q", 256), ("diff_k", 256), ("diff_v", 256), ("ml_qk", 512), ("ml_v", 256), ("ml_i", 4), ("ml_f", 4),
               ("ml_o", 256), ("sb_q", 256), ("sb_k", 256), ("sb_v", 256), ("dsa_q", 256), ("dsa_k", 64), ("dsa_v", 64),
               ("idx_q", 256), ("idx_k", 32), ("idx_w", 8), ("gates", 4096)):
    _IN_OFF[_n] = (_o, _o + _w)
    _o += _w


def _slab(w, n):
    K, N = w.shape
    return np.ascontiguousarray(w.reshape(K // 128, 128, N // n, n).transpose(2, 1, 0, 3).reshape(N // n, 128, (K // 128) * n))


def _cols(*names):
    idx = []
    for nm in names:
        a, b = _IN_OFF[nm]
        idx.extend(range(a, b))
    return np.array(idx)


def make_constants():
    p = np.arange(128)
    ccol = np.zeros((128, 8), np.float32)
    d32, d64 = p % 32, p % 64
    ccol[:, CC_INV32] = np.where(d32 < 8, ROPE_THETA ** (-(d32 % 4) / 4.0), 0.0)
    ccol[:, CC_SGN32] = np.where(d32 < 4, -1.0, np.where(d32 < 8, 1.0, 0.0))
    ccol[:, CC_INV64] = np.where(d64 < 16, ROPE_THETA ** (-(d64 % 8) / 8.0), 0.0)
    ccol[:, CC_SGN64] = np.where(d64 < 8, -1.0, np.where(d64 < 16, 1.0, 0.0))
    ccol[:, CC_EPS] = EPS
    ccol[:, CC_ONE] = 1.0
    i, j = p[:, None], p[None, :]
    cmatb = np.zeros((128, 6, 128), np.float32)
    cmatb[:, 0] = 1.0
    cmatb[:, 1] = (i // 32 == j // 32)
    cmatb[:, 2] = (i // 64 == j // 64)
    cmatb[:, 3] = (i >= j)
    cmatb[:, 4] = (i <= j)
    cmatb[:, 5] = (i < j)
    cmatf = np.zeros((128, 5, 128), np.float32)
    cmatf[:, 4, :] = (2.0 ** -(np.arange(128, dtype=np.float64) + 1.0)).astype(np.float32)[None, :]
    part32 = np.where(d32 < 4, p + 4, np.where(d32 < 8, p - 4, -1))
    part64 = np.where(d64 < 8, p + 8, np.where(d64 < 16, p - 8, -1))
    for m in range(128):
        if part32[m] >= 0:
            cmatf[part32[m], 0, m] = 1.0
        if part64[m] >= 0:
            cmatf[part64[m], 1, m] = 1.0
    cmatf[:, 2] = np.where(j <= i, 0.0, NEG)
    cmatf[:, 3] = (i == j)
    return ccol, cmatb, cmatf


def prep_shared(inp):
    L = inp["w_in"].shape[0]
    f32 = lambda a: np.ascontiguousarray(a, dtype=np.float32)
    out = {}
    for i, nm in ((1, "ffn1"), (2, "ffn2")):
        gu = np.asarray(inp[f"{nm}_w_gu"])
        dn = np.asarray(inp[f"{nm}_w_down"])
        g2 = np.concatenate([gu[:, :, :D_FF].reshape(L, D_MODEL, NFF, 128), gu[:, :, D_FF:].reshape(L, D_MODEL, NFF, 128)], axis=3)
        out[f"gu{i}"] = f32(np.stack([_slab(g2[l].reshape(D_MODEL, NFF * 256), 256) for l in range(L)]))
        out[f"dn{i}"] = f32(np.stack([_slab(dn[l], 128) for l in range(L)]))
    w_in = np.asarray(inp["w_in"])
    fm_idx = _cols("diff_q", "diff_k", "ml_qk", "ml_o", "sb_q", "sb_k", "dsa_q", "idx_q", "dsa_k", "idx_k", "ml_i", "ml_f")
    tm_idx = _cols("diff_v", "ml_v", "sb_v", "dsa_v", "idx_w")
    wfm = np.zeros((L, D_MODEL, N_FM * 128), np.float32)
    wfm[:, :, :len(fm_idx)] = w_in[:, :, fm_idx]
    out["wfm"] = f32(np.stack([_slab(wfm[l], 128) for l in range(L)]))
    wtm = w_in[:, :, tm_idx]
    out["wtm"] = f32(wtm.reshape(L, NDC, 128, N_TM).transpose(0, 2, 1, 3).reshape(L, 128, NDC * N_TM))
    g0 = _IN_OFF["gates"][0]
    out["wgt"] = f32(np.stack([_slab(w_in[l][:, g0:], 128) for l in range(L)]))
    wb = np.asarray(inp["w_branch"])
    out["wbr"] = f32(np.stack([np.concatenate([_slab(wb[l, b], 128) for b in range(4)], axis=0) for l in range(L)]))
    out["wout"] = f32(np.stack([_slab(np.asarray(inp["w_out"])[l], 128) for l in range(L)]))
    p = np.arange(128)
    colp = np.zeros((L, 128, NCOL), np.float32)
    for l in range(L):
        colp[l, :, C_FFN1:C_FFN1 + 8] = np.asarray(inp["ffn1_norm"])[l].reshape(8, 128).T
        colp[l, :, C_MIX:C_MIX + 8] = np.asarray(inp["mix_norm"])[l].reshape(8, 128).T
        colp[l, :, C_FFN2:C_FFN2 + 8] = np.asarray(inp["ffn2_norm"])[l].reshape(8, 128).T
        colp[l, :, C_DQN] = np.asarray(inp["diff_qk_norm"])[l, 0][p % 32]
        colp[l, :, C_DKN] = np.asarray(inp["diff_qk_norm"])[l, 1][p % 32]
        colp[l, :, C_DHN] = np.asarray(inp["diff_head_norm"])[l][p % 64]
        colp[l, :, C_MHN] = np.asarray(inp["ml_head_norm"])[l][p % 64]
        colp[l, :, C_SQN] = np.asarray(inp["dsa_qk_norm"])[l, 0][p % 64]
        colp[l, :, C_SKN] = np.asarray(inp["dsa_qk_norm"])[l, 1][p % 64]
        cw = np.asarray(inp["ml_conv_w"])[l]
        cb = np.asarray(inp["ml_conv_b"])[l]
        for k in range(4):
            for tap in range(4):
                colp[l, :, C_CW + k * 4 + tap] = cw[tap, k * 128:(k + 1) * 128]
            colp[l, :, C_CB + k] = cb[k * 128:(k + 1) * 128]
        gb = np.asarray(inp["ml_gate_bias"])[l]
        colp[l, 0:4, C_GB] = gb[0]
        colp[l, 0:4, C_GB + 1] = gb[1]
    out["colp"] = colp
    lam = np.asarray(inp["diff_lambda"]).reshape(L, 1, 128)
    out["lamb"] = f32(np.broadcast_to(lam, (L, 128, 128)))
    out["ccol"], out["cmatb"], out["cmatf"] = make_constants()
    return out


_CACHE = {}


def kernel(**inputs):
    x = np.asarray(inputs["x"])
    B, S, D = x.shape
    L = np.asarray(inputs["w_in"]).shape[0]
    shared = prep_shared(inputs)
    key = (S, L)
    if key not in _CACHE:
        _CACHE[key] = Builder(S, L).build()
    nc = _CACHE[key]
    pos = np.asarray(inputs["positions"]).astype(np.int32)
    in_maps = []
    for b in range(B):
        m = dict(shared)
        m["xT"] = np.ascontiguousarray(x[b].T)
        m["pos"] = np.ascontiguousarray(pos[b].reshape(1, S))
        in_maps.append(m)
    res = run_bass_kernel_spmd(nc, in_maps, core_ids=list(range(B)))
    out = np.stack([np.ascontiguousarray(r["outT"].T) for r in res.results], axis=0)
    return out.astype(np.float32)
```

```python
import math
from contextlib import ExitStack, contextmanager
import numpy as np
import concourse.bass as bass
import concourse.mybir as mybir
from concourse.bass_utils import run_bass_kernel_spmd

F32 = mybir.dt.float32
BF16 = mybir.dt.bfloat16
I32 = mybir.dt.int32
ALU = mybir.AluOpType
AF = mybir.ActivationFunctionType
AX = mybir.AxisListType

D_MODEL = 1024
D_FF = 2816
NFF = D_FF // 128
NDC = D_MODEL // 128
EPS = 1e-6
ROPE_THETA = 500000.0
N_FM = 19
N_TM = 840
TOPK = 256
NEG = -1.0e30

SAME_ENG_SYNC = True
ENGS = ("pe", "act", "dve", "pool", "sp")
_ENG_ATTR = {"pe": "tensor", "act": "scalar", "dve": "vector", "pool": "gpsimd", "sp": "sync"}


class Buf:
    __slots__ = ("name", "t", "w", "r", "dsem", "dcnt")

    def __init__(self, name, t):
        self.name = name
        self.t = t
        self.w = None
        self.r = {}
        self.dsem = None
        self.dcnt = 0

    def __getitem__(self, idx):
        return self.t[idx]


class Prog:
    def __init__(self, nc):
        self.nc = nc
        self.ops = {e: [] for e in ENGS}
        self.cnt = {e: 0 for e in ENGS}
        self.seen = {e: {} for e in ENGS}
        self.bufs = {}
        self.dsems = {}
        self.free_dsems = []
        self.sw_sems = set()
        self.bg_sems = set()
        self.nid = 0
        self.ninst = 0
        self.stack = None
        self.phase_bufs = []
        self.psum_ring = []
        self.psum_i = 0
        self.ring = list(range(8))

    def _reg(self, name, t):
        b = Buf(name, t)
        self.bufs[t.name] = b
        return b

    def sb(self, name, shape, dt):
        self.nid += 1
        t = self.stack.enter_context(self.nc.sbuf_tensor(f"{name}_{self.nid}", list(shape), dt))
        b = self._reg(name, t)
        self.phase_bufs.append(b)
        return b

    def sb_global(self, name, shape, dt):
        self.nid += 1
        t = self.nc.alloc_sbuf_tensor(f"{name}_{self.nid}", list(shape), dt)
        return self._reg(name, t)

    def init_psum(self):
        for i in range(8):
            t = self.nc.alloc_psum_tensor(f"psb{i}", [128, 512], F32)
            self.psum_ring.append(self._reg(f"psb{i}", t))

    def psum(self):
        r = self.ring
        b = self.psum_ring[r[self.psum_i % len(r)]]
        self.psum_i += 1
        return b

    def psum_pools(self, *sizes):
        out, k = [], 0
        for n in sizes:
            out.append([self.psum_ring[k + i] for i in range(n)])
            k += n
        self.ring = list(range(k, 8)) or [7]
        return out

    def psum_reserve(self, n):
        self.ring = list(range(n, 8))
        return [self.psum_ring[i] for i in range(n)]

    @contextmanager
    def phase(self, name):
        self.stack = ExitStack()
        self.phase_bufs = []
        self.ring = list(range(8))
        try:
            yield
        finally:
            self.barrier()
            for b in self.phase_bufs:
                if b.dsem is not None and b.dsem not in self.sw_sems:
                    self.free_dsems.append((b.dsem, b.dcnt))
                self.bufs.pop(b.t.name, None)
            self.stack.close()
            self.stack = None

    def _deps(self, eng, reads, writes):
        need = {}

        def add(k, v):
            if k == eng and (not SAME_ENG_SYNC or eng == "pe"):
                return
            if need.get(k, 0) < v:
                need[k] = v
        for b in reads:
            if b.w is not None:
                add(*b.w)
        for b in writes:
            if b.w is not None:
                add(*b.w)
            for k, v in b.r.items():
                add(k, v)
        out = []
        seen = self.seen[eng]
        for k, v in need.items():
            if seen.get(k, 0) >= v:
                continue
            seen[k] = v
            out.append((k, v))
        return out

    def _mark(self, ev, reads, writes):
        for b in reads:
            if b.r.get(ev[0], 0) < ev[1]:
                b.r[ev[0]] = ev[1]
        for b in writes:
            b.w = ev
            b.r = {}
        self.ninst += 1

    def _classify(self, kw):
        reads, writes = [], []
        for k, v in kw.items():
            if not hasattr(v, "tensor") or not hasattr(v, "space"):
                continue
            b = self.bufs.get(v.tensor.name)
            if b is None:
                continue
            if k in ("out", "accum_out", "ap"):
                if b not in writes:
                    writes.append(b)
            else:
                if b not in reads:
                    reads.append(b)
        return reads, writes

    def op(self, eng, name, _after=(), **kw):
        reads, writes = self._classify(kw)
        for b in _after:
            if b not in reads:
                reads.append(b)
        waits = self._deps(eng, reads, writes)
        self.cnt[eng] += 1
        ev = (eng, self.cnt[eng])
        self.ops[eng].append((waits, name, kw, (eng, 1)))
        self._mark(ev, reads, writes)

    def dma(self, q, out, in_, **kw):
        bo = self.bufs.get(out.tensor.name)
        bi = self.bufs.get(in_.tensor.name)
        owner = bo if bo is not None else bi
        assert owner is not None
        if owner.dsem is None:
            if self.free_dsems and q != "pool":
                owner.dsem, owner.dcnt = self.free_dsems.pop()
            else:
                owner.dsem = f"d{len(self.dsems)}"
            self.dsems[owner.dsem] = owner
        if q == "pool":
            self.sw_sems.add(owner.dsem)
        reads = [bi] if bi is not None else []
        writes = [bo] if bo is not None else []
        waits = self._deps(q, reads, writes)
        if owner.dcnt > 0:
            k, v = owner.dsem, owner.dcnt
            if self.seen[q].get(k, 0) < v:
                self.seen[q][k] = v
                waits.append((k, v))
        owner.dcnt += 16
        ev = (owner.dsem, owner.dcnt)
        d = dict(out=out, in_=in_)
        d.update(kw)
        self.ops[q].append((waits, "dma_start", d, (owner.dsem, 16)))
        self._mark(ev, reads, writes)

    def barrier(self):
        for e in ENGS:
            waits = []
            seen = self.seen[e]
            for e2 in ENGS:
                if e2 != e and self.cnt[e2] > seen.get(e2, 0):
                    seen[e2] = self.cnt[e2]
                    waits.append((e2, self.cnt[e2]))
            for k, b in self.dsems.items():
                if k in self.bg_sems:
                    continue
                if b.dsem == k and b.dcnt > seen.get(k, 0):
                    seen[k] = b.dcnt
                    waits.append((k, b.dcnt))
            self.ops[e].append((waits, None, None, None))

    def emit(self):
        nc = self.nc
        ops = self.ops
        keys = set(ENGS)
        for e in ENGS:
            for waits, name, kw, inc in ops[e]:
                for k, v in waits:
                    keys.add(k)
                if inc is not None:
                    keys.add(inc[0])
        sems = {k: nc.alloc_semaphore(f"s_{k}") for k in sorted(keys)}
        needed = {e: set() for e in ENGS}
        for e in ENGS:
            for waits, name, kw, inc in ops[e]:
                for k, v in waits:
                    if k in needed:
                        needed[k].add(v)
        remap = {}
        for e in ENGS:
            m = {}
            c = 0
            n = 0
            for waits, name, kw, inc in ops[e]:
                if inc is not None and inc[0] == e:
                    n += 1
                    if n in needed[e]:
                        c += 1
                        m[n] = c
            remap[e] = m

        def run(engobj, ename):
            n = 0
            for waits, name, kw, inc in ops[ename]:
                for k, v in waits:
                    if k in remap:
                        v = remap[k][v]
                    engobj.wait_ge(sems[k], v)
                if name is None:
                    continue
                ins = getattr(engobj, name)(**kw)
                if inc[0] == ename:
                    n += 1
                    if n in remap[ename]:
                        ins.then_inc(sems[ename], 1)
                else:
                    ins.then_inc(sems[inc[0]], inc[1])

        with nc.Block() as block:
            @block.tensor
            def _(e):
                run(e, "pe")

            @block.scalar
            def _(e):
                run(e, "act")

            @block.vector
            def _(e):
                run(e, "dve")

            @block.gpsimd
            def _(e):
                run(e, "pool")

            @block.sync
            def _(e):
                run(e, "sp")


class Stream:
    def __init__(self, P, ring, srcs, look=None, q="sp", outf=None, pre=None):
        self.P, self.ring, self.srcs, self.q = P, ring, srcs, q
        self.look = (len(ring) - 1) if look is None else look
        self.issued = 0
        self.outf = outf
        self.pre = pre

    def get(self, i):
        while self.issued < len(self.srcs) and self.issued <= i + self.look:
            k = self.issued
            b = self.ring[k % len(self.ring)]
            if self.pre is not None and self.pre[k] is not None:
                self.pre[k](b)
            self.P.dma(self.q, out=(b[:] if self.outf is None else self.outf[k](b)), in_=self.srcs[k])
            self.issued += 1
        return self.ring[i % len(self.ring)]


C_FFN1 = 0
C_MIX = 8
C_FFN2 = 16
C_DQN = 24
C_DKN = 25
C_DHN = 26
C_MHN = 27
C_SQN = 28
C_SKN = 29
C_CW = 30
C_CB = 46
C_GB = 50
NCOL = 52
CC_INV32, CC_SGN32, CC_INV64, CC_SGN64, CC_EPS, CC_ONE = 0, 1, 2, 3, 4, 5

CH_DQ, CH_DK, CH_MQ, CH_MK, CH_MO, CH_SQ, CH_SK, CH_AQ, CH_IQ, CH_MISC = 0, 2, 4, 6, 8, 10, 12, 14, 16, 18
DIRECT_CHUNKS = (8, 9, 10, 11, 12, 13)
TM_DV, TM_MV, TM_SV, TM_AV, TM_IW = 0, 256, 512, 768, 832


class Builder:
    def __init__(self, S, L, TT=1024, debug=False):
        self.S, self.L, self.TT, self.debug = S, L, TT, debug
        self.skip = ()
        self.wready = {}
        assert S % 512 == 0 and TT % 512 == 0 and S % TT == 0
        self.nc = nc = bass.Bass("TRN2", target_bir_lowering=False)
        self.P = P = Prog(nc)
        P.init_psum()
        ein = lambda n, shp, dt=F32: nc.dram_tensor(n, list(shp), dt, kind="ExternalInput")
        okind = "ExternalOutput" if debug else "Internal"
        scr = lambda n, shp, dt: nc.dram_tensor(n, list(shp), dt, kind=okind)
        self.xT = ein("xT", [D_MODEL, S])
        self.pos = ein("pos", [1, S], I32)
        self.w_gu = [ein(f"gu{i}", [L, NFF, 128, NDC * 256]) for i in (1, 2)]
        self.w_dn = [ein(f"dn{i}", [L, NDC, 128, NFF * 128]) for i in (1, 2)]
        self.w_fm = ein("wfm", [L, N_FM, 128, NDC * 128])
        self.w_tm = ein("wtm", [L, 128, NDC * N_TM])
        self.w_gt = ein("wgt", [L, 32, 128, NDC * 128])
        self.w_br = ein("wbr", [L, 32, 128, 2 * 128])
        self.w_out = ein("wout", [L, NDC, 128, NDC * 128])
        self.colp = ein("colp", [L, 128, NCOL])
        self.lamb = ein("lamb", [L, 128, 128])
        self.ccol = ein("ccol", [128, 8])
        self.cmatb = ein("cmatb", [128, 6, 128])
        self.cmatf = ein("cmatf", [128, 5, 128])
        self.outT = nc.dram_tensor("outT", [D_MODEL, S], F32, kind="ExternalOutput")
        self.b_gu = [scr(f"bgu{i}", [L, NFF, 128, NDC * 256], BF16) for i in (1, 2)]
        self.b_dn = [scr(f"bdn{i}", [L, NDC, 128, NFF * 128], BF16) for i in (1, 2)]
        self.b_fm = scr("bfm", [L, N_FM, 128, NDC * 128], BF16)
        self.b_tm = scr("btm", [L, 128, NDC * N_TM], BF16)
        self.b_gt = scr("bgt", [L, 32, 128, NDC * 128], BF16)
        self.b_br = scr("bbr", [L, 32, 128, 2 * 128], BF16)
        self.b_out = scr("bout", [L, NDC, 128, NDC * 128], BF16)
        self.xs = [scr(f"xs{i}", [D_MODEL, S], F32) for i in range(3)]
        self.zT = scr("zT", [N_FM * 128, S], F32)
        self.pT = scr("pT", [N_FM * 128, S], BF16)
        self.vtok = scr("vtok", [S, N_TM], F32)
        self.oT = scr("oT", [4, 256, S], BF16)
        self.rope = scr("rope", [4, 128, S], F32)
        self.grow = scr("grow", [8, S], F32)
        self.dbg = scr("dbg", [16, 128, 512], F32) if debug else None
        self.dbg_i = 0

    def consts(self):
        P = self.P
        self.k_ccol = P.sb_global("ccol", [128, 8], F32)
        self.k_matb = P.sb_global("cmatb", [128, 6, 128], BF16)
        self.k_matf = P.sb_global("cmatf", [128, 5, 128], F32)
        self.k_colp = P.sb_global("colp", [128, self.L, NCOL], F32)
        P.dma("sp", out=self.k_ccol[:], in_=self.ccol.ap())
        P.dma("pool", out=self.k_matb[:], in_=self.cmatb.ap())
        P.dma("sp", out=self.k_matf[:], in_=self.cmatf.ap())
        P.dma("sp", out=self.k_colp[:], in_=self.colp.ap().rearrange("l p c -> p l c"))
        self.ONES = self.k_matb[:, 0, :]
        self.BLK32 = self.k_matb[:, 1, :]
        self.BLK64 = self.k_matb[:, 2, :]
        self.UGE = self.k_matb[:, 3, :]
        self.MLE = self.k_matb[:, 4, :]
        self.MLT = self.k_matb[:, 5, :]
        self.PERM32 = self.k_matf[:, 0, :]
        self.PERM64 = self.k_matf[:, 1, :]
        self.ADDMASK = self.k_matf[:, 2, :]
        self.IDENTF = self.k_matf[:, 3, :]
        self.POW2 = self.k_matf[:, 4, :]

    def dump(self, ap, tag=""):
        if self.dbg is None or self.dbg_i >= 16:
            return
        P = self.P
        np_, nf = ap.shape[0], ap.shape[-1]
        t = P.sb("dbgt", [128, 512], F32)
        P.op("dve", "tensor_copy", out=t[0:np_, 0:nf], in_=ap)
        P.dma("sp", out=self.dbg.ap()[self.dbg_i, 0:np_, 0:nf], in_=t[0:np_, 0:nf])
        print("dbg", self.dbg_i, tag, np_, nf)
        self.dbg_i += 1

    @staticmethod
    def pipe(iters, stages, skews, drip=None, rate=2):
        n, D = len(iters), max(skews)
        for s_ in range(n + D):
            for f, k in zip(stages, skews):
                i = s_ - k
                if 0 <= i < n:
                    f(iters[i])
            if drip:
                for _ in range(rate):
                    if drip:
                        drip.pop(0)()
        while drip:
            drip.pop(0)()

    def col(self, l, c, p0=0, p1=128):
        return self.k_colp[p0:p1, l, c:c + 1]

    def cc(self, c, p0=0, p1=128):
        return self.k_ccol[p0:p1, c:c + 1]

    def cast_weights(self):
        P = self.P
        self.wready = {}
        jobs = []
        for l in range(self.L):
            for i in (0, 1):
                jobs.append((("gu", i, l), self.b_gu[i].ap()[l], self.w_gu[i].ap()[l], 2))
                jobs.append((("dn", i, l), self.b_dn[i].ap()[l].rearrange("s p (a n) -> s p a n", a=2),
                             self.w_dn[i].ap()[l].rearrange("s p (a n) -> s p a n", a=2), 2))
                if i == 0:
                    jobs.append((("fm", l), self.b_fm.ap()[l], self.w_fm.ap()[l], 1))
                    jobs.append((("tm", l), self.b_tm.ap()[l].rearrange("p (a n) -> p a n", a=8),
                                 self.w_tm.ap()[l].rearrange("p (a n) -> p a n", a=8), 1))
                    jobs.append((("gt", l), self.b_gt.ap()[l], self.w_gt.ap()[l], 2))
                    jobs.append((("br", l), self.b_br.ap()[l], self.w_br.ap()[l], 1))
                    jobs.append((("out", l), self.b_out.ap()[l], self.w_out.ap()[l], 1))
        k = 0

        def cp(owners, key, dst, src, nsplit):
            nonlocal k
            n0 = src.shape[0]
            step = (n0 + nsplit - 1) // nsplit
            evs = []
            for i in range(0, n0, step):
                ow = owners[k % len(owners)]
                k += 1
                self._dram_dma("pool", dst[i:i + step], src[i:i + step], ow)
                evs.append((ow.dsem, ow.dcnt))
            self.wready[key] = evs
        with P.phase("cast"):
            dummy = [P.sb(f"cd{i}", [1, 8], F32) for i in range(4)]
            for key, dst, src, ns in jobs[:2]:
                cp(dummy, key, dst, src, ns)
        self.wready[jobs[0][0]] = []
        self.wready[jobs[1][0]] = []
        bg = [P.sb_global(f"cg{i}", [1, 8], F32) for i in range(4)]
        for key, dst, src, ns in jobs[2:]:
            cp(bg, key, dst, src, ns)
        for b in bg:
            P.bg_sems.add(b.dsem)

    def need_weights(self, *keys):
        P = self.P
        waits = []
        for key in keys:
            for (sk, v) in self.wready.get(key, []):
                if P.seen["sp"].get(sk, 0) < v:
                    P.seen["sp"][sk] = v
                    waits.append((sk, v))
        if waits:
            P.ops["sp"].append((waits, None, None, None))

    def _dram_dma(self, q, out, in_, owner):
        P = self.P
        if owner.dsem is None:
            owner.dsem = f"d{len(P.dsems)}"
            P.dsems[owner.dsem] = owner
            P.sw_sems.add(owner.dsem)
        waits = []
        if owner.dcnt > 0:
            waits.append((owner.dsem, owner.dcnt))
        owner.dcnt += 16
        P.ops[q].append((waits, "dma_start", dict(out=out, in_=in_), (owner.dsem, 16)))
        P.ninst += 1

    def norm_parts(self, l, gcol, xt, hT, sq, rstd, TT):
        P = self.P

        ops = []
        for c in range(NDC):
            ops.append(lambda c=c: P.op("act", "activation", out=sq[:, c, :], in_=xt[:, c, :], func=AF.Square))

        def p1(hf):
            cs = slice(hf * 512, (hf + 1) * 512)
            ps = P.psum()
            for c in range(NDC):
                P.op("pe", "matmul", out=ps[:], lhsT=self.ONES, rhs=sq[:, c, cs], start=(c == 0), stop=(c == NDC - 1))
            P.op("act", "activation", out=rstd[:, cs], in_=ps[:], func=AF.Ln, scale=1.0 / D_MODEL, bias=self.cc(CC_EPS))
            P.op("act", "activation", out=rstd[:, cs], in_=rstd[:, cs], func=AF.Exp, scale=-0.5)
        for hf in range(TT // 512):
            ops.append(lambda hf=hf: p1(hf))
        for hf in range(TT // 512):
            cs = slice(hf * 512, (hf + 1) * 512)
            for c in range(NDC):
                ops.append(lambda c=c, cs=cs: P.op("dve", "scalar_tensor_tensor", out=hT[:, c, cs], in0=xt[:, c, cs], scalar=self.col(l, gcol + c),
                                                   in1=rstd[:, cs], op0=ALU.mult, op1=ALU.mult))
        return ops


    def ffn(self, l, which, xsrc, xdst):
        P, S, TT = self.P, self.S, self.TT
        NH = TT // 512
        gcol = C_FFN1 if which == 0 else C_FFN2
        bgu, bdn = self.b_gu[which].ap()[l], self.b_dn[which].ap()[l]
        with P.phase("ffn"):
            self.need_weights(("gu", which, l), ("dn", which, l))
            xts = [P.sb("xt", [128, NDC, TT], F32) for _ in range(2)]
            hTs = [P.sb("hT", [128, NDC, TT], BF16) for _ in range(2)]
            sq = P.sb("sq", [128, NDC, TT], BF16)
            rstd = P.sb("rstd", [128, TT], F32)
            aT = P.sb("aT", [128, NFF, TT], BF16)
            wg = [P.sb("wg", [128, NDC, 256], BF16) for _ in range(3)]
            wd = [P.sb("wd", [128, NFF, 128], BF16) for _ in range(2)]
            sg = [P.sb("sg", [128, 512], F32) for _ in range(2)]
            xv = lambda t, ti: t.ap().rearrange("(c p) s -> p c s", p=128)[:, :, ti * TT:(ti + 1) * TT]
            nt = S // TT
            sx = Stream(P, xts, [xv(xsrc, ti) for ti in range(nt)], look=0)
            sgu = Stream(P, wg, [bgu[j].rearrange("p (c n) -> p c n", c=NDC) for ti in range(nt) for j in range(NFF)])
            sdn = Stream(P, wd, [bdn[d].rearrange("p (c n) -> p c n", c=NFF) for ti in range(nt) for d in range(NDC)])
            for p_ in self.norm_parts(l, gcol, sx.get(0), hTs[0], sq, rstd, TT):
                p_()
            for ti in range(nt):
                xt = sx.get(ti)
                hT = hTs[ti % 2]
                sgu.get(ti * NFF)
                nxt = self.norm_parts(l, gcol, sx.get(ti + 1), hTs[(ti + 1) % 2], sq, rstd, TT) if ti + 1 < nt else None
                for j in range(NFF):
                    w = sgu.get(ti * NFF + j)
                    if nxt and j >= 2:
                        for _ in range(2):
                            if nxt:
                                nxt.pop(0)()
                    if j == NFF - 2:
                        sdn.get(ti * NDC)
                    for hf in range(NH):
                        cs = slice(hf * 512, (hf + 1) * 512)
                        pg, pu = P.psum(), P.psum()
                        for c in range(NDC):
                            P.op("pe", "matmul", out=pg[:], lhsT=w[:, c, 0:128], rhs=hT[:, c, cs], start=(c == 0), stop=(c == NDC - 1))
                        for c in range(NDC):
                            P.op("pe", "matmul", out=pu[:], lhsT=w[:, c, 128:256], rhs=hT[:, c, cs], start=(c == 0), stop=(c == NDC - 1))
                        s_ = sg[(j * NH + hf) % 2]
                        P.op("act", "activation", out=s_[:], in_=pg[:], func=AF.Silu)
                        P.op("dve", "tensor_tensor", out=aT[:, j, cs], in0=s_[:], in1=pu[:], op=ALU.mult)
                while nxt:
                    nxt.pop(0)()
                for d in range(NDC):
                    w = sdn.get(ti * NDC + d)
                    for hf in range(NH):
                        cs = slice(hf * 512, (hf + 1) * 512)
                        po = P.psum()
                        for c in range(NFF):
                            P.op("pe", "matmul", out=po[:], lhsT=w[:, c, :], rhs=aT[:, c, cs], start=(c == 0), stop=(c == NFF - 1))
                        P.op("dve", "scalar_tensor_tensor", out=xt[:, d, cs], in0=po[:], scalar=0.5, in1=xt[:, d, cs],
                             op0=ALU.mult, op1=ALU.add)
                P.dma("sp", out=xv(xdst, ti), in_=xt[:])

    def inproj(self, l, xsrc):
        P, S, TT = self.P, self.S, self.TT
        NH = TT // 512
        bfm, btm = self.b_fm.ap()[l], self.b_tm.ap()[l]
        with P.phase("inproj"):
            self.need_weights(("fm", l), ("tm", l))
            xts = [P.sb("xt", [128, NDC, TT], F32) for _ in range(2)]
            hTs = [P.sb("hT", [128, NDC, TT], BF16) for _ in range(2)]
            sq = P.sb("sq", [128, NDC, TT], BF16)
            rstd = P.sb("rstd", [128, TT], F32)
            wf = [P.sb("wf", [128, NDC, 128], BF16) for _ in range(3)]
            wt = P.sb("wt", [128, NDC, N_TM], BF16)
            zs = [P.sb("zs", [128, 512], F32) for _ in range(4)]
            zbs = [P.sb("zb", [128, 512], BF16) for _ in range(4)]
            vs = [P.sb("vs", [128, N_TM], F32) for _ in range(2)]
            xv = lambda t, ti: t.ap().rearrange("(c p) s -> p c s", p=128)[:, :, ti * TT:(ti + 1) * TT]
            nt = S // TT
            P.dma("sp", out=wt[:], in_=btm.rearrange("p (c n) -> p c n", c=NDC))
            sx = Stream(P, xts, [xv(xsrc, ti) for ti in range(nt)], look=0)
            sfm = Stream(P, wf, [bfm[j].rearrange("p (c n) -> p c n", c=NDC) for ti in range(nt) for j in range(N_FM)])
            zi = 0
            for p_ in self.norm_parts(l, C_MIX, sx.get(0), hTs[0], sq, rstd, TT):
                p_()
            for ti in range(nt):
                xt = sx.get(ti)
                hT = hTs[ti % 2]
                sfm.get(ti * N_FM)
                nxt = self.norm_parts(l, C_MIX, sx.get(ti + 1), hTs[(ti + 1) % 2], sq, rstd, TT) if ti + 1 < nt else None
                for j in range(N_FM):
                    w = sfm.get(ti * N_FM + j)
                    if nxt and j >= 1:
                        for _ in range(2):
                            if nxt:
                                nxt.pop(0)()
                    for hf in range(NH):
                        cs = slice(hf * 512, (hf + 1) * 512)
                        ps = P.psum()
                        for c in range(NDC):
                            P.op("pe", "matmul", out=ps[:], lhsT=w[:, c, :], rhs=hT[:, c, cs], start=(c == 0), stop=(c == NDC - 1))
                        cols = slice(ti * TT + hf * 512, ti * TT + (hf + 1) * 512)
                        if j in DIRECT_CHUNKS:
                            zb = zbs[zi % 4]
                            if j in (CH_MO, CH_MO + 1):
                                P.op("act", "activation", out=zb[:], in_=ps[:], func=AF.Sigmoid)
                            elif zi % 2 == 0:
                                P.op("act", "activation", out=zb[:], in_=ps[:], func=AF.Copy)
                            else:
                                P.op("dve", "tensor_copy", out=zb[:], in_=ps[:])
                            zi += 1
                            P.dma("sp", out=self.pT.ap()[j * 128:(j + 1) * 128, cols], in_=zb[:])
                            continue
                        z = zs[zi % 4]
                        if zi % 2 == 0:
                            P.op("act", "activation", out=z[:], in_=ps[:], func=AF.Copy)
                        else:
                            P.op("dve", "tensor_copy", out=z[:], in_=ps[:])
                        zi += 1
                        P.dma("sp", out=self.zT.ap()[j * 128:(j + 1) * 128, cols], in_=z[:])
                while nxt:
                    nxt.pop(0)()
                for tb in range(TT // 128):
                    ts_ = slice(tb * 128, (tb + 1) * 128)
                    v = vs[tb % 2]
                    for (c0, c1) in ((0, 512), (512, N_TM)):
                        ps = P.psum()
                        for c in range(NDC):
                            P.op("pe", "matmul", out=ps[:, 0:c1 - c0], lhsT=hT[:, c, ts_], rhs=wt[:, c, c0:c1], start=(c == 0), stop=(c == NDC - 1))
                        if c0 == 0:
                            P.op("act", "activation", out=v[:, c0:c1], in_=ps[:, 0:c1 - c0], func=AF.Copy)
                        else:
                            P.op("dve", "tensor_copy", out=v[:, c0:c1], in_=ps[:, 0:c1 - c0])
                    r0 = ti * TT + tb * 128
                    P.dma("sp", out=self.vtok.ap()[r0:r0 + 128, :], in_=v[:])

    def outproj(self, l, xsrc, xdst):
        P, S, TT = self.P, self.S, self.TT
        NH = TT // 512
        bgt, bbr, bout = self.b_gt.ap()[l], self.b_br.ap()[l], self.b_out.ap()[l]
        with P.phase("outproj"):
            self.need_weights(("gt", l), ("br", l), ("out", l))
            xts = [P.sb("xt", [128, NDC, TT], F32) for _ in range(2)]
            hTs = [P.sb("hT", [128, NDC, TT], BF16) for _ in range(2)]
            sq = P.sb("sq", [128, NDC, TT], BF16)
            rstd = P.sb("rstd", [128, TT], F32)
            ob = P.sb("ob", [128, 8, TT], BF16)
            yT = P.sb("yT", [128, NDC, TT], BF16)
            wgs = [P.sb("wgs", [128, NDC, 128], BF16) for _ in range(8)]
            wbs = [P.sb("wbs", [128, 2, 128], BF16) for _ in range(8)]
            wo = [P.sb("wo", [128, NDC, 128], BF16) for _ in range(2)]
            sg = [P.sb("sg", [128, 512], F32) for _ in range(2)]
            tb_ = [P.sb("tb", [128, 512], F32) for _ in range(4)]
            xv = lambda t, ti: t.ap().rearrange("(c p) s -> p c s", p=128)[:, :, ti * TT:(ti + 1) * TT]
            nt = S // TT
            sx = Stream(P, xts, [xv(xsrc, ti) for ti in range(nt)], look=0)
            order = [(b, d) for ti in range(nt) for d in range(NDC) for b in range(4)]
            sg1 = Stream(P, wgs, [bgt[b * 8 + d].rearrange("p (c n) -> p c n", c=NDC) for (b, d) in order], look=4)
            sg2 = Stream(P, wbs, [bbr[b * 8 + d].rearrange("p (c n) -> p c n", c=2) for (b, d) in order], look=4)
            swo = Stream(P, wo, [bout[d].rearrange("p (c n) -> p c n", c=NDC) for ti in range(nt) for d in range(NDC)])
            for p_ in self.norm_parts(l, C_MIX, sx.get(0), hTs[0], sq, rstd, TT):
                p_()
            for ti in range(nt):
                xt = sx.get(ti)
                hT = hTs[ti % 2]
                P.dma("sp", out=ob[:], in_=self.oT.ap().rearrange("b (c p) s -> p (b c) s", p=128)[:, :, ti * TT:(ti + 1) * TT])
                nxt = self.norm_parts(l, C_MIX, sx.get(ti + 1), hTs[(ti + 1) % 2], sq, rstd, TT) if ti + 1 < nt else None
                for d in range(NDC):
                    if nxt:
                        for _ in range(4):
                            if nxt:
                                nxt.pop(0)()
                    ws = []
                    for b in range(4):
                        i_ = (ti * NDC + d) * 4 + b
                        ws.append((sg1.get(i_), sg2.get(i_)))
                    for hf in range(NH):
                        cs = slice(hf * 512, (hf + 1) * 512)
                        for b in range(4):
                            w1, w2 = ws[b]
                            pg, pu = P.psum(), P.psum()
                            for c in range(NDC):
                                P.op("pe", "matmul", out=pg[:], lhsT=w1[:, c, :], rhs=hT[:, c, cs], start=(c == 0), stop=(c == NDC - 1))
                            for c in range(2):
                                P.op("pe", "matmul", out=pu[:], lhsT=w2[:, c, :], rhs=ob[:, b * 2 + c, cs], start=(c == 0), stop=(c == 1))
                            s_ = sg[b % 2]
                            P.op("act", "activation", out=s_[:], in_=pg[:], func=AF.Sigmoid)
                            P.op("dve", "tensor_tensor", out=tb_[b][:], in0=s_[:], in1=pu[:], op=ALU.mult)
                        P.op("pool", "tensor_tensor", out=tb_[0][:], in0=tb_[0][:], in1=tb_[1][:], op=ALU.add)
                        P.op("pool", "tensor_tensor", out=tb_[2][:], in0=tb_[2][:], in1=tb_[3][:], op=ALU.add)
                        P.op("pool", "tensor_tensor", out=yT[:, d, cs], in0=tb_[0][:], in1=tb_[2][:], op=ALU.add)
                while nxt:
                    nxt.pop(0)()
                for d in range(NDC):
                    w = swo.get(ti * NDC + d)
                    for hf in range(NH):
                        cs = slice(hf * 512, (hf + 1) * 512)
                        po = P.psum()
                        for c in range(NDC):
                            P.op("pe", "matmul", out=po[:], lhsT=w[:, c, :], rhs=yT[:, c, cs], start=(c == 0), stop=(c == NDC - 1))
                        P.op("dve", "tensor_tensor", out=xt[:, d, cs], in0=po[:], in1=xt[:, d, cs], op=ALU.add)
                P.dma("sp", out=xv(xdst, ti), in_=xt[:])

    def build(self, phases=("diff", "ml", "sb", "dsa")):
        P = self.P
        self.phases = phases
        skip = self.skip
        self.consts()
        if "cast" not in skip:
            self.cast_weights()
        if "rope" not in skip:
            self.rope_tables()
        x_in = self.xT
        for l in range(self.L):
            x1, x2 = self.xs[0], self.xs[1]
            last = (l == self.L - 1)
            if "ffn" not in skip:
                self.ffn(l, 0, x_in, x1)
            if "inproj" not in skip:
                self.inproj(l, x1)
            self.mixers(l)
            if "outproj" not in skip:
                self.outproj(l, x1, x2)
            x3 = self.outT if last else self.xs[2]
            if "ffn" not in skip:
                self.ffn(l, 1, x2, x3)
            x_in = x3
        P.barrier()
        P.emit()
        return self.nc

    def rope_tables(self):
        P, S = self.P, self.S
        TWO_PI = 2.0 * math.pi
        C1 = 6.28125
        C2 = TWO_PI - C1
        MAGIC = 12582912.0
        with P.phase("rope"):
            posi = P.sb("posi", [128, S], I32)
            posf = P.sb("posf", [128, S], F32)
            ang = P.sb("ang", [128, S], F32)
            a2 = P.sb("a2", [128, S], F32)
            tt = P.sb("tt", [128, S], F32)
            rr = [P.sb("rr", [128, S], F32) for _ in range(2)]
            P.dma("sp", out=posi[:], in_=self.pos.ap().broadcast_to([128, S]))
            P.op("dve", "tensor_copy", out=posf[:], in_=posi[:])
            k = 0
            for (ci, sg) in ((CC_INV32, CC_SGN32), (CC_INV64, CC_SGN64)):
                P.op("dve", "tensor_scalar", out=ang[:], in0=posf[:], scalar1=self.cc(ci), scalar2=None, op0=ALU.mult)
                for kind in (0, 1):
                    r = rr[k % 2]
                    P.op("dve", "tensor_scalar", out=a2[:], in0=ang[:], scalar1=(math.pi / 2 if kind == 0 else 0.0), scalar2=None, op0=ALU.add)
                    P.op("dve", "tensor_scalar", out=tt[:], in0=a2[:], scalar1=1.0 / TWO_PI, scalar2=MAGIC, op0=ALU.mult, op1=ALU.add)
                    P.op("dve", "tensor_scalar", out=tt[:], in0=tt[:], scalar1=-MAGIC, scalar2=None, op0=ALU.add)
                    P.op("dve", "scalar_tensor_tensor", out=a2[:], in0=tt[:], scalar=-C1, in1=a2[:], op0=ALU.mult, op1=ALU.add)
                    P.op("dve", "scalar_tensor_tensor", out=a2[:], in0=tt[:], scalar=-C2, in1=a2[:], op0=ALU.mult, op1=ALU.add)
                    P.op("dve", "tensor_scalar", out=a2[:], in0=a2[:], scalar1=3.1415925, scalar2=-3.1415925, op0=ALU.min, op1=ALU.max)
                    P.op("act", "activation", out=r[:], in_=a2[:], func=AF.Sin)
                    if kind == 1:
                        P.op("dve", "tensor_scalar", out=r[:], in0=r[:], scalar1=self.cc(sg), scalar2=None, op0=ALU.mult)
                    P.dma("sp", out=self.rope.ap()[k], in_=r[:])
                    k += 1

    def prep(self, l):
        P, S = self.P, self.S
        CW = min(S, 1024)
        with P.phase("prep"):
            tabs = [P.sb("tab", [128, CW], F32) for _ in range(4)]
            zin = [P.sb("zin", [128, CW + 3], F32) for _ in range(4)]
            sqs = [P.sb("sq", [128, CW], BF16) for _ in range(2)]
            rstds = [P.sb("rstd", [128, 512], F32) for _ in range(4)]
            qns = [P.sb("qn", [128, CW], F32) for _ in range(2)]
            t1s = [P.sb("t1", [128, 512], F32) for _ in range(4)]
            t2s = [P.sb("t2", [128, 512], F32) for _ in range(4)]
            outs = [P.sb("po", [128, CW], BF16) for _ in range(3)]
            zi = 0
            nr = [0, 0]

            def normrope(z, o, G, gcol, p0, p1):
                blk, perm = (self.BLK32, self.PERM32) if G == 32 else (self.BLK64, self.PERM64)
                ct, st = (tabs[0], tabs[1]) if G == 32 else (tabs[2], tabs[3])
                sq, qn = sqs[nr[0] % 2], qns[nr[0] % 2]
                nr[0] += 1
                if gcol is not None:
                    P.op("act", "activation", out=sq[:], in_=z, func=AF.Square)
                hs = []
                for hf in range(CW // 512):
                    k_ = nr[1] % 4
                    nr[1] += 1
                    hs.append(dict(cs=slice(hf * 512, (hf + 1) * 512), rstd=rstds[k_], t1=t1s[k_], t2=t2s[k_]))
                if gcol is not None:
                    for h_ in hs:
                        h_["ps"] = P.psum()
                        P.op("pe", "matmul", out=h_["ps"][:], lhsT=blk, rhs=sq[:, h_["cs"]], start=True, stop=True)
                    for h_ in hs:
                        P.op("act", "activation", out=h_["rstd"][:], in_=h_["ps"][:], func=AF.Ln, scale=1.0 / G, bias=self.cc(CC_EPS))
                    for h_ in hs:
                        P.op("act", "activation", out=h_["rstd"][:], in_=h_["rstd"][:], func=AF.Exp, scale=-0.5)
                    for h_ in hs:
                        P.op("dve", "scalar_tensor_tensor", out=qn[:, h_["cs"]], in0=z[:, h_["cs"]], scalar=self.col(l, gcol), in1=h_["rstd"][:],
                             op0=ALU.mult, op1=ALU.mult)
                else:
                    for h_ in hs:
                        P.op("pool", "tensor_copy", out=qn[:, h_["cs"]], in_=z[:, h_["cs"]])
                for h_ in hs:
                    h_["ps2"] = P.psum()
                    P.op("pe", "matmul", out=h_["ps2"][:], lhsT=perm, rhs=qn[:, h_["cs"]], start=True, stop=True)
                for h_ in hs:
                    P.op("pool", "tensor_tensor", out=h_["t2"][:], in0=qn[:, h_["cs"]], in1=ct[:, h_["cs"]], op=ALU.mult)
                for h_ in hs:
                    P.op("dve", "tensor_tensor", out=h_["t1"][:], in0=h_["ps2"][:], in1=st[:, h_["cs"]], op=ALU.mult)
                for h_ in hs:
                    P.op("dve", "tensor_tensor", out=o[p0:p1, h_["cs"]], in0=h_["t1"][p0:p1, :], in1=h_["t2"][p0:p1, :], op=ALU.add)

            PREP_CHUNKS = [j for j in range(N_FM) if j not in DIRECT_CHUNKS]
            srcs, outf, pre = [], [], []
            for ct_ in range(S // CW):
                c0 = ct_ * CW
                for j in PREP_CHUNKS:
                    rows = slice(j * 128, (j + 1) * 128)
                    conv = CH_MQ <= j < CH_MO
                    if conv and c0 > 0:
                        srcs.append(self.zT.ap()[rows, c0 - 3:c0 + CW]); outf.append(lambda b: b[:]); pre.append(None)
                    else:
                        srcs.append(self.zT.ap()[rows, c0:c0 + CW]); outf.append(lambda b: b[:, 3:])
                        pre.append((lambda b: P.op("pool", "memset", ap=b[:, 0:3], constant=0.0)) if conv else None)
            zs_ = Stream(P, zin, srcs, look=2, outf=outf, pre=pre)
            for ct_ in range(S // CW):
                c0 = ct_ * CW
                for k in range(4):
                    P.dma("sp", out=tabs[k][:], in_=self.rope.ap()[k][:, c0:c0 + CW])
                for ji, j in enumerate(PREP_CHUNKS):
                    zt = zs_.get(ct_ * len(PREP_CHUNKS) + ji)
                    o = outs[zi % 3]
                    zi += 1
                    rows = slice(j * 128, (j + 1) * 128)
                    conv = CH_MQ <= j < CH_MO
                    z = zt[:, 3:]
                    if j in (CH_DQ, CH_DQ + 1):
                        normrope(z, o, 32, C_DQN, 0, 128)
                    elif j in (CH_DK, CH_DK + 1):
                        normrope(z, o, 32, C_DKN, 0, 128)
                    elif conv:
                        kk = j - CH_MQ
                        qn = qns[nr[0] % 2]
                        nr[0] += 1
                        P.op("dve", "tensor_scalar", out=qn[:], in0=zt[:, 3:CW + 3], scalar1=self.col(l, C_CW + kk * 4 + 3),
                             scalar2=self.col(l, C_CB + kk), op0=ALU.mult, op1=ALU.add)
                        for tap in (2, 1, 0):
                            P.op("dve", "scalar_tensor_tensor", out=qn[:], in0=zt[:, tap:CW + tap], scalar=self.col(l, C_CW + kk * 4 + tap),
                                 in1=qn[:], op0=ALU.mult, op1=ALU.add)
                        P.op("act", "activation", out=o[:], in_=qn[:], func=AF.Silu)
                    elif j in (CH_MO, CH_MO + 1):
                        P.op("act", "activation", out=o[:], in_=z, func=AF.Sigmoid)
                    elif CH_SQ <= j < CH_AQ:
                        P.op("pool", "tensor_copy", out=o[:], in_=z)
                    elif j in (CH_AQ, CH_AQ + 1):
                        normrope(z, o, 64, C_SQN, 0, 128)
                    elif j in (CH_IQ, CH_IQ + 1):
                        normrope(z, o, 32, None, 0, 128)
                    else:
                        normrope(z, o, 64, C_SKN, 0, 64)
                        normrope(z, o, 32, None, 64, 96)
                        P.op("pool", "memset", ap=o[96:128, :], constant=0.0)
                    P.dma("sp", out=self.pT.ap()[rows, c0:c0 + CW], in_=o[:])

    def load_vaug(self, vaug, vst, col0, ones=True):
        P, S = self.P, self.S
        NB = S // 128
        P.dma("sp", out=vst[:], in_=self.vtok.ap().rearrange("(nb p) n -> p nb n", p=128)[:, :, col0:col0 + 64])
        P.op("act", "activation", out=vaug[:, :, 0:64], in_=vst[:], func=AF.Copy)
        if ones:
            P.op("pool", "memset", ap=vaug[:, :, 64:128], constant=1.0)

    def load_masked_q(self, qms, chunk, G):
        P = self.P
        for g, qm in enumerate(qms):
            P.op("pool", "memset", ap=qm[:], constant=0.0)
            P.dma("sp", out=qm[g * G:(g + 1) * G, :], in_=self.pT.ap()[chunk * 128 + g * G:chunk * 128 + (g + 1) * G, :])

    def head_norm_store(self, l, o32, gcol, scale, gate, dst, scr):
        P = self.P
        sqh, rs, ob = scr
        ops = []
        ops.append(lambda: P.op("act", "activation", out=sqh[0:64, :], in_=o32[0:64, :], func=AF.Square))

        def mm():
            ps = P.psum()
            P.op("pe", "matmul", out=ps[0:64, :], lhsT=self.k_matb[0:64, 2, 0:64], rhs=sqh[0:64, :], start=True, stop=True)
            P.op("act", "activation", out=rs[0:64, :], in_=ps[0:64, :], func=AF.Ln, scale=1.0 / 64, bias=self.cc(CC_EPS, 0, 64))
        ops.append(mm)
        ops.append(lambda: P.op("act", "activation", out=rs[0:64, :], in_=rs[0:64, :], func=AF.Exp, scale=-0.5))
        ops.append(lambda: P.op("dve", "scalar_tensor_tensor", out=o32[0:64, :], in0=o32[0:64, :], scalar=self.col(l, gcol, 0, 64),
                                in1=rs[0:64, :], op0=ALU.mult, op1=ALU.mult))
        if gate is not None:
            ops.append(lambda: P.op("dve", "tensor_tensor", out=ob[0:64, :], in0=o32[0:64, :], in1=gate, op=ALU.mult))
        else:
            ops.append(lambda: P.op("act", "activation", out=ob[0:64, :], in_=o32[0:64, :], func=AF.Copy, scale=float(scale)))
        ops.append(lambda: P.dma("sp", out=dst, in_=ob[0:64, :]))
        return ops

    def sb_attn(self, l):
        P, S = self.P, self.S
        NB = S // 128
        with P.phase("sb"):
            (acc,), zring, wring, tring = P.psum_pools(1, 3, 2, 2)
            qms2 = [[P.sb("qm", [128, S], BF16) for _ in range(2)] for _ in range(2)]
            kT2_ = [P.sb("kT", [128, S], BF16) for _ in range(2)]
            vst2 = [P.sb("vst", [128, NB, 64], F32) for _ in range(2)]
            V2 = [P.sb("V", [128, NB, 128], BF16) for _ in range(2)]

            def load_head(h):
                if h % 2 == 0:
                    self.load_masked_q(qms2[(h // 2) % 2], CH_SQ + h // 2, 64)
                    P.dma("sp", out=kT2_[(h // 2) % 2][:], in_=self.pT.ap()[(CH_SK + h // 2) * 128:(CH_SK + h // 2 + 1) * 128, :])
                self.load_vaug(V2[h % 2], vst2[h % 2], TM_SV + h * 64)
            R = P.sb("R", [128, 512], F32)
            E = [P.sb("E", [128, 512], F32) for _ in range(3)]
            spb = [P.sb("spb", [128, 512], BF16) for _ in range(3)]
            t1 = [P.sb("t1", [128, 512], F32) for _ in range(3)]
            Pm = [P.sb("Pm", [128, 512], BF16) for _ in range(3)]
            ob = [P.sb("ob", [64, 512], BF16) for _ in range(2)]
            zer = P.sb("zer", [128, 512], BF16)
            P.op("pool", "memset", ap=zer[:], constant=0.0)
            g = 0

            def stA(it):
                P.op("pe", "matmul", out=it["zp"][:, it["cs"]], lhsT=it["kT"][:, it["kb"] * 128:(it["kb"] + 1) * 128],
                     rhs=it["qm"][:, it["q0"] + it["c0"]:it["q0"] + 512], start=True, stop=True)

            def stB(it):
                cs, c0 = it["cs"], it["c0"]
                P.op("act", "activation", out=it["e"][:, cs], in_=it["zp"][:, cs], func=AF.Exp, scale=0.125)
                P.op("act", "activation", out=it["s"][:, cs], in_=it["e"][:, cs], func=AF.Ln, bias=1.0)
                if it["j"] >= 0:
                    P.op("pool", "tensor_tensor", out=it["s"][:, c0:c0 + 128], in0=it["s"][:, c0:c0 + 128], in1=self.MLT, op=ALU.mult)

            def stC(it):
                cs = it["cs"]
                if it["first"]:
                    P.op("pool", "memset", ap=R[:], constant=0.0)
                P.op("pe", "matmul", out=it["wp"][:, cs], lhsT=self.UGE, rhs=it["s"][:, cs], start=True, stop=True)
                P.op("pe", "matmul", out=it["tp"][:, cs], lhsT=self.ONES, rhs=it["s"][:, cs], start=True, stop=True)
                pv = it["prev"]
                if pv is not None:
                    P.op("dve", "tensor_tensor", out=R[:, pv["cs"]], in0=R[:, pv["cs"]], in1=pv["tp"][:, pv["cs"]], op=ALU.subtract)
                P.op("dve", "scalar_tensor_tensor", _after=[it["e"]], out=it["t"][:, cs], in0=it["zp"][:, cs], scalar=0.125, in1=R[:, cs],
                     op0=ALU.mult, op1=ALU.add)

            def stD(it):
                cs, c0 = it["cs"], it["c0"]
                P.op("dve", "tensor_tensor", out=it["t"][:, cs], in0=it["t"][:, cs], in1=it["wp"][:, cs], op=ALU.subtract)
                P.op("act", "activation", out=it["p"][:, cs], in_=it["t"][:, cs], func=AF.Exp)
                if it["j"] >= 0:
                    P.op("pool", "tensor_tensor", out=it["p"][:, c0:c0 + 128], in0=it["p"][:, c0:c0 + 128], in1=self.MLT, op=ALU.mult)

            def stE(it):
                if it["first"]:
                    P.op("pe", "matmul", out=acc[:, :], lhsT=it["V"][:, 0, :], rhs=zer[:], start=True, stop=False, skip_group_check=True)
                P.op("pe", "matmul", out=acc[:, it["cs"]], lhsT=it["V"][:, it["kb"], :], rhs=it["p"][:, it["cs"]], start=False, stop=it["last"], skip_group_check=True)
                if it["last"]:
                    o_ = ob[it["oi"] % 2]
                    P.op("act", "activation", out=o_[:], in_=acc[0:64, :], func=AF.Copy)
                    P.dma("sp", out=self.oT.ap()[2, it["h"] * 64:(it["h"] + 1) * 64, it["q0"]:it["q0"] + 512], in_=o_[:])

            for h in range(4):
                r0 = (h % 2) * 64
                if h == 0:
                    load_head(0)
                if h + 1 < 4:
                    load_head(h + 1)
                qm, kT, V = qms2[(h // 2) % 2][h % 2], kT2_[(h // 2) % 2], V2[h % 2]
                iters = []
                for qt in range(S // 512):
                    q0 = qt * 512
                    nkb = 4 * qt + 4
                    prev = None
                    for kb in range(nkb - 1, -1, -1):
                        j = kb - 4 * qt
                        c0 = 128 * max(j, 0)
                        it = dict(kb=kb, j=j, c0=c0, cs=slice(c0, 512), q0=q0, qm=qm, zp=zring[g % 3], wp=wring[g % 2], tp=tring[g % 2],
                                  e=E[g % 3], s=spb[g % 3], t=t1[g % 3], p=Pm[g % 3], first=(kb == nkb - 1), last=(kb == 0), prev=prev, kT=kT, V=V,
                                  h=h, oi=h * 8 + qt)
                        g += 1
                        iters.append(it)
                        prev = it
                self.pipe(iters, (stA, stB, stC, stD, stE), (0, 1, 2, 3, 4))

    def diff_attn(self, l):
        P, S = self.P, self.S
        NB = S // 128
        lam_init = 0.8 - 0.6 * math.exp(-0.3 * l)
        with P.phase("diff"):
            accA, accB, zring = P.psum_pools(2, 2, 3)
            acc_sets = (accA, accB)
            pend = []
            nq = 0
            qms2 = [[P.sb("qm", [128, S], BF16) for _ in range(4)] for _ in range(2)]
            kT2_ = [P.sb("kT", [128, S], BF16) for _ in range(2)]
            vst2 = [P.sb("vst", [128, NB, 64], F32) for _ in range(2)]
            V2 = [P.sb("V", [128, NB, 128], BF16) for _ in range(2)]

            def load_head(h):
                if h % 2 == 0:
                    self.load_masked_q(qms2[(h // 2) % 2], CH_DQ + h // 2, 32)
                    P.dma("sp", out=kT2_[(h // 2) % 2][:], in_=self.pT.ap()[(CH_DK + h // 2) * 128:(CH_DK + h // 2 + 1) * 128, :])
                self.load_vaug(V2[h % 2], vst2[h % 2], TM_DV + h * 64)
            Pm = [P.sb("Pm", [128, 512], BF16) for _ in range(4)]

            def stA(i_):
                P.op("pe", "matmul", out=i_["zp"][:, i_["cs"]], lhsT=i_["kT"][:, i_["kb"] * 128:(i_["kb"] + 1) * 128],
                     rhs=i_["qm"][:, i_["q0"] + i_["c0"]:i_["q0"] + 512], start=True, stop=True)

            def stB(i_):
                cs, c0 = i_["cs"], i_["c0"]
                P.op("act", "activation", out=i_["p"][:, cs], in_=i_["zp"][:, cs], func=AF.Exp, scale=32 ** -0.5)
                if i_["j"] >= 0:
                    P.op("pool", "tensor_tensor", out=i_["p"][:, c0:c0 + 128], in0=i_["p"][:, c0:c0 + 128], in1=self.MLE, op=ALU.mult)

            def stC(i_):
                P.op("pe", "matmul", out=i_["acc"][i_["c"]][:, i_["cs"]], lhsT=i_["V"][:, i_["kb"], :], rhs=i_["p"][:, i_["cs"]],
                     start=i_["first"], stop=i_["last"], skip_group_check=True)
            lt = P.sb("lt", [128, 128], F32)
            lp = P.sb("lp", [128, 64], F32)
            ls = P.sb("ls", [128, 4], F32)
            rr = [P.sb("rr", [64, 512], F32) for _ in range(2)]
            tt = [P.sb("tt", [64, 512], F32) for _ in range(2)]
            scr = (P.sb("sqh", [64, 512], BF16), P.sb("rs", [64, 512], F32), P.sb("obf", [64, 512], BF16))
            P.dma("sp", out=lt[:], in_=self.lamb.ap()[l])
            P.op("dve", "tensor_tensor", out=lp[:, 0:32], in0=lt[:, 0:32], in1=lt[:, 32:64], op=ALU.mult)
            P.op("dve", "tensor_tensor", out=lp[:, 32:64], in0=lt[:, 64:96], in1=lt[:, 96:128], op=ALU.mult)
            P.op("dve", "tensor_reduce", out=ls[:, 0:1], in_=lp[:, 0:32], axis=AX.X, op=ALU.add)
            P.op("dve", "tensor_reduce", out=ls[:, 1:2], in_=lp[:, 32:64], axis=AX.X, op=ALU.add)
            P.op("act", "activation", out=ls[:, 0:2], in_=ls[:, 0:2], func=AF.Exp)
            P.op("dve", "tensor_tensor", out=ls[:, 2:3], in0=ls[:, 1:2], in1=ls[:, 0:1], op=ALU.subtract)
            P.op("dve", "tensor_scalar", out=ls[:, 3:4], in0=ls[:, 2:3], scalar1=-lam_init, scalar2=None, op0=ALU.add)
            nlam = ls[0:64, 3:4]
            it = 0
            for h in range(4):
                r0 = (h % 2) * 64
                if h == 0:
                    load_head(0)
                if h + 1 < 4:
                    load_head(h + 1)
                qms, kT, V = qms2[(h // 2) % 2], kT2_[(h // 2) % 2], V2[h % 2]
                for qt in range(S // 512):
                    q0 = qt * 512
                    nkb = 4 * qt + 4
                    acc = acc_sets[nq % 2]
                    nq += 1
                    iters = []
                    for c in range(2):
                        for kb in range(nkb):
                            j = kb - 4 * qt
                            c0 = 128 * max(j, 0)
                            iters.append(dict(c=c, kb=kb, j=j, c0=c0, cs=slice(c0, 512), q0=q0, qm=qms[(h % 2) * 2 + c], kT=kT, V=V, zp=zring[it % 3], p=Pm[it % 4], acc=acc,
                                              first=(kb == 0), last=(kb == nkb - 1)))
                            it += 1
                    self.pipe(iters, (stA, stB, stC), (0, 1, 2), drip=pend)

                    def epi(acc=acc, h=h, q0=q0):
                        ops = []
                        for c in range(2):
                            ops.append(lambda c=c: P.op("act", "activation", out=rr[c][:], in_=acc[c][64:128, :], func=AF.Ln))
                            ops.append(lambda c=c: P.op("act", "activation", out=rr[c][:], in_=rr[c][:], func=AF.Exp, scale=-1.0))
                            ops.append(lambda c=c: P.op("dve", "tensor_tensor", out=tt[c][:], in0=acc[c][0:64, :], in1=rr[c][:], op=ALU.mult))
                        ops.append(lambda: P.op("dve", "scalar_tensor_tensor", out=tt[0][:], in0=tt[1][:], scalar=nlam, in1=tt[0][:],
                                                op0=ALU.mult, op1=ALU.add))
                        ops += self.head_norm_store(l, tt[0], C_DHN, 1.0 - lam_init, None, self.oT.ap()[0, h * 64:(h + 1) * 64, q0:q0 + 512], scr)
                        return ops
                    pend = epi()
            while pend:
                pend.pop(0)()

    def mlstm(self, l):
        P, S = self.P, self.S
        NB = S // 128
        g0 = CH_MISC * 128 + 96
        with P.phase("mlg"):
            zi = P.sb("zi", [4, S], F32)
            zf = P.sb("zf", [4, S], F32)
            on = P.sb("on", [4, S], F32)
            cs_ = P.sb("cs", [4, S], F32)
            P.dma("sp", out=zi[:], in_=self.zT.ap()[g0:g0 + 4, :])
            P.dma("sp", out=zf[:], in_=self.zT.ap()[g0 + 4:g0 + 8, :])
            P.op("pool", "memset", ap=on[:], constant=1.0)
            P.op("dve", "tensor_scalar", out=zi[:], in0=zi[:], scalar1=self.col(l, C_GB, 0, 4), scalar2=None, op0=ALU.add)
            P.op("dve", "tensor_scalar", out=zf[:], in0=zf[:], scalar1=self.col(l, C_GB + 1, 0, 4), scalar2=None, op0=ALU.add)
            P.op("act", "activation", out=zf[:], in_=zf[:], func=AF.Exp, scale=-1.0)
            P.op("act", "activation", out=zf[:], in_=zf[:], func=AF.Ln, bias=1.0)
            src, dst = zf, on
            d_ = 1
            while d_ < S:
                P.op("dve", "tensor_copy", out=dst[:, 0:d_], in_=src[:, 0:d_])
                P.op("dve", "tensor_tensor", out=dst[:, d_:S], in0=src[:, d_:S], in1=src[:, 0:S - d_], op=ALU.add)
                src, dst = dst, src
                d_ *= 2
            P.op("dve", "tensor_copy", out=cs_[:], in_=src[:])
            P.op("dve", "tensor_tensor", out=zi[:], in0=zi[:], in1=cs_[:], op=ALU.add)
            P.op("dve", "tensor_scalar", out=cs_[:], in0=cs_[:], scalar1=-1.0, scalar2=None, op0=ALU.mult)
            P.dma("sp", out=self.grow.ap()[0:4, :], in_=cs_[:])
            P.dma("sp", out=self.grow.ap()[4:8, :], in_=zi[:])
        with P.phase("ml"):
            accs, zring = P.psum_pools(2, 4)
            pend = []
            nq = 0
            qms2 = [[P.sb("qm", [128, S], BF16) for _ in range(2)] for _ in range(2)]
            kT2_ = [P.sb("kT", [128, S], BF16) for _ in range(2)]
            og2 = [P.sb("og", [64, S], BF16) for _ in range(2)]
            vst2 = [P.sb("vst", [128, NB, 64], F32) for _ in range(2)]
            V2 = [P.sb("V", [128, NB, 128], BF16) for _ in range(2)]
            Bbc2 = [P.sb("Bbc", [128, S], F32) for _ in range(2)]
            acol2 = [P.sb("acol", [128, NB], F32) for _ in range(2)]

            def load_head(h):
                r0 = (h % 2) * 64
                if h % 2 == 0:
                    self.load_masked_q(qms2[(h // 2) % 2], CH_MQ + h // 2, 64)
                    P.dma("sp", out=kT2_[(h // 2) % 2][:], in_=self.pT.ap()[(CH_MK + h // 2) * 128:(CH_MK + h // 2 + 1) * 128, :])
                P.dma("sp", out=og2[h % 2][:], in_=self.pT.ap()[(CH_MO + h // 2) * 128 + r0:(CH_MO + h // 2) * 128 + r0 + 64, :])
                P.dma("sp", out=Bbc2[h % 2][:], in_=self.grow.ap()[h:h + 1, :].broadcast_to([128, S]))
                P.dma("sp", out=acol2[h % 2][:], in_=self.grow.ap()[4 + h].rearrange("(nb p) -> p nb", p=128), allow_slow_non_contiguous=True)
                self.load_vaug(V2[h % 2], vst2[h % 2], TM_MV + h * 64)
            D = [P.sb("D", [128, 512], F32) for _ in range(3)]
            Pm = [P.sb("Pm", [128, 512], BF16) for _ in range(4)]

            def stA(i_):
                cs = i_["cs"]
                P.op("pe", "matmul", out=i_["zp"][:, cs], lhsT=i_["kT"][:, i_["kb"] * 128:(i_["kb"] + 1) * 128],
                     rhs=i_["qm"][:, i_["q0"] + i_["c0"]:i_["q0"] + 512], start=True, stop=True)
                P.op("act", "activation", out=i_["d"][:, cs], in_=i_["Bbc"][:, i_["q0"] + i_["c0"]:i_["q0"] + 512], func=AF.Exp,
                     bias=i_["acol"][:, i_["kb"]:i_["kb"] + 1])

            def stB(i_):
                cs, c0 = i_["cs"], i_["c0"]
                P.op("dve", "scalar_tensor_tensor", out=i_["p"][:, cs], in0=i_["zp"][:, cs], scalar=0.125, in1=i_["d"][:, cs],
                     op0=ALU.mult, op1=ALU.mult)
                if i_["j"] >= 0:
                    P.op("pool", "tensor_tensor", out=i_["p"][:, c0:c0 + 128], in0=i_["p"][:, c0:c0 + 128], in1=self.MLE, op=ALU.mult)

            def stC(i_):
                P.op("pe", "matmul", out=i_["acc"][:, i_["cs"]], lhsT=i_["V"][:, i_["kb"], :], rhs=i_["p"][:, i_["cs"]], start=i_["first"], stop=i_["last"], skip_group_check=True)
            dd = P.sb("dd", [64, 512], F32)
            hh = P.sb("hh", [64, 512], F32)
            scr = (P.sb("sqh", [64, 512], BF16), P.sb("rs", [64, 512], F32), P.sb("obf", [64, 512], BF16))
            it = 0
            for h in range(4):
                if h == 0:
                    load_head(0)
                if h + 1 < 4:
                    load_head(h + 1)
                qms, kT, og, V = qms2[(h // 2) % 2], kT2_[(h // 2) % 2], og2[h % 2], V2[h % 2]
                Bbc, acol = Bbc2[h % 2], acol2[h % 2]
                for qt in range(S // 512):
                    q0 = qt * 512
                    nkb = 4 * qt + 4
                    acc = accs[nq % 2]
                    nq += 1
                    iters = []
                    for kb in range(nkb):
                        j = kb - 4 * qt
                        c0 = 128 * max(j, 0)
                        iters.append(dict(kb=kb, j=j, c0=c0, cs=slice(c0, 512), q0=q0, qm=qms[h % 2], kT=kT, V=V, Bbc=Bbc, acol=acol, zp=zring[it % 4], d=D[it % 3], p=Pm[it % 4], acc=acc,
                                          first=(kb == 0), last=(kb == nkb - 1)))
                        it += 1
                    self.pipe(iters, (stA, stB, stC), (0, 1, 2), drip=pend)

                    def epi(acc=acc, h=h, q0=q0):
                        ops = [lambda: P.op("act", "activation", out=dd[:], in_=acc[64:128, :], func=AF.Abs),
                               lambda: P.op("dve", "tensor_scalar", out=dd[:], in0=dd[:], scalar1=1.0, scalar2=None, op0=ALU.max),
                               lambda: P.op("act", "activation", out=dd[:], in_=dd[:], func=AF.Ln),
                               lambda: P.op("act", "activation", out=dd[:], in_=dd[:], func=AF.Exp, scale=-1.0),
                               lambda: P.op("dve", "tensor_tensor", out=hh[:], in0=acc[0:64, :], in1=dd[:], op=ALU.mult)]
                        ops += self.head_norm_store(l, hh, C_MHN, 1.0, og[:, q0:q0 + 512], self.oT.ap()[1, h * 64:(h + 1) * 64, q0:q0 + 512], scr)
                        return ops
                    pend = epi()
                while pend:
                    pend.pop(0)()

    def dsa(self, l):
        P, S = self.P, self.S
        NB = S // 128
        NIT = 16
        with P.phase("dsa"):
            acc, zring = P.psum_pools(4, 4)
            P.ring = [4, 5, 6, 7]
            qms = [P.sb("qm", [128, S], BF16) for _ in range(4)]
            kT2 = P.sb("kT2", [128, S], BF16)
            qiT = [P.sb("qiT", [128, S], BF16) for _ in range(2)]
            kiT4 = P.sb("kiT4", [128, S], BF16)
            wq = P.sb("wq", [128, NB, 8], F32)
            V = P.sb("V", [128, NB, 128], BF16)
            score = P.sb("score", [128, S], F32)
            scoreB = P.sb("scoreB", [128, S], F32)
            msel = P.sb("msel", [128, S], F32)
            vst = msel[:, 0:NB * 64].rearrange("p (a b) -> p a b", b=64)
            junk = P.sb("junk", [128, S], BF16)
            junkB = P.sb("junkB", [128, S], BF16)
            smB = P.sb("smB", [128, 8], F32)
            wkB = P.sb("wkB", [128, 32], F32)
            nwkB = P.sb("nwkB", [128, 32], F32)
            nm = P.sb("nm", [128, 2], F32)
            maskT = P.sb("maskT", [128, NB, 512], BF16)
            rl = [P.sb("rl", [128, 512], F32) for _ in range(3)]
            E = [P.sb("E", [128, 512], BF16) for _ in range(4)]
            Pm = [P.sb("Pm", [128, 512], BF16) for _ in range(4)]

            def stA(i_):
                P.op("pe", "matmul", out=i_["zp"][:, i_["cs"]], lhsT=kT2[:, i_["kb"] * 128:(i_["kb"] + 1) * 128],
                     rhs=qms[i_["h"]][:, i_["q0"] + i_["c0"]:i_["q0"] + 512], start=True, stop=True)

            def stB(i_):
                cs = i_["cs"]
                P.op("act", "activation", out=i_["e"][:, cs], in_=i_["zp"][:, cs], func=AF.Exp, scale=0.125)
                P.op("dve", "tensor_tensor", out=i_["p"][:, cs], in0=i_["e"][:, cs], in1=maskT[:, i_["kb"], cs], op=ALU.mult)

            def stC(i_):
                P.op("pe", "matmul", out=acc[i_["h"]][:, i_["cs"]], lhsT=V[:, i_["kb"], :], rhs=i_["p"][:, i_["cs"]],
                     start=i_["first"], stop=i_["last"], skip_group_check=True)
            sm = P.sb("sm", [128, 8], F32)
            wk = P.sb("wk", [128, 32], F32)
            nwk = P.sb("nwk", [128, 32], F32)
            rr = P.sb("rr", [64, 512], F32)
            ob = [P.sb("ob", [64, 512], BF16) for _ in range(2)]
            for c in range(2):
                self.load_masked_q(qms[2 * c:2 * c + 2], CH_AQ + c, 64)
                P.dma("sp", out=qiT[c][:], in_=self.pT.ap()[(CH_IQ + c) * 128:(CH_IQ + c + 1) * 128, :])
                P.dma("sp", out=kT2[c * 64:(c + 1) * 64, :], in_=self.pT.ap()[CH_MISC * 128:CH_MISC * 128 + 64, :])
            for c in range(4):
                P.dma("sp", out=kiT4[c * 32:(c + 1) * 32, :], in_=self.pT.ap()[CH_MISC * 128 + 64:CH_MISC * 128 + 96, :])
            P.dma("sp", out=wq[:], in_=self.vtok.ap().rearrange("(nb p) n -> p nb n", p=128)[:, :, TM_IW:TM_IW + 8])
            self.load_vaug(V, vst, TM_AV)
            it = 0

            def indexer(qb, sc):
                nonlocal it
                n = (qb + 1) * 128
                for kc in range((n + 511) // 512):
                    nk = min(512, n - kc * 512)
                    ks = slice(kc * 512, kc * 512 + nk)
                    for hh in range(8):
                        g_ = hh % 4
                        rp = P.psum()
                        r_ = rl[it % 3]
                        it += 1
                        P.op("pe", "matmul", out=rp[:, 0:nk], lhsT=qiT[hh // 4][32 * g_:32 * g_ + 32, qb * 128:(qb + 1) * 128],
                             rhs=kiT4[32 * g_:32 * g_ + 32, ks], start=True, stop=True, tile_position=(32 * g_, 0))
                        P.op("act", "activation", out=r_[:, 0:nk], in_=rp[:, 0:nk], func=AF.Relu)
                        if hh == 0:
                            P.op("dve", "tensor_scalar", out=sc[:, ks], in0=r_[:, 0:nk], scalar1=wq[:, qb, 0:1], scalar2=None, op0=ALU.mult)
                        else:
                            P.op("dve", "scalar_tensor_tensor", out=sc[:, ks], in0=r_[:, 0:nk], scalar=wq[:, qb, hh:hh + 1], in1=sc[:, ks],
                                 op0=ALU.mult, op1=ALU.add)
                P.op("pool", "tensor_tensor", out=sc[:, qb * 128:n], in0=sc[:, qb * 128:n], in1=self.ADDMASK, op=ALU.add)

            def bis_setup(qb, sc, sm_, wk_, nwk_):
                n = (qb + 1) * 128
                P.op("dve", "tensor_reduce", out=sm_[:, 5:6], in_=sc[:, 0:n], axis=AX.X, op=ALU.max)
                P.op("dve", "tensor_reduce", out=sm_[:, 0:1], in_=sc[:, 0:qb * 128], axis=AX.X, op=ALU.min)
                P.op("dve", "tensor_tensor", out=sm_[:, 1:2], in0=sm_[:, 5:6], in1=sm_[:, 0:1], op=ALU.subtract)
                P.op("dve", "tensor_scalar", out=sm_[:, 1:2], in0=sm_[:, 1:2], scalar1=1.0001, scalar2=1e-6, op0=ALU.mult, op1=ALU.add)
                P.op("dve", "tensor_scalar", out=wk_[:, 0:NIT + 1], in0=self.POW2[:, 0:NIT + 1], scalar1=sm_[:, 1:2], scalar2=None, op0=ALU.mult)
                P.op("dve", "tensor_scalar", out=nwk_[:, 0:NIT + 1], in0=wk_[:, 0:NIT + 1], scalar1=-1.0, scalar2=None, op0=ALU.mult)

            def transposes(qb, j4):
                for kb0 in range(0, qb + 1, 4):
                    nb_ = min(4, qb + 1 - kb0)
                    tp = P.psum()
                    for i in range(nb_):
                        P.op("pe", "transpose", out=tp[:, i * 128:(i + 1) * 128], in_=msel[:, (kb0 + i) * 128:(kb0 + i + 1) * 128], identity=self.IDENTF)
                    P.op("act", "activation", out=maskT[:, kb0:kb0 + nb_, j4 * 128:(j4 + 1) * 128],
                         in_=tp[:, 0:nb_ * 128].rearrange("p (a b) -> p a b", a=nb_), func=AF.Copy)

            for qt in range(S // 512):
                q0 = qt * 512
                for jp in range(2):
                    qa, qb_ = 4 * qt + 2 * jp, 4 * qt + 2 * jp + 1
                    na, nb2 = (qa + 1) * 128, (qb_ + 1) * 128
                    indexer(qa, score)
                    indexer(qb_, scoreB)
                    if qa < 2:
                        P.op("dve", "tensor_scalar", out=msel[:, 0:na], in0=score[:, 0:na], scalar1=-1.0e29, scalar2=None, op0=ALU.is_ge)
                        transposes(qa, 2 * jp)
                        P.op("dve", "tensor_scalar", out=msel[:, 0:nb2], in0=scoreB[:, 0:nb2], scalar1=-1.0e29, scalar2=None, op0=ALU.is_ge)
                        transposes(qb_, 2 * jp + 1)
                        continue
                    bis_setup(qa, score, sm, wk, nwk)
                    bis_setup(qb_, scoreB, smB, wkB, nwkB)
                    P.op("dve", "tensor_tensor", out=sm[:, 2:3], in0=sm[:, 0:1], in1=wk[:, 0:1], op=ALU.add)
                    P.op("dve", "scalar_tensor_tensor", out=nm[:, 0:1], in0=smB[:, 0:1], scalar=-1.0, in1=nwkB[:, 0:1], op0=ALU.mult, op1=ALU.add)
                    for k in range(NIT):
                        P.op("dve", "tensor_scalar", out=junk[:, 0:na], in0=score[:, 0:na], scalar1=sm[:, 2:3], scalar2=None,
                             op0=ALU.is_ge, op1=ALU.add, accum_out=sm[:, 3:4])
                        P.op("dve", "scalar_tensor_tensor", out=sm[:, 4:5], in0=sm[:, 3:4], scalar=TOPK - 0.5, in1=wk[:, k:k + 1],
                             op0=ALU.is_ge, op1=ALU.mult)
                        P.op("dve", "scalar_tensor_tensor", out=sm[:, 2:3], in0=sm[:, 4:5], scalar=nwk[:, k + 1:k + 2], in1=sm[:, 2:3],
                             op0=ALU.add, op1=ALU.add)
                        P.op("act", "activation", out=junkB[:, 0:nb2], in_=scoreB[:, 0:nb2], func=AF.Sign, bias=nm[:, k % 2:k % 2 + 1],
                             accum_out=smB[:, 3:4])
                        P.op("act", "activation", out=smB[:, 4:5], in_=smB[:, 3:4], func=AF.Sign, bias=float(nb2 - 2 * TOPK + 1))
                        P.op("act", "activation", out=nm[:, (k + 1) % 2:(k + 1) % 2 + 1], in_=smB[:, 4:5], func=AF.Identity,
                             scale=nwkB[:, k + 1:k + 2], bias=nm[:, k % 2:k % 2 + 1])
                    P.op("dve", "tensor_tensor", out=sm[:, 6:7], in0=sm[:, 2:3], in1=nwk[:, NIT:NIT + 1], op=ALU.add)
                    P.op("dve", "scalar_tensor_tensor", out=smB[:, 6:7], in0=nm[:, NIT % 2:NIT % 2 + 1], scalar=-1.0, in1=nwkB[:, NIT:NIT + 1],
                         op0=ALU.mult, op1=ALU.add)
                    P.op("dve", "tensor_scalar", out=msel[:, 0:na], in0=score[:, 0:na], scalar1=sm[:, 6:7], scalar2=None, op0=ALU.is_ge)
                    transposes(qa, 2 * jp)
                    P.op("dve", "tensor_scalar", out=msel[:, 0:nb2], in0=scoreB[:, 0:nb2], scalar1=smB[:, 6:7], scalar2=None, op0=ALU.is_ge)
                    transposes(qb_, 2 * jp + 1)
                nkb = 4 * qt + 4
                iters = []
                for kb in range(nkb):
                    j = kb - 4 * qt
                    c0 = 128 * max(j, 0)
                    for h in range(4):
                        iters.append(dict(h=h, kb=kb, j=j, c0=c0, cs=slice(c0, 512), q0=q0, zp=zring[it % 4], e=E[it % 4], p=Pm[it % 4],
                                          first=(kb == 0), last=(kb == nkb - 1)))
                        it += 1
                self.pipe(iters, (stA, stB, stC), (0, 1, 2))
                for h in range(4):
                    o_ = ob[h % 2]
                    P.op("act", "activation", out=rr[:], in_=acc[h][64:128, :], func=AF.Ln)
                    P.op("act", "activation", out=rr[:], in_=rr[:], func=AF.Exp, scale=-1.0)
                    P.op("dve", "tensor_tensor", out=o_[:], in0=acc[h][0:64, :], in1=rr[:], op=ALU.mult)
                    P.dma("sp", out=self.oT.ap()[3, h * 64:(h + 1) * 64, q0:q0 + 512], in_=o_[:])

    def mixers(self, l):
        ph = self.phases
        if "prep" not in self.skip:
            self.prep(l)
        if "diff" in ph:
            self.diff_attn(l)
        if "ml" in ph:
            self.mlstm(l)
        if "sb" in ph:
            self.sb_attn(l)
        if "dsa" in ph:
            self.dsa(l)


_IN_OFF = {}
_o = 0
for _n, _w in (("diff_q", 256), ("diff_k", 256), ("diff_v", 256), ("ml_qk", 512), ("ml_v", 256), ("ml_i", 4), ("ml_f", 4),
               ("ml_o", 256), ("sb_q", 256), ("sb_k", 256), ("sb_v", 256), ("dsa_q", 256), ("dsa_k", 64), ("dsa_v", 64),
               ("idx_q", 256), ("idx_k", 32), ("idx_w", 8), ("gates", 4096)):
    _IN_OFF[_n] = (_o, _o + _w)
    _o += _w


def _slab(w, n):
    K, N = w.shape
    return np.ascontiguousarray(w.reshape(K // 128, 128, N // n, n).transpose(2, 1, 0, 3).reshape(N // n, 128, (K // 128) * n))


def _cols(*names):
    idx = []
    for nm in names:
        a, b = _IN_OFF[nm]
        idx.extend(range(a, b))
    return np.array(idx)


def make_constants():
    p = np.arange(128)
    ccol = np.zeros((128, 8), np.float32)
    d32, d64 = p % 32, p % 64
    ccol[:, CC_INV32] = np.where(d32 < 8, ROPE_THETA ** (-(d32 % 4) / 4.0), 0.0)
    ccol[:, CC_SGN32] = np.where(d32 < 4, -1.0, np.where(d32 < 8, 1.0, 0.0))
    ccol[:, CC_INV64] = np.where(d64 < 16, ROPE_THETA ** (-(d64 % 8) / 8.0), 0.0)
    ccol[:, CC_SGN64] = np.where(d64 < 8, -1.0, np.where(d64 < 16, 1.0, 0.0))
    ccol[:, CC_EPS] = EPS
    ccol[:, CC_ONE] = 1.0
    i, j = p[:, None], p[None, :]
    cmatb = np.zeros((128, 6, 128), np.float32)
    cmatb[:, 0] = 1.0
    cmatb[:, 1] = (i // 32 == j // 32)
    cmatb[:, 2] = (i // 64 == j // 64)
    cmatb[:, 3] = (i >= j)
    cmatb[:, 4] = (i <= j)
    cmatb[:, 5] = (i < j)
    cmatf = np.zeros((128, 5, 128), np.float32)
    cmatf[:, 4, :] = (2.0 ** -(np.arange(128, dtype=np.float64) + 1.0)).astype(np.float32)[None, :]
    part32 = np.where(d32 < 4, p + 4, np.where(d32 < 8, p - 4, -1))
    part64 = np.where(d64 < 8, p + 8, np.where(d64 < 16, p - 8, -1))
    for m in range(128):
        if part32[m] >= 0:
            cmatf[part32[m], 0, m] = 1.0
        if part64[m] >= 0:
            cmatf[part64[m], 1, m] = 1.0
    cmatf[:, 2] = np.where(j <= i, 0.0, NEG)
    cmatf[:, 3] = (i == j)
    return ccol, cmatb, cmatf


def prep_shared(inp):
    L = inp["w_in"].shape[0]
    f32 = lambda a: np.ascontiguousarray(a, dtype=np.float32)
    out = {}
    for i, nm in ((1, "ffn1"), (2, "ffn2")):
        gu = np.asarray(inp[f"{nm}_w_gu"])
        dn = np.asarray(inp[f"{nm}_w_down"])
        g2 = np.concatenate([gu[:, :, :D_FF].reshape(L, D_MODEL, NFF, 128), gu[:, :, D_FF:].reshape(L, D_MODEL, NFF, 128)], axis=3)
        out[f"gu{i}"] = f32(np.stack([_slab(g2[l].reshape(D_MODEL, NFF * 256), 256) for l in range(L)]))
        out[f"dn{i}"] = f32(np.stack([_slab(dn[l], 128) for l in range(L)]))
    w_in = np.asarray(inp["w_in"])
    fm_idx = _cols("diff_q", "diff_k", "ml_qk", "ml_o", "sb_q", "sb_k", "dsa_q", "idx_q", "dsa_k", "idx_k", "ml_i", "ml_f")
    tm_idx = _cols("diff_v", "ml_v", "sb_v", "dsa_v", "idx_w")
    wfm = np.zeros((L, D_MODEL, N_FM * 128), np.float32)
    wfm[:, :, :len(fm_idx)] = w_in[:, :, fm_idx]
    out["wfm"] = f32(np.stack([_slab(wfm[l], 128) for l in range(L)]))
    wtm = w_in[:, :, tm_idx]
    out["wtm"] = f32(wtm.reshape(L, NDC, 128, N_TM).transpose(0, 2, 1, 3).reshape(L, 128, NDC * N_TM))
    g0 = _IN_OFF["gates"][0]
    out["wgt"] = f32(np.stack([_slab(w_in[l][:, g0:], 128) for l in range(L)]))
    wb = np.asarray(inp["w_branch"])
    out["wbr"] = f32(np.stack([np.concatenate([_slab(wb[l, b], 128) for b in range(4)], axis=0) for l in range(L)]))
    out["wout"] = f32(np.stack([_slab(np.asarray(inp["w_out"])[l], 128) for l in range(L)]))
    p = np.arange(128)
    colp = np.zeros((L, 128, NCOL), np.float32)
    for l in range(L):
        colp[l, :, C_FFN1:C_FFN1 + 8] = np.asarray(inp["ffn1_norm"])[l].reshape(8, 128).T
        colp[l, :, C_MIX:C_MIX + 8] = np.asarray(inp["mix_norm"])[l].reshape(8, 128).T
        colp[l, :, C_FFN2:C_FFN2 + 8] = np.asarray(inp["ffn2_norm"])[l].reshape(8, 128).T
        colp[l, :, C_DQN] = np.asarray(inp["diff_qk_norm"])[l, 0][p % 32]
        colp[l, :, C_DKN] = np.asarray(inp["diff_qk_norm"])[l, 1][p % 32]
        colp[l, :, C_DHN] = np.asarray(inp["diff_head_norm"])[l][p % 64]
        colp[l, :, C_MHN] = np.asarray(inp["ml_head_norm"])[l][p % 64]
        colp[l, :, C_SQN] = np.asarray(inp["dsa_qk_norm"])[l, 0][p % 64]
        colp[l, :, C_SKN] = np.asarray(inp["dsa_qk_norm"])[l, 1][p % 64]
        cw = np.asarray(inp["ml_conv_w"])[l]
        cb = np.asarray(inp["ml_conv_b"])[l]
        for k in range(4):
            for tap in range(4):
                colp[l, :, C_CW + k * 4 + tap] = cw[tap, k * 128:(k + 1) * 128]
            colp[l, :, C_CB + k] = cb[k * 128:(k + 1) * 128]
        gb = np.asarray(inp["ml_gate_bias"])[l]
        colp[l, 0:4, C_GB] = gb[0]
        colp[l, 0:4, C_GB + 1] = gb[1]
    out["colp"] = colp
    lam = np.asarray(inp["diff_lambda"]).reshape(L, 1, 128)
    out["lamb"] = f32(np.broadcast_to(lam, (L, 128, 128)))
    out["ccol"], out["cmatb"], out["cmatf"] = make_constants()
    return out


_CACHE = {}


def kernel(**inputs):
    x = np.asarray(inputs["x"])
    B, S, D = x.shape
    L = np.asarray(inputs["w_in"]).shape[0]
    shared = prep_shared(inputs)
    key = (S, L)
    if key not in _CACHE:
        _CACHE[key] = Builder(S, L).build()
    nc = _CACHE[key]
    pos = np.asarray(inputs["positions"]).astype(np.int32)
    in_maps = []
    for b in range(B):
        m = dict(shared)
        m["xT"] = np.ascontiguousarray(x[b].T)
        m["pos"] = np.ascontiguousarray(pos[b].reshape(1, S))
        in_maps.append(m)
    res = run_bass_kernel_spmd(nc, in_maps, core_ids=list(range(B)))
    out = np.stack([np.ascontiguousarray(r["outT"].T) for r in res.results], axis=0)
    return out.astype(np.float32)
```
